# Optimizing a Trainium2 kernel written in Bass

```python
import jax, jax.numpy as jnp
from jax import lax
import numpy as np

D_MODEL = 4096
BATCH = 4
SEQ = 2048
DEPTH = 1
DEC_BATCH = 128
DEC_SEQ = 4
PAST_LEN = 16384
PAGE_SIZE = 128

R_WIDTH = D_MODEL // 2
R_HEAD = 64
R_HEADS = R_WIDTH // R_HEAD
R_DECAY_LORA = 64
R_AAA_LORA = 64
R_GATE_LORA = 256
R_COLS = 3 * R_WIDTH + R_DECAY_LORA + R_AAA_LORA + R_GATE_LORA
RWKV_GN_EPS = 64e-5
G_WIDTH = D_MODEL // 2
G_HEAD = 128
G_HEADS = G_WIDTH // G_HEAD
G_CONV = 4
G_CHUNK = 64
G_CONV_COLS = 3 * G_WIDTH
G_COLS = G_CONV_COLS + G_WIDTH + 2 * G_HEADS
MERGE_COLS = 2 * D_MODEL
IN_COLS = R_COLS + G_COLS + MERGE_COLS
D_FF = 4 * D_MODEL
NORM_EPS = 1e-6

kernel_name = "rwkv7_gdn_parallel_gated_adaln_decoder_step"

R_SPLITS = (R_WIDTH, 2 * R_WIDTH, 3 * R_WIDTH, 3 * R_WIDTH + R_DECAY_LORA,
            3 * R_WIDTH + R_DECAY_LORA + R_AAA_LORA)
G_SPLITS = (G_CONV_COLS, G_CONV_COLS + G_WIDTH, G_CONV_COLS + G_WIDTH + G_HEADS)


def rms_norm(x, w, eps=NORM_EPS):
    xf = x.astype(jnp.float32)
    y = xf * lax.rsqrt(jnp.mean(xf * xf, axis=-1, keepdims=True) + eps)
    return (y * w.astype(jnp.float32)).astype(x.dtype)


def l2_normalize(x, eps=1e-6):
    xf = x.astype(jnp.float32)
    return xf * lax.rsqrt(jnp.sum(xf * xf, axis=-1, keepdims=True) + eps)


def adaln_params(c, w_ada, b_ada):
    m = jax.nn.silu(c) @ w_ada + b_ada
    return [t[:, None, :] for t in jnp.split(m, 6, axis=-1)]


def rwkv7_recurrence(r, w, k, v, kk, a, S0):
    def step(S, inp):
        r_t, w_t, k_t, v_t, kk_t, a_t = inp
        sa = jnp.einsum('bhvk,bhk->bhv', S, -kk_t)
        S = (S * w_t[:, :, None, :] + sa[..., None] * (kk_t * a_t)[:, :, None, :]
             + v_t[..., None] * k_t[:, :, None, :])
        return S, jnp.einsum('bhvk,bhk->bhv', S, r_t)
    xs = tuple(jnp.moveaxis(t, 1, 0) for t in (r, w, k, v, kk, a))
    S, o = lax.scan(step, S0.astype(jnp.float32), xs)
    return jnp.moveaxis(o, 0, 1), S


def rwkv7_branch(pr, shift_prev, S0, lw):
    B, T, _ = pr.shape
    f32 = jnp.float32
    prev = jnp.concatenate([shift_prev[:, None, :].astype(pr.dtype), pr[:, :-1]], axis=1)
    xs = pr + (prev - pr) * lw['r_mu']
    r, k, v, dw, da, dg = jnp.split(xs, R_SPLITS, axis=-1)
    log_w = -jax.nn.softplus(-(lw['r_w0'] + jnp.tanh(dw) @ lw['r_w_w2']).astype(f32)) - 0.5
    w = jnp.exp(-jnp.exp(log_w))
    a = jax.nn.sigmoid((lw['r_a0'] + da @ lw['r_w_a2']).astype(f32))
    g = (jax.nn.sigmoid(dg) @ lw['r_w_g2']).astype(f32)
    heads = lambda t: t.astype(f32).reshape(B, T, R_HEADS, R_HEAD)
    r, k, v, w, a = heads(r), heads(k), heads(v), heads(w), heads(a)
    kk = l2_normalize(k * lw['r_k_k'].reshape(R_HEADS, R_HEAD))
    k = k * (1.0 + (a - 1.0) * lw['r_k_a'].reshape(R_HEADS, R_HEAD))
    o, S = rwkv7_recurrence(r, w, k, v, kk, a, S0)
    mu = jnp.mean(o, axis=-1, keepdims=True)
    var = jnp.mean(jnp.square(o - mu), axis=-1, keepdims=True)
    o = ((o - mu) * lax.rsqrt(var + RWKV_GN_EPS)).reshape(B, T, R_WIDTH)
    o = o * lw['r_lnx_w'] + lw['r_lnx_b']
    bonus = jnp.sum(r * k * lw['r_r_k'], axis=-1, keepdims=True) * v
    o = (o + bonus.reshape(B, T, R_WIDTH)) * g
    return o.astype(pr.dtype), S, pr[:, -1]


def gated_delta_chunked(q, k, v, log_alpha, beta, S0):
    B, T, H, Dk = q.shape
    Dv = v.shape[-1]
    C = min(G_CHUNK, T)
    n = -(-T // C)
    pad = n * C - T

    def prep(t):
        t = jnp.pad(t, [(0, 0), (0, pad)] + [(0, 0)] * (t.ndim - 2))
        t = t.reshape((B, n, C) + t.shape[2:])
        return jnp.moveaxis(jnp.moveaxis(t, 3, 2), 1, 0)

    q, k, v, la, bt = prep(q), prep(k), prep(v), prep(log_alpha), prep(beta)
    g = jnp.cumsum(la, axis=-1)
    idx = jnp.arange(C)
    causal = idx[:, None] >= idx[None, :]
    strict = idx[:, None] > idx[None, :]
    decay = jnp.exp(jnp.where(causal, g[..., :, None] - g[..., None, :], -jnp.inf))
    kb = k * bt[..., None]
    L = jnp.where(strict, jnp.einsum('nbhcd,nbhsd->nbhcs', kb, k) * decay, 0.0)
    eye = jnp.eye(C, dtype=jnp.float32)
    Tm = lax.linalg.triangular_solve(eye + L, jnp.broadcast_to(eye, L.shape), left_side=True,
                                     lower=True, unit_diagonal=True)
    u = Tm @ (v * bt[..., None])
    wk = Tm @ (kb * jnp.exp(g)[..., None])
    attn = jnp.where(causal, jnp.einsum('nbhcd,nbhsd->nbhcs', q, k) * decay, 0.0)
    q_g = q * jnp.exp(g)[..., None]
    g_last = g[..., -1:]
    k_tail = k * jnp.exp(g_last - g)[..., None]
    chunk_decay = jnp.exp(g_last)[..., None]

    def step(S, inp):
        u_i, w_i, qg_i, at_i, kt_i, cd_i = inp
        v_new = u_i - w_i @ S
        o_i = qg_i @ S + at_i @ v_new
        S = S * cd_i + jnp.swapaxes(kt_i, -1, -2) @ v_new
        return S, o_i

    S, o = lax.scan(step, S0.astype(jnp.float32), (u, wk, q_g, attn, k_tail, chunk_decay))
    o = jnp.moveaxis(jnp.moveaxis(o, 0, 1), 2, 3).reshape(B, n * C, H, Dv)[:, :T]
    return o, S


def gdn_branch(pg, conv_prev, S0, lw):
    B, T, _ = pg.shape
    f32 = jnp.float32
    qkv, z, b_raw, a_raw = jnp.split(pg, G_SPLITS, axis=-1)
    full = jnp.concatenate([conv_prev.astype(qkv.dtype), qkv], axis=1)
    cw = lw['g_conv_w']
    conv = full[:, 0:T] * cw[0]
    for i in range(1, G_CONV):
        conv = conv + full[:, i:i + T] * cw[i]
    q, k, v = jnp.split(jax.nn.silu(conv), 3, axis=-1)
    heads = lambda t: t.reshape(B, T, G_HEADS, G_HEAD)
    q = l2_normalize(heads(q)) * (G_HEAD ** -0.5)
    k = l2_normalize(heads(k))
    v = heads(v).astype(f32)
    beta = jax.nn.sigmoid(b_raw.astype(f32))
    log_alpha = -jnp.exp(lw['g_a_log'].astype(f32)) * jax.nn.softplus(a_raw.astype(f32) + lw['g_dt_bias'])
    o, S = gated_delta_chunked(q, k, v, log_alpha, beta, S0)
    o = o * lax.rsqrt(jnp.mean(o * o, axis=-1, keepdims=True) + NORM_EPS) * lw['g_norm_w']
    o = o * jax.nn.silu(heads(z).astype(f32))
    return o.reshape(B, T, G_WIDTH).astype(pg.dtype), S, full[:, -(G_CONV - 1):]


def trunk_layer(x, c, wkv0, shift0, gdn0, conv0, lw):
    sh1, sc1, gt1, sh2, sc2, gt2 = adaln_params(c, lw['w_ada'], lw['b_ada'])
    h = rms_norm(x, lw['norm1_w']) * (1.0 + sc1) + sh1
    p = h @ lw['w_in']
    pr, pg, pm = jnp.split(p, (R_COLS, R_COLS + G_COLS), axis=-1)
    ya, wkv, shift = rwkv7_branch(pr, shift0, wkv0, lw)
    yb, gdn, conv = gdn_branch(pg, conv0, gdn0, lw)
    gate_a, gate_b = jnp.split(jax.nn.sigmoid(pm), 2, axis=-1)
    merged = gate_a * (ya @ lw['w_out_a']) + gate_b * (yb @ lw['w_out_b'])
    x = x + gt1 * (merged @ lw['w_out'])
    h2 = rms_norm(x, lw['norm2_w']) * (1.0 + sc2) + sh2
    x = x + gt2 * (jnp.square(jax.nn.relu(h2 @ lw['w_up'])) @ lw['w_down'])
    return x, (wkv, shift, gdn, conv)


def run_group(x, c, wkv0, shift0, gdn0, conv0, layers, final_norm_w):
    new = []
    for l in range(DEPTH):
        x, st = trunk_layer(x, c, wkv0[l], shift0[l], gdn0[l], conv0[l], layers[l])
        new.append(st)
    y = rms_norm(x, final_norm_w)
    return y, [jnp.stack([s[i] for s in new]) for i in range(4)]


def setup_inputs(seed: int = 0) -> dict:
    key = jax.random.key(seed)
    ks = list(jax.random.split(key, 48))
    f32 = jnp.float32
    nrm = lambda shape, s=1.0: s * jax.random.normal(ks.pop(), shape, f32)
    uni = lambda shape, lo, hi: jax.random.uniform(ks.pop(), shape, f32, lo, hi)
    L = DEPTH
    inp = {}
    inp['x_prompt'] = nrm((BATCH, SEQ, D_MODEL))
    inp['x_sample'] = nrm((DEC_BATCH, DEC_SEQ, D_MODEL))
    inp['state_rwkv_wkv'] = nrm((L, DEC_BATCH, R_HEADS, R_HEAD, R_HEAD), 0.5)
    inp['state_rwkv_shift'] = nrm((L, DEC_BATCH, R_COLS))
    inp['state_gdn'] = nrm((L, DEC_BATCH, G_HEADS, G_HEAD, G_HEAD), 0.5)
    inp['state_gdn_conv'] = nrm((L, DEC_BATCH, G_CONV - 1, G_CONV_COLS))
    inp['c_prompt'] = nrm((BATCH, D_MODEL))
    inp['c_sample'] = nrm((DEC_BATCH, D_MODEL))
    inp['norm1_w'] = 1.0 + nrm((L, D_MODEL), 0.01)
    inp['norm2_w'] = 1.0 + nrm((L, D_MODEL), 0.01)
    inp['w_ada'] = nrm((L, D_MODEL, 6 * D_MODEL), D_MODEL ** -0.5)
    inp['b_ada'] = nrm((L, 6 * D_MODEL), 0.01)
    inp['w_in'] = nrm((L, D_MODEL, IN_COLS), D_MODEL ** -0.5)
    inp['r_mu'] = uni((L, R_COLS), 0.0, 1.0)
    inp['r_w0'] = uni((L, R_WIDTH), -4.0, 1.0)
    inp['r_w_w2'] = nrm((L, R_DECAY_LORA, R_WIDTH), R_DECAY_LORA ** -0.5)
    inp['r_a0'] = nrm((L, R_WIDTH), 0.1)
    inp['r_w_a2'] = nrm((L, R_AAA_LORA, R_WIDTH), R_AAA_LORA ** -0.5)
    inp['r_w_g2'] = nrm((L, R_GATE_LORA, R_WIDTH), R_GATE_LORA ** -0.5)
    inp['r_k_k'] = 0.85 + nrm((L, R_WIDTH), 0.1)
    inp['r_k_a'] = 1.0 + nrm((L, R_WIDTH), 0.1)
    inp['r_r_k'] = nrm((L, R_HEADS, R_HEAD), 0.1)
    inp['r_lnx_w'] = 1.0 + nrm((L, R_WIDTH), 0.01)
    inp['r_lnx_b'] = nrm((L, R_WIDTH), 0.01)
    inp['g_conv_w'] = nrm((L, G_CONV, G_CONV_COLS), G_CONV ** -0.5)
    inp['g_a_log'] = jnp.log(uni((L, G_HEADS), 1.0, 16.0))
    inp['g_dt_bias'] = jnp.log(jnp.expm1(uni((L, G_HEADS), 0.001, 0.1)))
    inp['g_norm_w'] = 1.0 + nrm((L, G_HEAD), 0.01)
    inp['w_out_a'] = nrm((L, R_WIDTH, D_MODEL), R_WIDTH ** -0.5)
    inp['w_out_b'] = nrm((L, G_WIDTH, D_MODEL), G_WIDTH ** -0.5)
    inp['w_out'] = nrm((L, D_MODEL, D_MODEL), D_MODEL ** -0.5)
    inp['w_up'] = nrm((L, D_MODEL, D_FF), D_MODEL ** -0.5)
    inp['w_down'] = nrm((L, D_FF, D_MODEL), D_FF ** -0.5)
    inp['final_norm_w'] = 1.0 + nrm((D_MODEL,), 0.01)
    return inp


def reference(x_prompt, x_sample, state_rwkv_wkv, state_rwkv_shift, state_gdn, state_gdn_conv,
              c_prompt, c_sample, norm1_w, norm2_w, w_ada, b_ada, w_in, r_mu, r_w0, r_w_w2, r_a0,
              r_w_a2, r_w_g2, r_k_k, r_k_a, r_r_k, r_lnx_w, r_lnx_b, g_conv_w, g_a_log, g_dt_bias,
              g_norm_w, w_out_a, w_out_b, w_out, w_up, w_down, final_norm_w):
    layers = [dict(norm1_w=norm1_w[l], norm2_w=norm2_w[l], w_ada=w_ada[l], b_ada=b_ada[l],
                   w_in=w_in[l], r_mu=r_mu[l], r_w0=r_w0[l], r_w_w2=r_w_w2[l], r_a0=r_a0[l],
                   r_w_a2=r_w_a2[l], r_w_g2=r_w_g2[l], r_k_k=r_k_k[l], r_k_a=r_k_a[l],
                   r_r_k=r_r_k[l], r_lnx_w=r_lnx_w[l], r_lnx_b=r_lnx_b[l], g_conv_w=g_conv_w[l],
                   g_a_log=g_a_log[l], g_dt_bias=g_dt_bias[l], g_norm_w=g_norm_w[l],
                   w_out_a=w_out_a[l], w_out_b=w_out_b[l], w_out=w_out[l], w_up=w_up[l],
                   w_down=w_down[l]) for l in range(DEPTH)]
    bp = x_prompt.shape[0]
    y_prompt, (p_wkv, p_shift, p_gdn, p_conv) = run_group(
        x_prompt, c_prompt,
        jnp.zeros((DEPTH, bp, R_HEADS, R_HEAD, R_HEAD), jnp.float32),
        jnp.zeros((DEPTH, bp, R_COLS), x_prompt.dtype),
        jnp.zeros((DEPTH, bp, G_HEADS, G_HEAD, G_HEAD), jnp.float32),
        jnp.zeros((DEPTH, bp, G_CONV - 1, G_CONV_COLS), x_prompt.dtype),
        layers, final_norm_w)
    y_sample, (s_wkv, s_shift, s_gdn, s_conv) = run_group(
        x_sample, c_sample, state_rwkv_wkv, state_rwkv_shift, state_gdn, state_gdn_conv,
        layers, final_norm_w)
    return (y_prompt, y_sample, p_wkv, p_shift, p_gdn, p_conv, s_wkv, s_shift, s_gdn, s_conv)
```

```python
import numpy as np
import concourse.bass as bass
import concourse.mybir as mybir
from concourse.bass_utils import run_bass_kernel_spmd

F32 = mybir.dt.float32
BF16 = mybir.dt.bfloat16
AF = mybir.ActivationFunctionType
ALU = mybir.AluOpType
NORM_EPS = 1e-6
GN_EPS = 64e-5


class Cfg:
    def __init__(s, D=4096, SEQ=2048, TP=256, NS=16, LS=4):
        s.D = D; s.KC = D // 128; s.SEQ = SEQ; s.TP = TP
        s.NSLAB = SEQ // TP
        s.NS = NS; s.LS = LS; s.TS = NS * LS
        s.RW = D // 2; s.RH = s.RW // 64; s.NHP = s.RH // 2
        s.GW = D // 2; s.GH = s.GW // 128
        s.RC = 3 * s.NHP + 3
        s.RCOLS = 3 * s.RW + 384
        s.GCOLS = 4 * s.GW + 2 * s.GH
        s.FF = 4 * D; s.FC = s.FF // 128
        s.C = 64
        s.t_r = 0; s.t_k = s.NHP; s.t_v = 2 * s.NHP; s.t_lora = 3 * s.NHP
        s.t_gq = s.RC; s.t_gk = s.RC + s.GH; s.t_gv = s.RC + 2 * s.GH; s.t_gz = s.RC + 3 * s.GH
        s.t_gb = s.RC + 4 * s.GH; s.t_ga = s.t_gb + 1
        s.t_ma = s.t_ga + 1; s.t_mb = s.t_ma + s.KC
        s.NT_IN = s.t_mb + s.KC
        o = 0
        def take(n):
            nonlocal o
            r = o; o += n; return r
        s.v_bada = take(6 * s.KC); s.v_n1 = take(s.KC); s.v_n2 = take(s.KC); s.v_fw = take(s.KC)
        s.v_mu = take(s.RC); s.v_w0 = take(s.NHP); s.v_a0 = take(s.NHP); s.v_kk = take(s.NHP)
        s.v_ka = take(s.NHP); s.v_rk = take(s.NHP); s.v_lw = take(s.NHP); s.v_lb = take(s.NHP)
        s.v_cw = take(4 * 3 * s.GH); s.v_gnw = take(1)
        s.NV = o


def KW(m, *a, **k):
    return lambda en: getattr(en, m)(*a, **k)


class StopBuild(Exception):
    pass


class Res:
    __slots__ = ("w", "rs")
    def __init__(s):
        s.w = None; s.rs = []


class Til:
    def __init__(s, t, nres=1):
        s.t = t; s.r = [Res() for _ in range(nres)]
    def __getitem__(s, k):
        return s.t[k]


ENGS = ["pe", "act", "dve", "pool", "sp"]


class Prog:
    def __init__(s, nc):
        s.nc = nc
        s.q = {e: [] for e in ENGS}
        s.cnt = {e: 0 for e in ENGS}
        s.seen = {e: {} for e in ENGS}
        s.streams = []
        s.sb_off = 0
        s.rr = 0

    def sb(s, name, shape, dt, nres=1, at=None):
        esz = 4 if dt == F32 else 2
        n = 1
        for d in shape[1:]:
            n *= d
        nbytes = (n * esz + 31) // 32 * 32
        if at is None:
            at = s.sb_off
            s.sb_off += nbytes
        t = s.nc.alloc_sbuf_tensor("sb_" + name, list(shape), dt)
        return Til(t, nres)

    def stream(s):
        st = {"id": len(s.streams), "n": 0}
        s.streams.append(st)
        return st

    def _need(s, e, ev, waits, raw=False):
        if ev is None:
            return
        key, idx = ev
        if key == e and not (raw and e != "pe"):
            return
        if not isinstance(key, str):
            idx = s.streams[key[1]]["n"]
        if s.seen[e].get(key, 0) < idx:
            waits[key] = max(waits.get(key, 0), idx)

    def _deps(s, e, reads, writes):
        waits = {}
        for r in reads:
            s._need(e, r.w, waits, raw=True)
        for r in writes:
            s._need(e, r.w, waits)
            for x in r.rs:
                s._need(e, x, waits)
        for k, v in waits.items():
            s.seen[e][k] = v
        return list(waits.items())

    def op(s, e, fn, reads=(), writes=()):
        reads = [r for x in reads for r in (x.r if isinstance(x, Til) else [x])]
        writes = [r for x in writes for r in (x.r if isinstance(x, Til) else [x])]
        waits = s._deps(e, reads, writes)
        s.cnt[e] += 1
        me = (e, s.cnt[e])
        s.q[e].append(["op", fn, waits, s.cnt[e]])
        for r in reads:
            r.rs.append(me)
        for r in writes:
            r.w = me; r.rs = []

    def dma(s, q, st, out, in_, reads=(), writes=()):
        reads = [r for x in reads for r in (x.r if isinstance(x, Til) else [x])]
        writes = [r for x in writes for r in (x.r if isinstance(x, Til) else [x])]
        waits = s._deps(q, reads, writes)
        st["n"] += 1
        me = (("dma", st["id"]), st["n"])
        s.q[q].append(["dma", (out, in_), waits, st])
        for r in reads:
            r.rs.append(me)
        for r in writes:
            r.w = me; r.rs = []

    def barrier(s):
        for e in ENGS:
            waits = {}
            for f in ENGS:
                if f != e and s.cnt[f] > 0:
                    s._need(e, (f, s.cnt[f]), waits)
            for st in s.streams:
                if st["n"] > 0:
                    s._need(e, (("dma", st["id"]), st["n"]), waits)
            for k, v in waits.items():
                s.seen[e][k] = v
            if waits:
                s.q[e].append(["wait", None, list(waits.items()), None])

    def act(s, reads, writes, **kw):
        s.op("act", lambda e: e.activation(**kw), reads, writes)

    def emit(s, es):
        nc = s.nc
        EPOCH = getattr(s, 'epoch', 30000)
        DSEG = getattr(s, 'dseg', 1800)
        marked = {e: set() for e in ENGS}
        for e in ENGS:
            for item in s.q[e]:
                for key, idx in item[2]:
                    if isinstance(key, str):
                        marked[key].add(idx)
        rank = {}
        for e in ENGS:
            rank[e] = {idx: i for i, idx in enumerate(sorted(marked[e]))}
        sems = {e: [es.enter_context(nc.semaphore("sem_%s_%d" % (e, k))) for k in range(len(rank[e]) // EPOCH + 1)] for e in ENGS}
        dsems = [[es.enter_context(nc.semaphore("dsem%d_%d" % (st["id"], k))) for k in range((max(st["n"], 1) - 1) // DSEG + 1)] for st in s.streams]
        block = es.enter_context(nc.Block())

        def dma_wait(eng, sid, n):
            k = (n - 1) // DSEG
            if k > 0:
                eng.wait_ge(dsems[sid][k - 1], 16 * DSEG)
            eng.wait_ge(dsems[sid][k], 16 * (n - k * DSEG))

        def run(e, eng):
            dcnt = {}
            for item in s.q[e]:
                kind, payload, waits, extra = item
                for key, idx in waits:
                    if isinstance(key, str):
                        r = rank[key][idx]
                        eng.wait_ge(sems[key][r // EPOCH], r % EPOCH + 1)
                    else:
                        dma_wait(eng, key[1], idx)
                if kind == "op":
                    ins = payload(eng)
                    if extra in rank[e]:
                        ins.then_inc(sems[e][rank[e][extra] // EPOCH], 1)
                elif kind == "dma":
                    out, in_ = payload
                    sid = extra["id"]
                    dcnt[sid] = dcnt.get(sid, 0) + 1
                    eng.dma_start(out=out, in_=in_).then_inc(dsems[sid][(dcnt[sid] - 1) // DSEG], 16)
            if e == "sp":
                for st in s.streams:
                    if st["n"] > 0:
                        for k in range((st["n"] - 1) // DSEG + 1):
                            eng.wait_ge(dsems[st["id"]][k], 16 * min(DSEG, st["n"] - k * DSEG))

        @block.tensor
        def _(eng):
            run("pe", eng)

        @block.scalar
        def _(eng):
            run("act", eng)

        @block.vector
        def _(eng):
            run("dve", eng)

        @block.gpsimd
        def _(eng):
            run("pool", eng)

        @block.sync
        def _(eng):
            run("sp", eng)


def build(cfg, dbg=None):
    nc = bass.Bass("TRN2", target_bir_lowering=False)
    P = Prog(nc)
    P.final_waits = []
    g = cfg
    D, KC, TP, C = g.D, g.KC, g.TP, g.C
    NS, LS, TS = g.NS, g.LS, g.TS
    NHP, GH, RC = g.NHP, g.GH, g.RC
    TM = max(TP, TS)

    def din(name, shape, dt=F32):
        return nc.dram_tensor(name, list(shape), dt, kind="ExternalInput").ap()

    def dout(name, shape, dt=F32):
        return nc.dram_tensor(name, list(shape), dt, kind="ExternalOutput").ap()

    d_xp = din("xp", [g.NSLAB, 128, KC, TP])
    d_xs = din("xs", [128, KC, TS])
    d_cT = din("cT", [128, KC, 1 + NS])
    d_flag = din("flag", [128, 1])
    d_vecs = din("vecsT", [128, g.NV])
    d_gsc = din("gsc", [128, 2 * GH])
    d_const = din("consts", [128, 10, 128])
    d_cm = din("cm", [128, NS, C], BF16)
    d_sm = din("sm", [C, NS])
    d_rst = din("rst", [128, 2, TM])
    d_wada = din("wada", [6 * KC, 128, KC, 128])
    d_win = din("win", [g.NT_IN, 128, KC, 128])
    d_ww2 = din("ww2", [128, g.RW])
    d_wg2 = din("wg2", [128, 2, g.RW])
    d_woa = din("woa", [KC, 128, NHP, 128])
    d_wob = din("wob", [KC, 128, GH, 128])
    d_wo = din("wo", [KC, 128, KC, 128])
    d_wup = din("wup", [g.FC, 128, KC, 128])
    d_wdn = din("wdn", [g.FC // 8, KC, 128, 8, 128])
    d_swkv = din("s_wkv", [NHP, 128, NS, 64])
    d_sshift = din("s_shift", [128, RC, NS])
    d_sgdn = din("s_gdn", [GH, 128, NS, 128])
    d_sconv = din("s_conv", [128, 3 * GH, NS, 3])

    o_yp = dout("o_yp", [g.NSLAB // 2, 128, KC, TP])
    o_ys = dout("o_ys", [128, KC, TS])
    o_pwkv = dout("o_pwkv", [NHP, 128, 1, 64])
    o_pshift = dout("o_pshift", [128, RC, 1])
    o_pgdn = dout("o_pgdn", [GH, 128, 1, 128])
    o_pconv = dout("o_pconv", [128, 3 * GH, 1, 3])
    o_swkv = dout("o_swkv", [NHP, 128, NS, 64])
    o_sshift = dout("o_sshift", [128, RC, NS])
    o_sgdn = dout("o_sgdn", [GH, 128, NS, 128])
    o_sconv = dout("o_sconv", [128, 3 * GH, NS, 3])
    if dbg:
        o_dbg = dout("o_dbg", dbg["shape"])

    vec = P.sb("vec", [128, g.NV], F32)
    gsc = P.sb("gsc", [128, 2 * GH], F32)
    flag = P.sb("flag", [128, 1], F32)
    cst = P.sb("cst", [128, 10, 128], F32)
    cstb = P.sb("cstb", [128, 10, 128], BF16)
    cm = P.sb("cm", [128, NS, C], BF16)
    smk = P.sb("smk", [C, NS], F32)
    rst = P.sb("rst", [128, 2, TM], F32)
    mT = P.sb("mT", [128, 6, KC, 1 + NS], F32)
    cTf = P.sb("cTf", [128, KC, 1 + NS], F32)
    cTb = P.sb("cTb", [128, KC, 1 + NS], BF16)
    ww2b = P.sb("ww2b", [128, g.RW], BF16)
    wg2b = P.sb("wg2b", [128, 2, g.RW], BF16)
    okc = P.sb("okc", [128, NHP], F32)
    nw0 = P.sb("nw0", [128, NHP], F32)
    pZf = [P.sb("pZf%d" % i, [128, 1, 64], F32) for i in range(NHP)]
    pSf = [P.sb("pSf%d" % i, [128, 1, 128], F32) for i in range(GH)]
    pshift = P.sb("pshift", [128, RC, 1], F32)
    pconv = P.sb("pconv", [128, 3 * GH, 1, 3], F32)
    hT = P.sb("hT", [128, KC, TM], BF16, nres=KC)
    yaT = P.sb("yaT", [128, NHP, TM], BF16, nres=NHP)
    ybT = P.sb("ybT", [128, GH, TM], BF16, nres=GH)
    mgT = P.sb("mgT", [128, KC, TM], BF16, nres=KC)
    x1T = P.sb("x1T", [128, KC, TM], F32, nres=KC)
    rstd = P.sb("rstd", [128, TM], F32)
    NWB = 2
    WCOLS = max(KC * 128, 1024, 2 * g.RW)
    wst = [P.sb("wst%d" % i, [128, WCOLS], F32) for i in range(NWB)]
    wbf = [P.sb("wbf%d" % i, [128, WCOLS], BF16) for i in range(NWB)]
    wst_s = [P.stream() for _ in range(NWB)]
    NTMP = 26
    tmpf = [P.sb("tmp%d" % i, [128, TM + 8], F32) for i in range(NTMP)]
    NTB = 10
    tmpb = [P.sb("tmb%d" % i, [128, TM], BF16) for i in range(NTB)]
    NCH = TM // C
    tokb = [P.sb("tok%d" % i, [C, NCH, 128], BF16) for i in range(3)]
    pw = P.sb("pw", [C, 2, 4, C], BF16)
    invP = [P.sb("invP%d" % i, [C, 2, 2, C], F32) for i in range(2)]
    invT = [P.sb("invT%d" % i, [C, 2, C], F32) for i in range(2)]
    TTb = P.sb("TTb", [C, 2, C], BF16)
    chs = P.sb("chs", [C, 128], BF16)
    upad = P.sb("upad", [C, 2, 128], BF16)
    vpad = P.sb("vpad", [C, 2, 128], BF16)
    vnew = P.sb("vnew", [C, 128], BF16)
    expa = [P.sb("expa%d" % i, [128, NS, C], BF16) for i in range(2)]
    expt = [P.sb("expt%d" % i, [C, NS, 128], BF16) for i in range(2)]
    T2T = P.sb("T2T", [C, NCH, C], BF16)
    ATs = P.sb("ATs", [C, NCH, C], BF16)
    cds = P.sb("cds", [128, NCH, NS], F32)
    tk = P.sb("tk", [C, NCH, 6, GH], F32)
    Zf_w = P.sb("Zf_w", [128, NS, 64], F32)
    Zb = P.sb("Zb", [128, NS, 128], BF16)
    Sf_w = P.sb("Sf_w", [128, NS, 128], F32)
    Sb = Zb
    sst = P.sb("sst", [128, RC, NS], F32)
    cvin = [P.sb("cvin%d" % i, [128, NS, 3], F32) for i in range(2)]
    cvin_s = [P.stream() for _ in range(2)]
    cvout_s = P.stream()
    sso = P.sb("sso", [128, RC, NS], F32)
    xch = [P.sb("xch%d" % i, [128, TM], F32) for i in range(2)]
    xch_s = [P.stream() for _ in range(2)]
    ych = [P.sb("ych%d" % i, [128, TM], F32) for i in range(2)]
    ych_s = [P.stream() for _ in range(2)]
    uh = P.sb("uh", [128, 8, TM], BF16, nres=8)
    assert P.sb_off <= 208 * 1024, P.sb_off

    acc = [Til(nc.alloc_psum_tensor("acc%d" % i, [128, 512], F32)) for i in range(4)]
    smp = [Til(nc.alloc_psum_tensor("smp%d" % i, [128, 512], F32)) for i in range(4)]
    misc_s = P.stream()

    ident = cst[:, 0, :]; bones = cst[:, 1, :]; ones = cst[:, 2, :]

    def load(dst_til, dst_ap, src_ap, st=None, q="pool"):
        P.dma(q, st or misc_s, dst_ap, src_ap, writes=[dst_til])

    def store(dst_ap, src_til, src_ap, st=None, q="pool"):
        P.dma(q, st or misc_s, dst_ap, src_ap, reads=[src_til])

    cast_rr = [0]

    def wget(src_ap, npart, ncols):
        i = P.rr % NWB
        P.rr += 1
        P.dma("sp", wst_s[i], wst[i][0:npart, 0:ncols], src_ap, writes=[wst[i]])
        e = ["dve", "act", "pool"][cast_rr[0] % 3]
        cast_rr[0] += 1
        if e == "act":
            P.op("act", KW("activation", out=wbf[i][0:npart, 0:ncols], in_=wst[i][0:npart, 0:ncols], func=AF.Copy),
                 [wst[i]], [wbf[i]])
        else:
            P.op(e, KW("tensor_copy", out=wbf[i][0:npart, 0:ncols], in_=wst[i][0:npart, 0:ncols]),
                 [wst[i]], [wbf[i]])
        return wbf[i]

    def mm(out_til, out_ap, lhsT, rhs, start, stop, reads):
        P.op("pe", KW("matmul", out_ap, lhsT=lhsT, rhs=rhs, start=start, stop=stop), reads, [out_til])

    def big_mm(acc_t, T, wt, act_til, kcn, npart=128, rhs_fn=None):
        for kc in range(kcn):
            mm(acc_t, acc_t[:, 0:T], wt[0:npart, kc * 128:(kc + 1) * 128], rhs_fn(kc), kc == 0, kc == kcn - 1,
               [wt, act_til.r[kc]])

    def dve(fn, reads, writes):
        P.op("dve", fn, reads, writes)

    def act(reads, writes, **kw):
        P.op("act", KW("activation", **kw), reads, writes)

    def rsqrt_to(dst_til, dst_ap, src_til, src_ap, scale, eps, tmp_til, tmp_ap):
        act([src_til, cst], [tmp_til], out=tmp_ap, in_=src_ap, func=AF.Sqrt, scale=scale, bias=epsc[eps])
        dve(KW("reciprocal", out=dst_ap, in_=tmp_ap), [tmp_til], [dst_til])

    load(vec, vec[:], d_vecs); load(gsc, gsc[:], d_gsc); load(flag, flag[:], d_flag)
    load(cst, cst[:], d_const); load(cm, cm[:], d_cm); load(smk, smk[:], d_sm); load(rst, rst[:], d_rst)
    load(cTf, cTf[:], d_cT)
    load(sst, sst[:], d_sshift)
    dve(KW("tensor_copy", out=cstb[:], in_=cst[:]), [cst], [cstb])
    P.dma("sp", wst_s[0], wst[0][:, 0:g.RW], d_ww2, writes=[wst[0]])
    dve(KW("tensor_copy", out=ww2b[:], in_=wst[0][:, 0:g.RW]), [wst[0]], [ww2b])
    P.dma("sp", wst_s[1], wst[1][:, 0:2 * g.RW], d_wg2.rearrange("p a n -> p (a n)"), writes=[wst[1]])
    dve(KW("tensor_copy", out=wg2b[:].rearrange("p a n -> p (a n)"), in_=wst[1][:, 0:2 * g.RW]), [wst[1]], [wg2b])
    epst = P.sb("epst", [128, 8], F32)
    epsc = {}
    for i, v in enumerate([NORM_EPS, GN_EPS, 1e-6, -0.5, 1.0, 0.0]):
        dve(KW("memset", epst[:, i:i + 1], v), [], [epst])
        epsc[v] = epst[:, i:i + 1]
    dve(KW("tensor_scalar", out=okc[:], in0=vec[:, g.v_ka:g.v_ka + NHP], scalar1=-1.0, scalar2=1.0, op0=ALU.mult, op1=ALU.add), [vec], [okc])
    dve(KW("tensor_scalar", out=nw0[:], in0=vec[:, g.v_w0:g.v_w0 + NHP], scalar1=-1.0, scalar2=None, op0=ALU.mult), [vec], [nw0])
    act([gsc], [gsc], out=gsc[:, 0:GH], in_=gsc[:, 0:GH], func=AF.Exp)
    dve(KW("tensor_scalar", out=gsc[:, 0:GH], in0=gsc[:, 0:GH], scalar1=-1.0, scalar2=None, op0=ALU.mult), [gsc], [gsc])
    act([cTf], [cTb], out=cTb[:], in_=cTf[:], func=AF.Silu)
    NSQ = 1 + NS
    for j in range(6):
        for c in range(KC):
            wt = wget(d_wada[j * KC + c].rearrange("p k n -> p (k n)"), 128, KC * 128)
            a_ = acc[(j * KC + c) % 2]
            for kc in range(KC):
                mm(a_, a_[:, 0:NSQ], wt[:, kc * 128:(kc + 1) * 128], cTb[:, kc, :], kc == 0, kc == KC - 1, [wt, cTb])
            col = g.v_bada + j * KC + c
            act([a_, vec], [mT], out=mT[:, j, c, :], in_=a_[:, 0:NSQ], func=AF.Identity, bias=vec[:, col:col + 1], scale=1.0)
    for slot, vcol in ((1, g.v_n1), (4, g.v_n2)):
        dve(KW("tensor_scalar", out=mT[:, slot], in0=mT[:, slot], scalar1=1.0, scalar2=None, op0=ALU.add), [mT], [mT])
        dve(KW("tensor_tensor", out=mT[:, slot], in0=mT[:, slot],
                                                              in1=vec[:, vcol:vcol + KC].unsqueeze(2).broadcast_to([128, KC, NSQ]), op=ALU.mult), [mT, vec], [mT])

    for t_ in pZf + pSf + [pshift, pconv]:
        dve(KW("memset", t_[:], 0.0), [], [t_])
    dve(KW("memset", Zb[:], 0.0), [], [Zb])
    dve(KW("memset", upad[:], 0.0), [], [upad])
    dve(KW("memset", vpad[:], 0.0), [], [vpad])

    def stop(n):
        if dbg and dbg.get('stop') == n:
            raise StopBuild()

    def dump(name, til, ap, np_=128):
        if dbg and dbg['what'] == name:
            t0 = tmpf[0]
            P.op("pool", KW("tensor_copy", out=t0[0:np_, 0:ap.shape[1]], in_=ap), [til], [t0])
            store(o_dbg[0:np_, 0, 0:ap.shape[1]], t0, t0[0:np_, 0:ap.shape[1]])
            raise StopBuild()

    def dump_chunks(name, til, n, T, kind, si):
        if dbg and dbg['what'] == name and dbg['slab'] == (kind, si):
            t0 = tmpf[5]
            for c in range(n):
                P.op("pool", KW("tensor_copy", out=t0[:, 0:T], in_=til[:, c, 0:T]), [til.r[c]], [t0])
                store(o_dbg[:, c, 0:T], t0, t0[:, 0:T])
            raise StopBuild()

    def _run_slabs():
        for si_ in range((dbg['slab'][1] + 1 if dbg['slab'][0] == 'p' else 0) if dbg else g.NSLAB):
            slab('p', si_)
        if not dbg or dbg['slab'][0] == 's':
            slab('s', 0)

    def slab(kind, si):
        if kind == "p":
            T = TP; nseg = 1; L = TP; full = si >= g.NSLAB // 2 or bool(dbg and dbg.get('allfull'))
            xsrc = d_xp[si]; seg0 = 0; ri = 0
            mskS = cstb[0:C, 3, 0:C]; mskI = cstb[0:C, 4, 0:C]; mskSf = cst[0:C, 3, 0:C]; mskIf = cst[0:C, 4, 0:C]
            segones = cst[0:C, 2, 0:C]
        else:
            T = TS; nseg = NS; L = LS; full = True
            xsrc = d_xs; seg0 = 1; ri = 1
            mskS = cstb[0:C, 5, 0:C]; mskI = cstb[0:C, 6, 0:C]; mskSf = cst[0:C, 5, 0:C]; mskIf = cst[0:C, 6, 0:C]
            segones = cst[0:C, 7, 0:C]
        nch = T // C
        W = nseg * (L + 3)

        def segv(ap2d, lo, n):
            return ap2d[:, 0:W].rearrange("p (s l) -> p s l", s=nseg)[:, :, lo:lo + n]

        def tokv(ap2d):
            return ap2d[:, 0:T].rearrange("p (s l) -> p s l", s=nseg)

        def seg_affine(eng, out_ap, in_til, in_ap, sc_slot, sc_c, add_til, add_ap, reads, writes):
            for sgi in range(nseg):
                sl = slice(sgi * L, (sgi + 1) * L)
                scol = mT[:, sc_slot, sc_c, seg0 + sgi:seg0 + sgi + 1]
                if isinstance(add_ap, tuple):
                    bcol = mT[:, add_ap[0], add_ap[1], seg0 + sgi:seg0 + sgi + 1]
                    act(reads, writes, out=out_ap[:, sl], in_=in_ap[:, sl], func=AF.Identity, bias=bcol, scale=scol)
                else:
                    dve(KW("scalar_tensor_tensor", out=out_ap[:, sl], in0=in_ap[:, sl], scalar=scol,
                                                                              in1=add_ap[:, sl], op0=ALU.mult, op1=ALU.add), reads, writes)

        def stats_rstd(src_fn, nchunks, scale, eps):
            a_ = acc[3]
            for c in range(nchunks):
                til, ap = src_fn(c)
                tq = tmpf[c % 2]
                act([til], [tq], out=tq[:, 0:T], in_=ap, func=AF.Square)
                mm(a_, a_[:, 0:T], ones, tq[:, 0:T], c == 0, c == nchunks - 1, [cst, tq])
            rsqrt_to(rstd, rstd[:, 0:T], a_, a_[:, 0:T], scale, eps, tmpf[2], tmpf[2][:, 0:T])

        def xload(c):
            i = c % 2
            P.dma("pool", xch_s[i], xch[i][:, 0:T], xsrc[:, c, :], writes=[xch[i]])
            return xch[i], xch[i][:, 0:T]
        stats_rstd(xload, KC, 1.0 / D, NORM_EPS)
        for c in range(KC):
            til, ap = xload(c)
            t0 = tmpf[c % 2]
            dve(KW("tensor_tensor", out=t0[:, 0:T], in0=ap, in1=rstd[:, 0:T], op=ALU.mult), [til, rstd], [t0])
            seg_affine("act", hT[:, c, :], t0, t0, 1, c, None, (0, c), [t0, mT], [hT.r[c]])
        if dbg and dbg["what"] == "hT" and dbg["slab"] == (kind, si):
            t0 = tmpf[5]
            for c in range(KC):
                P.op("pool", KW("tensor_copy", out=t0[:, 0:T], in_=hT[:, c, 0:T]), [hT.r[c]], [t0])
                store(o_dbg[:, c, 0:T], t0, t0[:, 0:T])
            return

        Cg = C if kind == "p" else LS
        ngrp = T // Cg
        nlev = 6 if kind == "p" else 2
        mskL = cst[0:C, 8 if kind == "p" else 9, 0:C]
        shsrc = pshift if kind == "p" else sst
        shdst = pshift if kind == "p" else sso
        acc_rr = [0]

        def nacc():
            acc_rr[0] += 1
            return acc[acc_rr[0] % 3]

        def proj(tid, ncols=128):
            wt = wget(d_win[tid].rearrange("p k n -> p (k n)"), 128, KC * 128)
            a_ = nacc()
            for kc in range(KC):
                mm(a_, a_[0:ncols, 0:T], wt[:, kc * 128:kc * 128 + ncols], hT[:, kc, 0:T], kc == 0, kc == KC - 1, [wt, hT.r[kc]])
            return a_

        def shift_mix(chunk, a_, xs_t):
            pe_, d_ = tmpf[3], tmpf[4]
            W1 = nseg * (L + 1)
            pv = pe_[:, 0:W1].rearrange("p (s l) -> p s l", s=nseg)
            act([a_], [pe_], out=pv[:, :, 1:L + 1], in_=tokv(a_.t), func=AF.Copy)
            P.op("pool", KW("tensor_copy", out=pv[:, :, 0:1], in_=shsrc[:, chunk, :].unsqueeze(2)), [shsrc], [pe_])
            P.op("pool", KW("tensor_copy", out=shdst[:, chunk, :].unsqueeze(2), in_=pv[:, :, L:L + 1]), [pe_], [shdst])
            dve(KW("tensor_tensor", out=tokv(d_.t), in0=pv[:, :, 0:L], in1=pv[:, :, 1:L + 1], op=ALU.subtract), [pe_], [d_])
            mucol = vec[:, g.v_mu + chunk:g.v_mu + chunk + 1]
            dve(KW("scalar_tensor_tensor", out=tokv(xs_t.t), in0=tokv(d_.t), scalar=mucol, in1=pv[:, :, 1:L + 1],
                                                   op0=ALU.mult, op1=ALU.add), [d_, pe_, vec], [xs_t])

        dve(KW("memset", Zb[:], 0.0), [], [Zb])
        ldw, sdg0, sdg1 = tmpb[0], tmpb[1], tmpb[2]
        xs_t = tmpf[5]
        a_ = proj(g.t_lora + 0); shift_mix(g.t_lora + 0, a_, xs_t)
        act([xs_t], [ldw], out=ldw[0:64, 0:T], in_=xs_t[0:64, 0:T], func=AF.Tanh)
        act([xs_t], [ldw], out=ldw[64:128, 0:T], in_=xs_t[64:128, 0:T], func=AF.Copy)
        for j, sd in ((1, sdg0), (2, sdg1)):
            a_ = proj(g.t_lora + j); shift_mix(g.t_lora + j, a_, xs_t)
            if full:
                act([xs_t], [sd], out=sd[:, 0:T], in_=xs_t[:, 0:T], func=AF.Sigmoid)

        stop(1)
        chn = {"rhs": smp[0], "u": smp[1], "o": smp[0]}
        sun = [smp[2], smp[3]]
        tts = smp[1]
        ivp = smp[0]
        B_RHS, B_U, B_O, B_IVP, B_TTS = smp[0], smp[1], smp[0], smp[0], smp[1]

        def inverse(nl):
            pp, tt = 0, 0
            for lev in range(1, nl):
                last = lev == nl - 1
                for hh in range(2):
                    mm(ivp, B_IVP[0:C, (hh * 2) * C:(hh * 2 + 1) * C], invP[pp][:, hh, 1, :], invP[pp][:, hh, 0, :], True, True, [invP[pp]])
                    if True:
                        mm(ivp, B_IVP[0:C, (hh * 2 + 1) * C:(hh * 2 + 2) * C], invP[pp][:, hh, 0, :], invP[pp][:, hh, 1, :], True, True, [invP[pp]])
                stop(51)
                pn = 1 - pp
                act([ivp], [invP[pn]], out=invP[pn][:].rearrange("p a b c -> p (a b c)"), in_=B_IVP[0:C, 0:4 * C], func=AF.Copy)
                stop(52)
                for hh in range(2):
                    mm(tts, B_TTS[0:C, 4 * C + hh * C:4 * C + (hh + 1) * C], invP[pn][:, hh, 0, :], invT[tt][:, hh, :], True, True, [invP[pn], invT[tt]])
                stop(53)
                tn = 1 - tt
                dve(KW("tensor_tensor", out=invT[tn][:].rearrange("p a c -> p (a c)"), in0=B_TTS[0:C, 4 * C:6 * C],
                                                              in1=invT[tt][:].rearrange("p a c -> p (a c)"), op=ALU.add), [tts, invT[tt]], [invT[tn]])
                pp, tt = pn, tn
                stop(60 + lev)
            return tt

        for hp in range(NHP):
            xr, xk, xv = tmpf[6], tmpf[7], tmpf[8]
            for tid, xt_ in ((g.t_r + hp, xr), (g.t_k + hp, xk), (g.t_v + hp, xv)):
                a_ = proj(tid); shift_mix(tid, a_, xt_)
            hc = slice(hp * 128, (hp + 1) * 128)
            e2, at_, gt_, kk, sc_, kt_, kka, bon, cs, E1, E2, E3, E4, Bh, Kh, oT, sc2 = (tmpf[i] for i in range(9, 26))
            a_ = nacc()
            mm(a_, a_[:, 0:T], ww2b[0:64, hc], ldw[0:64, 0:T], True, True, [ww2b, ldw])
            act([a_, nw0], [e2], out=e2[:, 0:T], in_=a_[:, 0:T], func=AF.Exp, scale=-1.0, bias=nw0[:, hp:hp + 1])
            act([e2, epst], [sc_], out=sc_[:, 0:T], in_=e2[:, 0:T], func=AF.Ln, bias=epsc[1.0], scale=1.0)
            act([sc_, epst], [e2], out=e2[:, 0:T], in_=sc_[:, 0:T], func=AF.Exp, scale=-1.0, bias=epsc[-0.5])
            a_ = nacc()
            mm(a_, a_[:, 0:T], ww2b[64:128, hc], ldw[64:128, 0:T], True, True, [ww2b, ldw])
            act([a_, vec], [at_], out=at_[:, 0:T], in_=a_[:, 0:T], func=AF.Sigmoid, bias=vec[:, g.v_a0 + hp:g.v_a0 + hp + 1], scale=1.0)
            if full:
                a_ = nacc()
                mm(a_, a_[:, 0:T], wg2b[:, 0, hc], sdg0[:, 0:T], True, False, [wg2b, sdg0])
                mm(a_, a_[:, 0:T], wg2b[:, 1, hc], sdg1[:, 0:T], False, True, [wg2b, sdg1])
                act([a_], [gt_], out=gt_[:, 0:T], in_=a_[:, 0:T], func=AF.Copy)
            dve(KW("tensor_scalar", out=kk[:, 0:T], in0=xk[:, 0:T], scalar1=vec[:, g.v_kk + hp:g.v_kk + hp + 1], scalar2=None, op0=ALU.mult), [xk, vec], [kk])
            act([kk], [sc_], out=sc_[:, 0:T], in_=kk[:, 0:T], func=AF.Square)
            a_ = nacc()
            mm(a_, a_[:, 0:T], bones, sc_[:, 0:T], True, True, [cst, sc_])
            rsqrt_to(sc_, sc_[:, 0:T], a_, a_[:, 0:T], 1.0, 1e-6, sc2, sc2[:, 0:T])
            dve(KW("tensor_tensor", out=kk[:, 0:T], in0=kk[:, 0:T], in1=sc_[:, 0:T], op=ALU.mult), [kk, sc_], [kk])
            dve(KW("tensor_scalar", out=kt_[:, 0:T], in0=at_[:, 0:T], scalar1=vec[:, g.v_ka + hp:g.v_ka + hp + 1], scalar2=okc[:, hp:hp + 1],
                                            op0=ALU.mult, op1=ALU.add), [at_, vec, okc], [kt_])
            dve(KW("tensor_tensor", out=kt_[:, 0:T], in0=kt_[:, 0:T], in1=xk[:, 0:T], op=ALU.mult), [kt_, xk], [kt_])
            dve(KW("tensor_tensor", out=kka[:, 0:T], in0=kk[:, 0:T], in1=at_[:, 0:T], op=ALU.mult), [kk, at_], [kka])
            if full:
                dve(KW("scalar_tensor_tensor", out=bon[:, 0:T], in0=xr[:, 0:T], scalar=vec[:, g.v_rk + hp:g.v_rk + hp + 1], in1=kt_[:, 0:T],
                                                       op0=ALU.mult, op1=ALU.mult), [xr, vec, kt_], [bon])
                a_ = nacc()
                mm(a_, a_[:, 0:T], bones, bon[:, 0:T], True, True, [cst, bon])
                dve(KW("tensor_tensor", out=bon[:, 0:T], in0=a_[:, 0:T], in1=xv[:, 0:T], op=ALU.mult), [a_, xv], [bon])
            dve(KW("tensor_tensor_scan", out=cs[:, 0:T], data0=rst[:, ri, 0:T], data1=e2[:, 0:T], initial=0.0, op0=ALU.mult, op1=ALU.add), [rst, e2], [cs])
            act([cs], [E1], out=E1[:, 0:T], in_=cs[:, 0:T], func=AF.Exp, scale=-1.0)
            act([cs], [E2], out=E2[:, 0:T], in_=cs[:, 0:T], func=AF.Exp, scale=1.0)
            dve(KW("tensor_tensor", out=E3[:, 0:T], in0=cs[:, 0:T], in1=e2[:, 0:T], op=ALU.subtract), [cs, e2], [E3])
            act([E3], [E3], out=E3[:, 0:T], in_=E3[:, 0:T], func=AF.Exp, scale=-1.0)
            gv = lambda t_: t_[:, 0:T].rearrange("p (n c) -> p n c", c=Cg)
            dve(KW("tensor_tensor", out=gv(E4), in0=gv(E2), in1=gv(E1)[:, :, Cg - 1:Cg].broadcast_to([128, ngrp, Cg]), op=ALU.mult), [E2, E1], [E4])
            At, Bt, Kt, Rt = tmpb[3], tmpb[4], tmpb[5], tmpb[6]
            dve(KW("scalar_tensor_tensor", out=At[:, 0:T], in0=kk[:, 0:T], scalar=-1.0, in1=E3[:, 0:T], op0=ALU.mult, op1=ALU.mult), [kk, E3], [At])
            dve(KW("tensor_tensor", out=Bt[:, 0:T], in0=kka[:, 0:T], in1=E2[:, 0:T], op=ALU.mult), [kka, E2], [Bt])
            dve(KW("tensor_tensor", out=Kt[:, 0:T], in0=kt_[:, 0:T], in1=E2[:, 0:T], op=ALU.mult), [kt_, E2], [Kt])
            if full:
                dve(KW("tensor_tensor", out=Rt[:, 0:T], in0=xr[:, 0:T], in1=E1[:, 0:T], op=ALU.mult), [xr, E1], [Rt])
            dve(KW("tensor_tensor", out=Bh[:, 0:T], in0=kka[:, 0:T], in1=E4[:, 0:T], op=ALU.mult), [kka, E4], [Bh])
            dve(KW("tensor_tensor", out=Kh[:, 0:T], in0=kt_[:, 0:T], in1=E4[:, 0:T], op=ALU.mult), [kt_, E4], [Kh])
            dump('xr', xr, xr[:, 0:T]); dump('xk', xk, xk[:, 0:T]); dump('e2', e2, e2[:, 0:T]); dump('a', at_, at_[:, 0:T]); dump('kk', kk, kk[:, 0:T])
            dump('kmod', kt_, kt_[:, 0:T]); dump('cs', cs, cs[:, 0:T]); dump('E4', E4, E4[:, 0:T])
            if full:
                dump('g', gt_, gt_[:, 0:T]); dump('bon', bon, bon[:, 0:T])
            stop(2)
            if kind == "p":
                Zf = pZf[hp]
            else:
                Zf = Zf_w
                load(Zf_w, Zf_w[:], d_swkv[hp])
            for hh in range(2):
                rs_ = slice(hh * 64, hh * 64 + 64)
                P.op("pool", KW("tensor_copy", out=Zb[rs_, 0:nseg, rs_], in_=Zf[rs_, 0:nseg, :]), [Zf], [Zb])
            for ch in range(nch):
                cc = slice(ch * C, (ch + 1) * C)
                for src, dst in ((Bh, 0), (Kh, 1), (xv, 2)):
                    a_ = nacc()
                    P.op("pe", KW("transpose", a_[0:C, 0:128], src[:, cc], ident), [src, cst], [a_])
                    act([a_], [tokb[dst]], out=tokb[dst][:, ch, :], in_=a_[0:C, 0:128], func=AF.Copy)
                stop(3)
                for hh in range(2):
                    rs_ = slice(hh * 64, hh * 64 + 64)
                    pairs = [(Bt, At), (Kt, At)] + ([(Bt, Rt), (Kt, Rt)] if full else [])
                    bk = smp[hh]
                    for sl_, (l_, r_) in enumerate(pairs):
                        mm(bk, bk[0:C, sl_ * C:(sl_ + 1) * C], l_[rs_, cc], r_[rs_, cc], True, True, [l_, r_])
                    mm(bk, bk[0:C, 4 * C:5 * C], At[rs_, cc], Bt[rs_, cc], True, True, [At, Bt])
                stop(35)
                for hh in range(2):
                    bk = smp[hh]
                    psv = bk[0:C, 0:4 * C].rearrange("p (s c) -> p s c", s=4)
                    dve(KW("tensor_tensor", out=pw[:, hh, 0:2, :], in0=psv[:, 0:2, :], in1=mskSf.unsqueeze(1).broadcast_to([C, 2, C]), op=ALU.mult), [bk, cst], [pw])
                    if full:
                        dve(KW("tensor_tensor", out=pw[:, hh, 2:4, :], in0=psv[:, 2:4, :], in1=mskIf.unsqueeze(1).broadcast_to([C, 2, C]), op=ALU.mult), [bk, cst], [pw])
                    dve(KW("tensor_tensor", out=invP[0][:, hh, 0, :], in0=bk[0:C, 4 * C:5 * C], in1=mskL, op=ALU.mult), [bk, cst], [invP[0]])
                    dve(KW("tensor_tensor", out=invP[0][:, hh, 1, :], in0=bk[0:C, 0:C], in1=mskSf, op=ALU.mult), [bk, cst], [invP[0]])
                stop(4)
                dve(KW("tensor_tensor", out=invT[0][:], in0=invP[0][:, :, 1, :], in1=cst[0:C, 0, 0:C].unsqueeze(1).broadcast_to([C, 2, C]), op=ALU.add), [invP[0], cst], [invT[0]])
                for i_ in range(4):
                    dump('pw%d' % i_, pw, pw[:, 0, i_, :], C)
                dump('X', invP[0], invP[0][:, 0, 0, :], C); dump('tokB', tokb[0], tokb[0][:, ch, :], C); dump('tokK', tokb[1], tokb[1][:, ch, :], C); dump('tokV', tokb[2], tokb[2][:, ch, :], C)
                stop(5)
                tt = inverse(nlev)
                dump('TT', invT[tt], invT[tt][:, 0, :], C)
                stop(6)
                P.op("pool", KW("tensor_copy", out=TTb[:], in_=invT[tt][:]), [invT[tt]], [TTb])
                TT = TTb
                if nseg > 1:
                    dve(KW("tensor_tensor", out=expa[0][:], in0=At[:, cc].unsqueeze(1).broadcast_to([128, nseg, C]), in1=cm[:], op=ALU.mult), [At, cm], [expa[0]])
                    dve(KW("tensor_tensor", out=expa[1][:], in0=Rt[:, cc].unsqueeze(1).broadcast_to([128, nseg, C]), in1=cm[:], op=ALU.mult), [Rt, cm], [expa[1]])
                    for i_ in range(2):
                        dve(KW("tensor_tensor", out=expt[i_][:], in0=tokb[i_][:, ch, :].unsqueeze(1).broadcast_to([C, nseg, 128]),
                                                               in1=smk[:].unsqueeze(2).broadcast_to([C, nseg, 128]), op=ALU.mult), [tokb[i_], smk], [expt[i_]])
                    At_s = lambda s_: expa[0][:, s_, :]
                    Rt_s = (lambda s_: expa[1][:, s_, :]) if not (dbg and dbg.get('exp') == 'X') else (lambda s_: expa[0][:, s_, :])
                    Bh_s = lambda s_: expt[0][:, s_, :]
                    Kh_s = lambda s_: expt[1][:, s_, :]
                    rdA, rdR, rdB, rdK = [expa[0]], [expa[1]], [expt[0]], [expt[1]]
                else:
                    At_s = lambda s_: At[:, cc]
                    Rt_s = lambda s_: Rt[:, cc]
                    Bh_s = lambda s_: tokb[0][:, ch, :]
                    Kh_s = lambda s_: tokb[1][:, ch, :]
                    rdA, rdR, rdB, rdK = [At], [Rt], [tokb[0]], [tokb[1]]
                stop(66)
                Vt = tokb[2]
                rhs_ap = B_RHS[0:C, 0:128]
                for s_ in range(nseg):
                    mm(chn["rhs"], rhs_ap, At_s(s_), Zb[:, s_, :], s_ == 0, False, rdA + [Zb])
                for hh in range(2):
                    mm(chn["rhs"], B_RHS[0:C, hh * 64:hh * 64 + 64], pw[:, hh, 1, :], Vt[:, ch, hh * 64:hh * 64 + 64], False, hh == 1, [pw, Vt])
                stop(67)
                act([chn["rhs"]], [chs], out=chs[:], in_=rhs_ap, func=AF.Copy)
                dump('chs', chs, chs[:], C)
                for hh in range(2):
                    mm(chn["u"], B_U[0:C, 128 + hh * 64:128 + hh * 64 + 64], TT[:, hh, :], chs[:, hh * 64:hh * 64 + 64], True, True, [TT, chs])
                for hh in range(2):
                    rs_ = slice(hh * 64, hh * 64 + 64)
                    act([chn["u"]], [upad], out=upad[:, hh, rs_], in_=B_U[0:C, 128 + hh * 64:128 + hh * 64 + 64], func=AF.Copy)
                    P.op("pool", KW("tensor_copy", out=vpad[:, hh, rs_], in_=Vt[:, ch, rs_]), [Vt], [vpad])
                stop(68)
                dump('upad', upad, upad[:, 0, :], C)
                if full and nseg == 1:
                    o_ap = B_O[:, 256:256 + C]
                    mm(chn["o"], o_ap, Zb[:, 0, :], Rt_s(0), True, False, rdR + [Zb])
                    for hh in range(2):
                        mm(chn["o"], o_ap, upad[:, hh, :], pw[:, hh, 2, :], False, False, [upad, pw])
                        mm(chn["o"], o_ap, vpad[:, hh, :], pw[:, hh, 3, :], False, hh == 1, [vpad, pw])
                    act([chn["o"]], [oT], out=oT[:, cc], in_=o_ap, func=AF.Copy)
                    dump('oc', oT, oT[:, cc])
                elif full:
                    ot_ap = B_O[0:C, 256:384]
                    for s_ in range(nseg if not (dbg and str(dbg.get('exp', '')).isdigit()) else int(dbg['exp'])):
                        mm(chn["o"], ot_ap, Rt_s(s_), Zb[:, s_, :], s_ == 0, False, rdR + [Zb])
                    stop(69)
                    for hh in range(2):
                        rs_ = slice(hh * 64, hh * 64 + 64)
                        mm(chn["o"], B_O[0:C, 256 + hh * 64:256 + hh * 64 + 64], pw[:, hh, 2, :], upad[:, hh, rs_], False, False, [pw, upad])
                        mm(chn["o"], B_O[0:C, 256 + hh * 64:256 + hh * 64 + 64], pw[:, hh, 3, :], Vt[:, ch, rs_], False, hh == 1, [pw, Vt])
                    stop(70)
                    act([chn["o"]], [sc2], out=sc2[0:C, 0:128], in_=ot_ap, func=AF.Copy)
                    stop(71)
                    a_ = nacc()
                    P.op("pe", KW("transpose", a_[:, 0:C], sc2[0:C, 0:128], cst[0:C, 0, 0:C]), [sc2, cst], [a_])
                    act([a_], [oT], out=oT[:, cc], in_=a_[:, 0:C], func=AF.Copy)
                stop(7)
                for s_ in range(nseg):
                    su = sun[s_ % 2]
                    su_ap = smp[2 + (s_ % 2)][:, 384:512]
                    for hh in range(2):
                        rs_ = slice(hh * 64, hh * 64 + 64)
                        mm(su, su_ap[:, rs_], Bh_s(s_), upad[:, hh, rs_], True, False, rdB + [upad])
                        mm(su, su_ap[:, rs_], Kh_s(s_), Vt[:, ch, rs_], False, True, rdK + [Vt])
                    gi = (ch * C) // Cg + (s_ if nseg > 1 else (C // Cg - 1))
                    gcol = gi * Cg + Cg - 1
                    for hh in range(2):
                        rs_ = slice(hh * 64, hh * 64 + 64)
                        dve(KW("scalar_tensor_tensor",
                            out=Zf[rs_, s_, :], in0=Zf[rs_, s_, :], scalar=E1[rs_, gcol:gcol + 1], in1=su_ap[rs_, rs_], op0=ALU.mult, op1=ALU.add), [Zf, E1, su], [Zf])
                        P.op("pool", KW("tensor_copy", out=Zb[rs_, s_, rs_], in_=Zf[rs_, s_, :]), [Zf], [Zb])
            if kind == "s":
                store(o_swkv[hp], Zf_w, Zf_w[:])
            elif si == g.NSLAB - 1:
                store(o_pwkv[hp], Zf, Zf[:])
            if full:
                dump('oT', oT, oT[:, 0:T])
                a_ = nacc()
                mm(a_, a_[:, 0:T], bones, oT[:, 0:T], True, True, [cst, oT])
                dve(KW("scalar_tensor_tensor", out=sc2[:, 0:T], in0=a_[:, 0:T], scalar=-1.0 / 64, in1=oT[:, 0:T], op0=ALU.mult, op1=ALU.add), [a_, oT], [sc2])
                act([sc2], [sc_], out=sc_[:, 0:T], in_=sc2[:, 0:T], func=AF.Square)
                a_ = nacc()
                mm(a_, a_[:, 0:T], bones, sc_[:, 0:T], True, True, [cst, sc_])
                rsqrt_to(sc_, sc_[:, 0:T], a_, a_[:, 0:T], 1.0 / 64, GN_EPS, E3, E3[:, 0:T])
                dve(KW("tensor_tensor", out=sc2[:, 0:T], in0=sc2[:, 0:T], in1=sc_[:, 0:T], op=ALU.mult), [sc2, sc_], [sc2])
                dve(KW("tensor_scalar", out=sc2[:, 0:T], in0=sc2[:, 0:T], scalar1=vec[:, g.v_lw + hp:g.v_lw + hp + 1], scalar2=vec[:, g.v_lb + hp:g.v_lb + hp + 1],
                                                op0=ALU.mult, op1=ALU.add), [sc2, vec], [sc2])
                dve(KW("tensor_tensor", out=sc2[:, 0:T], in0=sc2[:, 0:T], in1=bon[:, 0:T], op=ALU.add), [sc2, bon], [sc2])
                dve(KW("tensor_tensor", out=yaT[:, hp, 0:T], in0=sc2[:, 0:T], in1=gt_[:, 0:T], op=ALU.mult), [sc2, gt_], [yaT.r[hp]])
        if kind == "s":
            store(o_sshift, sso, sso[:])
        elif si == g.NSLAB - 1:
            store(o_pshift, pshift, pshift[:])
        if dbg and dbg["what"] == "yaT" and dbg["slab"] == (kind, si):
            t0 = tmpf[5]
            for c in range(NHP):
                P.op("pool", KW("tensor_copy", out=t0[:, 0:T], in_=yaT[:, c, 0:T]), [yaT.r[c]], [t0])
                store(o_dbg[:, c, 0:T], t0, t0[:, 0:T])
            return
        if dbg and (dbg['what'] in ('hT', 'yaT', 'xr', 'xk', 'e2', 'a', 'kk', 'kmod', 'cs', 'E4', 'g', 'bon', 'oT') or dbg['what'][:2] in ('pw', 'X', 'to', 'TT', 'ch', 'up', 'oc')):
            return

        wtb = wget(d_win[g.t_gb].rearrange("p k n -> p (k n)"), 128, KC * 128)
        wta = wget(d_win[g.t_ga].rearrange("p k n -> p (k n)"), 128, KC * 128)
        for ch in range(nch):
            cc = slice(ch * C, (ch + 1) * C)
            a_ = nacc()
            for wt_, o_ in ((wtb, 0), (wta, GH)):
                for kc in range(KC):
                    mm(a_, a_[0:C, o_:o_ + GH], hT[:, kc, cc], wt_[:, kc * 128:kc * 128 + GH], kc == 0, kc == KC - 1, [wt_, hT.r[kc]])
            act([a_], [tk], out=tk[:, ch, 0, :], in_=a_[0:C, 0:GH], func=AF.Sigmoid)
            sc_ = tmpf[4]
            dve(KW("tensor_tensor", out=sc_[0:C, 0:GH], in0=a_[0:C, GH:2 * GH], in1=gsc[0:C, GH:2 * GH], op=ALU.add), [a_, gsc], [sc_])
            act([sc_], [sc_], out=sc_[0:C, 0:GH], in_=sc_[0:C, 0:GH], func=AF.Exp)
            act([sc_, epst], [sc_], out=sc_[0:C, 0:GH], in_=sc_[0:C, 0:GH], func=AF.Ln, bias=epsc[1.0][0:C, :], scale=1.0)
            dve(KW("tensor_tensor", out=tk[:, ch, 1, :], in0=sc_[0:C, 0:GH], in1=gsc[0:C, 0:GH], op=ALU.mult), [sc_, gsc], [tk])
            a_ = nacc()
            mm(a_, a_[0:C, 0:GH], mskIf, tk[:, ch, 1, :], True, True, [cst, tk])
            mm(a_, a_[0:C, GH:2 * GH], segones, tk[:, ch, 1, :], True, True, [cst, tk])
            act([a_], [tk], out=tk[:, ch, 2, :], in_=a_[0:C, 0:GH], func=AF.Copy)
            act([a_], [tk], out=tk[:, ch, 3, :], in_=a_[0:C, 0:GH], func=AF.Exp)
            dve(KW("tensor_scalar", out=tk[:, ch, 4, :], in0=tk[:, ch, 3, :], scalar1=-1.0, scalar2=None, op0=ALU.mult), [tk], [tk])
            dve(KW("tensor_tensor", out=tk[:, ch, 5, :], in0=a_[0:C, GH:2 * GH], in1=tk[:, ch, 2, :], op=ALU.subtract), [a_, tk], [tk])
            act([tk], [tk], out=tk[:, ch, 5, :], in_=tk[:, ch, 5, :], func=AF.Exp)

        def conv_mix(chunk, a_, out_t):
            pe_ = tmpf[3]
            W3 = nseg * (L + 3)
            pv = pe_[:, 0:W3].rearrange("p (s l) -> p s l", s=nseg)
            act([a_], [pe_], out=pv[:, :, 3:L + 3], in_=tokv(a_.t), func=AF.Copy)
            if kind == "p":
                P.op("pool", KW("tensor_copy", out=pv[:, :, 0:3], in_=pconv[:, chunk, :, :]), [pconv], [pe_])
                P.op("pool", KW("tensor_copy", out=pconv[:, chunk, :, :], in_=pv[:, :, L:L + 3]), [pe_], [pconv])
            else:
                ci = chunk % 2
                P.dma("pool", cvin_s[ci], cvin[ci][:], d_sconv[:, chunk, :, :], writes=[cvin[ci]])
                P.op("pool", KW("tensor_copy", out=pv[:, :, 0:3], in_=cvin[ci][:]), [cvin[ci]], [pe_])
                P.dma("pool", cvout_s, o_sconv[:, chunk, :, :], pv[:, :, L:L + 3], reads=[pe_])
            ov = tokv(out_t.t)
            cwc = lambda i_: vec[:, g.v_cw + i_ * 3 * GH + chunk:g.v_cw + i_ * 3 * GH + chunk + 1]
            dve(KW("tensor_scalar", out=ov, in0=pv[:, :, 0:L], scalar1=cwc(0), scalar2=None, op0=ALU.mult), [pe_, vec], [out_t])
            for i_ in range(1, 4):
                dve(KW("scalar_tensor_tensor", out=ov, in0=pv[:, :, i_:i_ + L], scalar=cwc(i_), in1=ov, op0=ALU.mult, op1=ALU.add), [pe_, vec, out_t], [out_t])
            act([out_t], [out_t], out=out_t[:, 0:T], in_=out_t[:, 0:T], func=AF.Silu)

        kso, kw_, koo = smp[0], smp[1], smp[0]
        for h in range(GH):
            qn, kn, vn, zs = tmpf[6], tmpf[7], tmpf[8], tmpf[9]
            sq_, rn_, sc2 = tmpf[10], tmpf[11], tmpf[12]
            for tid, cchunk, xt_ in ((g.t_gq + h, h, qn), (g.t_gk + h, GH + h, kn), (g.t_gv + h, 2 * GH + h, vn)):
                if tid == g.t_gq + h and not full:
                    pass
                a_ = proj(tid); conv_mix(cchunk, a_, xt_)
            if full:
                a_ = proj(g.t_gz + h)
                act([a_], [zs], out=zs[:, 0:T], in_=a_[:, 0:T], func=AF.Silu)
            qb, kb, Qg = tmpb[3], tmpb[4], tmpb[5]
            for x_, scl, xb_ in ((qn, 128.0 ** -0.5, qb), (kn, 1.0, kb)):
                act([x_], [sq_], out=sq_[:, 0:T], in_=x_[:, 0:T], func=AF.Square)
                a_ = nacc()
                mm(a_, a_[:, 0:T], ones, sq_[:, 0:T], True, True, [cst, sq_])
                rsqrt_to(rn_, rn_[:, 0:T], a_, a_[:, 0:T], 1.0, 1e-6, sc2, sc2[:, 0:T])
                dve(KW("scalar_tensor_tensor", out=x_[:, 0:T], in0=x_[:, 0:T], scalar=scl, in1=rn_[:, 0:T], op0=ALU.mult, op1=ALU.mult), [x_, rn_], [x_])
                dve(KW("tensor_copy", out=xb_[:, 0:T], in_=x_[:, 0:T]), [x_], [xb_])
            dump('gq', qn, qn[:, 0:T]); dump('gk', kn, kn[:, 0:T]); dump('gv', vn, vn[:, 0:T])
            if kind == "p":
                Sf = pSf[h]
            else:
                Sf = Sf_w
                load(Sf_w, Sf_w[:], d_sgdn[h])
            P.op("pool", KW("tensor_copy", out=Sb[:, 0:nseg, :], in_=Sf[:, 0:nseg, :]), [Sf], [Sb])
            egB, eDT, dsT, diT, XTf = tmpf[18], tmpf[19], tmpf[20], tmpf[21], tmpf[22]
            for cp in range((nch + 1) // 2):
                for hh in range(2):
                    ch = min(cp * 2 + hh, nch - 1)
                    cc = slice(ch * C, (ch + 1) * C)
                    col = lambda j_: tk[:, ch, j_, h:h + 1]
                    if hh == 0 or cp * 2 + hh < nch:
                        for src, dst, scol in ((kn, 1, col(5)), (vn, 2, None)):
                            a_ = nacc()
                            P.op("pe", KW("transpose", a_[0:C, 0:128], src[:, cc], ident), [src, cst], [a_])
                            if scol is None:
                                act([a_], [tokb[dst]], out=tokb[dst][:, ch, :], in_=a_[0:C, 0:128], func=AF.Copy)
                            else:
                                act([a_, tk], [tokb[dst]], out=tokb[dst][:, ch, :], in_=a_[0:C, 0:128], func=AF.Copy, scale=scol)
                    mm(smp[0], smp[0][0:C, 0:C], kb[:, cc], kb[:, cc], True, True, [kb])
                    mm(smp[0], smp[0][0:C, C:2 * C], kb[:, cc], qb[:, cc], True, True, [kb, qb])
                    dve(KW("tensor_scalar", out=sc2[0:C, 0:C], in0=mskIf, scalar1=col(1), scalar2=None, op0=ALU.mult), [cst, tk], [sc2])
                    mm(smp[1], smp[1][:, 0:C], cst[0:C, 2, :], sc2[0:C, 0:C], True, True, [cst, sc2])
                    dve(KW("tensor_scalar", out=eDT[0:C, 0:C], in0=smp[1][0:C, 0:C], scalar1=col(2), scalar2=0.0, op0=ALU.subtract, op1=ALU.min), [smp[1], tk], [eDT])
                    act([eDT], [eDT], out=eDT[0:C, 0:C], in_=eDT[0:C, 0:C], func=AF.Exp)
                    act([smp[1]], [egB], out=egB[:, 0:C], in_=smp[1][:, 0:C], func=AF.Exp)
                    dve(KW("scalar_tensor_tensor", out=dsT[0:C, 0:C], in0=eDT[0:C, 0:C], scalar=col(0), in1=mskSf, op0=ALU.mult, op1=ALU.mult), [eDT, tk, cst], [dsT])
                    dve(KW("scalar_tensor_tensor", out=XTf[0:C, 0:C], in0=smp[0][0:C, 0:C], scalar=-1.0, in1=dsT[0:C, 0:C], op0=ALU.mult, op1=ALU.mult), [smp[0], dsT], [XTf])
                    P.op("pool", KW("tensor_copy", out=invP[0][:, hh, 1, :], in_=XTf[0:C, 0:C]), [XTf], [invP[0]])
                    dve(KW("tensor_tensor", out=invT[0][:, hh, :], in0=XTf[0:C, 0:C], in1=cst[0:C, 0, 0:C], op=ALU.add), [XTf, cst], [invT[0]])
                    P.op("pe", KW("transpose", smp[1][0:C, 128:128 + C], XTf[0:C, 0:C], cst[0:C, 0, 0:C]), [XTf, cst], [smp[1]])
                    act([smp[1]], [invP[0]], out=invP[0][:, hh, 0, :], in_=smp[1][0:C, 128:128 + C], func=AF.Copy)
                    if hh == 0 or cp * 2 + hh < nch:
                        if full:
                            dve(KW("tensor_tensor", out=diT[0:C, 0:C], in0=eDT[0:C, 0:C], in1=mskIf, op=ALU.mult), [eDT, cst], [diT])
                            dve(KW("tensor_tensor", out=ATs[:, ch, :], in0=smp[0][0:C, C:2 * C], in1=diT[0:C, 0:C], op=ALU.mult), [smp[0], diT], [ATs])
                            dve(KW("tensor_tensor", out=Qg[:, cc], in0=qn[:, cc], in1=egB[:, 0:C], op=ALU.mult), [qn, egB], [Qg])
                        if nseg == 1:
                            P.op("pool", KW("tensor_copy", out=cds[:, ch, 0:1], in_=egB[:, C - 1:C]), [egB], [cds])
                        else:
                            P.op("pool", KW("tensor_copy", out=cds[:, ch, :], in_=egB[:, 0:C].rearrange("p (s l) -> p s l", l=LS)[:, :, LS - 1]), [egB], [cds])
                tt = inverse(nlev)
                for hh in range(2):
                    if cp * 2 + hh < nch:
                        P.op("pool", KW("tensor_copy", out=T2T[:, cp * 2 + hh, :], in_=invT[tt][:, hh, :]), [invT[tt]], [T2T])
            dump('T2T', T2T, T2T[:, 0, :], C); dump('ATs', ATs, ATs[:, 0, :], C)
            oT = tmpf[24]
            for ch in range(nch):
                cc = slice(ch * C, (ch + 1) * C)
                col = lambda j_: tk[:, ch, j_, h:h + 1]
                if nseg > 1:
                    dve(KW("tensor_tensor", out=expa[0][:], in0=kb[:, cc].unsqueeze(1).broadcast_to([128, nseg, C]), in1=cm[:], op=ALU.mult), [kb, cm], [expa[0]])
                    if full:
                        dve(KW("tensor_tensor", out=expa[1][:], in0=Qg[:, cc].unsqueeze(1).broadcast_to([128, nseg, C]), in1=cm[:], op=ALU.mult), [Qg, cm], [expa[1]])
                    dve(KW("tensor_tensor", out=expt[0][:], in0=tokb[1][:, ch, :].unsqueeze(1).broadcast_to([C, nseg, 128]),
                           in1=smk[:].unsqueeze(2).broadcast_to([C, nseg, 128]), op=ALU.mult), [tokb[1], smk], [expt[0]])
                    K_s = lambda s_: expa[0][:, s_, :]
                    Q_s = lambda s_: expa[1][:, s_, :]
                    Kt_s = lambda s_: expt[0][:, s_, :]
                    rdK, rdQ, rdT = [expa[0]], [expa[1]], [expt[0]]
                else:
                    K_s = lambda s_: kb[:, cc]
                    Q_s = lambda s_: Qg[:, cc]
                    Kt_s = lambda s_: tokb[1][:, ch, :]
                    rdK, rdQ, rdT = [kb], [Qg], [tokb[1]]
                ks_ap = smp[0][0:C, 0:128]
                for s_ in range(nseg):
                    mm(kso, ks_ap, K_s(s_), Sb[:, s_, :], s_ == 0, s_ == nseg - 1, rdK + [Sb])
                dve(KW("scalar_tensor_tensor", out=chs[:], in0=ks_ap, scalar=col(4), in1=tokb[2][:, ch, :], op0=ALU.mult, op1=ALU.add), [kso, tk, tokb[2]], [chs])
                w_ap = smp[1][0:C, 128:256]
                mm(kw_, w_ap, T2T[:, ch, :], chs[:], True, True, [T2T, chs])
                act([kw_, tk], [vnew], out=vnew[:], in_=w_ap, func=AF.Copy, scale=col(0))
                if full and nseg == 1:
                    o_ap = smp[0][:, 256:256 + C]
                    mm(koo, o_ap, Sb[:, 0, :], Q_s(0), True, False, rdQ + [Sb])
                    mm(koo, o_ap, vnew[:], ATs[:, ch, :], False, True, [vnew, ATs])
                    act([koo], [oT], out=oT[:, cc], in_=o_ap, func=AF.Copy)
                elif full:
                    ot_ap = smp[0][0:C, 256:384]
                    for s_ in range(nseg):
                        mm(koo, ot_ap, Q_s(s_), Sb[:, s_, :], s_ == 0, False, rdQ + [Sb])
                    mm(koo, ot_ap, ATs[:, ch, :], vnew[:], False, True, [ATs, vnew])
                    act([koo], [sc2], out=sc2[0:C, 0:128], in_=ot_ap, func=AF.Copy)
                    a_ = nacc()
                    P.op("pe", KW("transpose", a_[:, 0:C], sc2[0:C, 0:128], cst[0:C, 0, 0:C]), [sc2, cst], [a_])
                    act([a_], [oT], out=oT[:, cc], in_=a_[:, 0:C], func=AF.Copy)
                for s_ in range(nseg):
                    su = sun[s_ % 2]
                    su_ap = smp[2 + (s_ % 2)][:, 384:512]
                    mm(su, su_ap, Kt_s(s_), vnew[:], True, True, rdT + [vnew])
                    dve(KW("scalar_tensor_tensor", out=Sf[:, s_, :], in0=Sf[:, s_, :], scalar=cds[:, ch, s_:s_ + 1], in1=su_ap, op0=ALU.mult, op1=ALU.add), [Sf, cds, su], [Sf])
                    P.op("pool", KW("tensor_copy", out=Sb[:, s_, :], in_=Sf[:, s_, :]), [Sf], [Sb])
            if kind == "s":
                store(o_sgdn[h], Sf_w, Sf_w[:])
            elif si == g.NSLAB - 1:
                store(o_pgdn[h], Sf, Sf[:])
            if full:
                dump('goT', oT, oT[:, 0:T])
                act([oT], [sq_], out=sq_[:, 0:T], in_=oT[:, 0:T], func=AF.Square)
                a_ = nacc()
                mm(a_, a_[:, 0:T], ones, sq_[:, 0:T], True, True, [cst, sq_])
                rsqrt_to(rn_, rn_[:, 0:T], a_, a_[:, 0:T], 1.0 / 128, NORM_EPS, sc2, sc2[:, 0:T])
                dve(KW("tensor_tensor", out=sc2[:, 0:T], in0=oT[:, 0:T], in1=rn_[:, 0:T], op=ALU.mult), [oT, rn_], [sc2])
                dve(KW("scalar_tensor_tensor", out=ybT[:, h, 0:T], in0=sc2[:, 0:T], scalar=vec[:, g.v_gnw:g.v_gnw + 1], in1=zs[:, 0:T], op0=ALU.mult, op1=ALU.mult), [sc2, vec, zs], [ybT.r[h]])
        if kind == "p" and si == g.NSLAB - 1:
            store(o_pconv, pconv, pconv[:])
        if dbg and dbg["what"] == "ybT" and dbg["slab"] == (kind, si):
            t0 = tmpf[5]
            for c in range(GH):
                P.op("pool", KW("tensor_copy", out=t0[:, 0:T], in_=ybT[:, c, 0:T]), [ybT.r[c]], [t0])
                store(o_dbg[:, c, 0:T], t0, t0[:, 0:T])
            return
        if dbg and dbg['what'] in ('ybT', 'gq', 'gk', 'gv', 'T2T', 'ATs', 'goT'):
            return

        if not full:
            if si == g.NSLAB // 2 - 1:
                for t_ in pZf + pSf + [pshift, pconv]:
                    fl = t_[:].rearrange("p a b -> p (a b)") if len(t_.t.shape) == 3 else t_[:].rearrange("p a b c -> p (a b c)")
                    dve(KW("tensor_scalar", out=fl, in0=fl, scalar1=flag[:, 0:1], scalar2=None, op0=ALU.mult), [t_, flag], [t_])
            return
        for c in range(KC):
            sg = []
            for tid, wsrc, nk, yT in ((g.t_ma + c, d_woa[c], NHP, yaT), (g.t_mb + c, d_wob[c], GH, ybT)):
                a_ = proj(tid)
                sgt = tmpf[3 + len(sg)]
                act([a_], [sgt], out=sgt[:, 0:T], in_=a_[:, 0:T], func=AF.Sigmoid)
                wt = wget(wsrc.rearrange("p k n -> p (k n)"), 128, nk * 128)
                a2 = nacc()
                for kc in range(nk):
                    mm(a2, a2[:, 0:T], wt[:, kc * 128:(kc + 1) * 128], yT[:, kc, 0:T], kc == 0, kc == nk - 1, [wt, yT.r[kc]])
                dve(KW("tensor_tensor", out=sgt[:, 0:T], in0=sgt[:, 0:T], in1=a2[:, 0:T], op=ALU.mult), [sgt, a2], [sgt])
                sg.append(sgt)
            dve(KW("tensor_tensor", out=mgT[:, c, 0:T], in0=sg[0][:, 0:T], in1=sg[1][:, 0:T], op=ALU.add), [sg[0], sg[1]], [mgT.r[c]])
        dump_chunks('mgT', mgT, KC, T, kind, si)
        for c in range(KC):
            wt = wget(d_wo[c].rearrange("p k n -> p (k n)"), 128, KC * 128)
            a_ = nacc()
            for kc in range(KC):
                mm(a_, a_[:, 0:T], wt[:, kc * 128:(kc + 1) * 128], mgT[:, kc, 0:T], kc == 0, kc == KC - 1, [wt, mgT.r[kc]])
            xt_, xap = xload(c)
            seg_affine("dve", x1T[:, c, :], a_, a_, 2, c, xt_, xap, [a_, mT, xt_], [x1T.r[c]])
        dump_chunks('x1T', x1T, KC, T, kind, si)
        stats_rstd(lambda c: (x1T.r[c], x1T[:, c, 0:T]), KC, 1.0 / D, NORM_EPS)
        for c in range(KC):
            t0 = tmpf[3 + c % 2]
            dve(KW("tensor_tensor", out=t0[:, 0:T], in0=x1T[:, c, 0:T], in1=rstd[:, 0:T], op=ALU.mult), [x1T.r[c], rstd], [t0])
            seg_affine("act", hT[:, c, :], t0, t0, 4, c, None, (3, c), [t0, mT], [hT.r[c]])
        dump_chunks('h2T', hT, KC, T, kind, si)
        for g8 in range(g.FC // 8):
            for j in range(8):
                wt = wget(d_wup[g8 * 8 + j].rearrange("p k n -> p (k n)"), 128, KC * 128)
                a_ = nacc()
                for kc in range(KC):
                    mm(a_, a_[:, 0:T], wt[:, kc * 128:(kc + 1) * 128], hT[:, kc, 0:T], kc == 0, kc == KC - 1, [wt, hT.r[kc]])
                t0 = tmpf[3 + j % 2]
                act([a_], [t0], out=t0[:, 0:T], in_=a_[:, 0:T], func=AF.Relu)
                P.op("pool" if j % 2 else "dve", KW("tensor_tensor", out=uh[:, j, 0:T], in0=t0[:, 0:T], in1=t0[:, 0:T], op=ALU.mult), [t0], [uh.r[j]])
            for c in range(KC):
                wt = wget(d_wdn[g8, c].rearrange("p k n -> p (k n)"), 128, 8 * 128)
                a_ = nacc()
                for j in range(8):
                    mm(a_, a_[:, 0:T], wt[:, j * 128:(j + 1) * 128], uh[:, j, 0:T], j == 0, j == 7, [wt, uh.r[j]])
                seg_affine("dve", x1T[:, c, :], a_, a_, 5, c, x1T.r[c], x1T[:, c, :], [a_, mT, x1T.r[c]], [x1T.r[c]])
        dump_chunks('x2T', x1T, KC, T, kind, si)
        stats_rstd(lambda c: (x1T.r[c], x1T[:, c, 0:T]), KC, 1.0 / D, NORM_EPS)
        for c in range(KC):
            i = c % 2
            dve(KW("tensor_tensor", out=ych[i][:, 0:T], in0=x1T[:, c, 0:T], in1=rstd[:, 0:T], op=ALU.mult), [x1T.r[c], rstd], [ych[i]])
            dve(KW("tensor_scalar", out=ych[i][:, 0:T], in0=ych[i][:, 0:T], scalar1=vec[:, g.v_fw + c:g.v_fw + c + 1], scalar2=None, op0=ALU.mult), [ych[i], vec], [ych[i]])
            dst = o_yp[si - g.NSLAB // 2][:, c, :] if kind == "p" else o_ys[:, c, :]
            P.dma("pool", ych_s[i], dst, ych[i][:, 0:T], reads=[ych[i]])

    try:
        _run_slabs()
    except StopBuild:
        pass
    return nc, P


def tile_cols(Wm, col_lists):
    K = Wm.shape[0]
    out = np.zeros((len(col_lists), 128, K // 128, 128), np.float32)
    for t, (c0, n) in enumerate(col_lists):
        blk = Wm[:, c0:c0 + n].reshape(K // 128, 128, n)
        out[t, :, :, :n] = blk.transpose(1, 0, 2)
    return out


def fm(v, n):
    return np.ascontiguousarray(np.asarray(v, np.float32).reshape(n, 128).T)


def host_shared(cfg, inp):
    g = cfg
    D, KC, NHP, GH, RW, GW = g.D, g.KC, g.NHP, g.GH, g.RW, g.GW
    sh = {}
    w_in = np.asarray(inp["w_in"][0]); w_ada = np.asarray(inp["w_ada"][0])
    sh["wada"] = tile_cols(w_ada, [(j * D + c * 128, 128) for j in range(6) for c in range(KC)])
    cols = [(i * 128, 128) for i in range(g.RC)]
    G0 = g.RCOLS
    cols += [(G0 + i * 128, 128) for i in range(4 * GH)]
    cols += [(G0 + 4 * GW, GH), (G0 + 4 * GW + GH, GH)]
    M0 = g.RCOLS + g.GCOLS
    cols += [(M0 + i * 128, 128) for i in range(2 * KC)]
    assert len(cols) == g.NT_IN
    sh["win"] = tile_cols(w_in, cols)
    sh["ww2"] = np.concatenate([inp["r_w_w2"][0], inp["r_w_a2"][0]], 0).astype(np.float32)
    sh["wg2"] = np.ascontiguousarray(np.asarray(inp["r_w_g2"][0]).reshape(2, 128, RW).transpose(1, 0, 2))
    woa = np.asarray(inp["w_out_a"][0])
    sh["woa"] = tile_cols(woa, [(c * 128, 128) for c in range(KC)])
    wob = np.asarray(inp["w_out_b"][0])
    sh["wob"] = tile_cols(wob, [(c * 128, 128) for c in range(KC)])
    sh["wo"] = tile_cols(np.asarray(inp["w_out"][0]), [(c * 128, 128) for c in range(KC)])
    sh["wup"] = tile_cols(np.asarray(inp["w_up"][0]), [(c * 128, 128) for c in range(g.FC)])
    wdn = np.asarray(inp["w_down"][0])
    sh["wdn"] = np.ascontiguousarray(wdn.reshape(g.FC // 8, 8, 128, KC, 128).transpose(0, 3, 2, 1, 4))
    v = np.zeros((128, g.NV), np.float32)
    v[:, g.v_bada:g.v_bada + 6 * KC] = fm(inp["b_ada"][0], 6 * KC)
    v[:, g.v_n1:g.v_n1 + KC] = fm(inp["norm1_w"][0], KC)
    v[:, g.v_n2:g.v_n2 + KC] = fm(inp["norm2_w"][0], KC)
    v[:, g.v_fw:g.v_fw + KC] = fm(inp["final_norm_w"], KC)
    v[:, g.v_mu:g.v_mu + g.RC] = fm(inp["r_mu"][0], g.RC)
    for nm, col in (("r_w0", g.v_w0), ("r_a0", g.v_a0), ("r_k_k", g.v_kk), ("r_k_a", g.v_ka), ("r_lnx_w", g.v_lw), ("r_lnx_b", g.v_lb)):
        v[:, col:col + NHP] = fm(inp[nm][0], NHP)
    v[:, g.v_rk:g.v_rk + NHP] = fm(np.asarray(inp["r_r_k"][0]).reshape(-1), NHP)
    cw = np.asarray(inp["g_conv_w"][0])
    for i in range(4):
        v[:, g.v_cw + i * 3 * GH:g.v_cw + (i + 1) * 3 * GH] = fm(cw[i], 3 * GH)
    v[:, g.v_gnw] = np.asarray(inp["g_norm_w"][0])
    sh["vecsT"] = v
    sh["gsc"] = np.concatenate([np.tile(np.asarray(inp["g_a_log"][0])[None], (128, 1)), np.tile(np.asarray(inp["g_dt_bias"][0])[None], (128, 1))], 1).astype(np.float32)
    C = g.C
    cst = np.zeros((128, 10, 128), np.float32)
    cst[:, 0] = np.eye(128)
    bo = np.zeros((128, 128)); bo[:64, :64] = 1; bo[64:, 64:] = 1
    cst[:, 1] = bo
    cst[:, 2] = 1.0
    ii = np.arange(C)
    cst[:C, 3, :C] = (ii[None, :] > ii[:, None])
    cst[:C, 4, :C] = (ii[None, :] >= ii[:, None])
    sg = ii // g.LS
    same = sg[None, :] == sg[:, None]
    cst[:C, 5, :C] = (ii[None, :] > ii[:, None]) & same
    cst[:C, 6, :C] = (ii[None, :] >= ii[:, None]) & same
    cst[:C, 7, :C] = same
    cst[:C, 8, :C] = (ii[None, :] < ii[:, None])
    cst[:C, 9, :C] = (ii[None, :] < ii[:, None]) & same
    sh["consts"] = cst
    cmk = np.zeros((128, g.NS, C), np.float32)
    smk = np.zeros((C, g.NS), np.float32)
    for s_ in range(g.NS):
        cmk[:, s_, s_ * g.LS:(s_ + 1) * g.LS] = 1; smk[s_ * g.LS:(s_ + 1) * g.LS, s_] = 1
    import ml_dtypes
    sh["cm"] = cmk.astype(ml_dtypes.bfloat16); sh["sm"] = smk
    TM = max(g.TP, g.TS)
    r = np.ones((128, 2, TM), np.float32)
    r[:, 0, 0::C] = 0; r[:, 1, 0::g.LS] = 0
    sh["rst"] = r
    return sh


def host_core(cfg, inp, core):
    g = cfg
    KC, NS, LS = g.KC, g.NS, g.LS
    b = core // 2; half = core % 2
    m = {}
    xp = np.asarray(inp["x_prompt"][b])
    H = g.SEQ // 2
    if half == 1:
        toks = xp
    else:
        toks = np.concatenate([xp[:H], xp[:H]], 0)
    m["xp"] = np.ascontiguousarray(toks.reshape(g.NSLAB, g.TP, KC, 128).transpose(0, 3, 2, 1))
    sl = slice(core * NS, (core + 1) * NS)
    xs = np.asarray(inp["x_sample"][sl]).reshape(NS * LS, KC, 128)
    m["xs"] = np.ascontiguousarray(xs.transpose(2, 1, 0))
    cc = np.concatenate([np.asarray(inp["c_prompt"][b])[None], np.asarray(inp["c_sample"][sl])], 0)
    m["cT"] = np.ascontiguousarray(cc.reshape(1 + NS, KC, 128).transpose(2, 1, 0))
    m["flag"] = np.full((128, 1), float(half), np.float32)
    wkv = np.asarray(inp["state_rwkv_wkv"][0, sl])
    m["s_wkv"] = np.ascontiguousarray(wkv.reshape(NS, g.NHP, 2, 64, 64).transpose(1, 2, 4, 0, 3).reshape(g.NHP, 128, NS, 64))
    shf = np.asarray(inp["state_rwkv_shift"][0, sl])
    m["s_shift"] = np.ascontiguousarray(shf.reshape(NS, g.RC, 128).transpose(2, 1, 0))
    gdn = np.asarray(inp["state_gdn"][0, sl])
    m["s_gdn"] = np.ascontiguousarray(gdn.transpose(1, 2, 0, 3))
    cv = np.asarray(inp["state_gdn_conv"][0, sl])
    m["s_conv"] = np.ascontiguousarray(cv.reshape(NS, 3, 3 * g.GH, 128).transpose(3, 2, 0, 1))
    return m


_CACHE = {}


def _get_prog(cfg_key):
    if cfg_key not in _CACHE:
        cfg = Cfg(*cfg_key)
        nc, P = build(cfg)
        if cfg.D < 4096:
            P.epoch = 300; P.dseg = 20
        from contextlib import ExitStack
        es = ExitStack()
        P.emit(es)
        _CACHE[cfg_key] = (cfg, nc, P, es)
    return _CACHE[cfg_key]


def run(cfg_key, inputs):
    cfg, nc, P, es = _get_prog(cfg_key)
    g = cfg
    inp = {k: np.asarray(v) for k, v in inputs.items()}
    sh = host_shared(cfg, inp)
    maps = []
    for c in range(8):
        m = dict(sh); m.update(host_core(cfg, inp, c)); maps.append(m)
    res = run_bass_kernel_spmd(nc, maps, core_ids=list(range(8)))
    R = res.results
    B = 4; D = g.D; NS = g.NS; LS = g.LS; KC = g.KC
    f32 = np.float32
    tokm = lambda a: np.ascontiguousarray(np.asarray(a).transpose(2, 1, 0)).reshape(a.shape[2], KC * 128)
    y_p = np.zeros((B, g.SEQ, D), f32); y_s = np.zeros((8 * NS, LS, D), f32)
    p_wkv = np.zeros((1, B, g.RH, 64, 64), f32); p_shift = np.zeros((1, B, g.RCOLS), f32)
    p_gdn = np.zeros((1, B, g.GH, 128, 128), f32); p_conv = np.zeros((1, B, 3, 3 * g.GW), f32)
    s_wkv = np.zeros((1, 8 * NS, g.RH, 64, 64), f32); s_shift = np.zeros((1, 8 * NS, g.RCOLS), f32)
    s_gdn = np.zeros((1, 8 * NS, g.GH, 128, 128), f32); s_conv = np.zeros((1, 8 * NS, 3, 3 * g.GW), f32)

    def wkv_host(a, n):
        a = np.asarray(a).reshape(g.NHP, 2, 64, n, 64)
        return np.ascontiguousarray(a.transpose(3, 0, 1, 4, 2)).reshape(n, g.RH, 64, 64)

    def shift_host(a, n):
        return np.ascontiguousarray(np.asarray(a).transpose(2, 1, 0)).reshape(n, g.RCOLS)

    def gdn_host(a, n):
        return np.ascontiguousarray(np.asarray(a).transpose(2, 0, 1, 3))

    def conv_host(a, n):
        return np.ascontiguousarray(np.asarray(a).transpose(2, 3, 1, 0)).reshape(n, 3, 3 * g.GW)

    H = g.SEQ // 2
    for c in range(8):
        b = c // 2; half = c % 2
        r = R[c]
        for j in range(g.NSLAB // 2):
            y_p[b, half * H + j * g.TP: half * H + (j + 1) * g.TP] = tokm(r["o_yp"][j])
        y_s[c * NS:(c + 1) * NS] = tokm(r["o_ys"]).reshape(NS, LS, D)
        if half == 1:
            p_wkv[0, b] = wkv_host(r["o_pwkv"], 1)[0]; p_shift[0, b] = shift_host(r["o_pshift"], 1)[0]
            p_gdn[0, b] = gdn_host(r["o_pgdn"], 1)[0]; p_conv[0, b] = conv_host(r["o_pconv"], 1)[0]
        sl = slice(c * NS, (c + 1) * NS)
        s_wkv[0, sl] = wkv_host(r["o_swkv"], NS); s_shift[0, sl] = shift_host(r["o_sshift"], NS)
        s_gdn[0, sl] = gdn_host(r["o_sgdn"], NS); s_conv[0, sl] = conv_host(r["o_sconv"], NS)
    return (y_p, y_s, p_wkv, p_shift, p_gdn, p_conv, s_wkv, s_shift, s_gdn, s_conv)


def kernel(**inputs):
    return run((4096, 2048, 128, 16, 4), inputs)
```

```python
import numpy as np
import concourse.bass as bass
import concourse.mybir as mybir
from concourse.bass_utils import run_bass_kernel_spmd

F32 = mybir.dt.float32
BF16 = mybir.dt.bfloat16
AF = mybir.ActivationFunctionType
ALU = mybir.AluOpType
NORM_EPS = 1e-6
GN_EPS = 64e-5


class Cfg:
    def __init__(s, D=4096, SEQ=2048, TP=256, NS=16, LS=4):
        s.D = D; s.KC = D // 128; s.SEQ = SEQ; s.TP = TP
        s.NSLAB = SEQ // TP
        s.NS = NS; s.LS = LS; s.TS = NS * LS
        s.RW = D // 2; s.RH = s.RW // 64; s.NHP = s.RH // 2
        s.GW = D // 2; s.GH = s.GW // 128
        s.RC = 3 * s.NHP + 3
        s.RCOLS = 3 * s.RW + 384
        s.GCOLS = 4 * s.GW + 2 * s.GH
        s.FF = 4 * D; s.FC = s.FF // 128
        s.C = 64
        s.t_r = 0; s.t_k = s.NHP; s.t_v = 2 * s.NHP; s.t_lora = 3 * s.NHP
        s.t_gq = s.RC; s.t_gk = s.RC + s.GH; s.t_gv = s.RC + 2 * s.GH; s.t_gz = s.RC + 3 * s.GH
        s.t_gb = s.RC + 4 * s.GH; s.t_ga = s.t_gb + 1
        s.t_ma = s.t_ga + 1; s.t_mb = s.t_ma + s.KC
        s.NT_IN = s.t_mb + s.KC
        o = 0
        def take(n):
            nonlocal o
            r = o; o += n; return r
        s.v_bada = take(6 * s.KC); s.v_n1 = take(s.KC); s.v_n2 = take(s.KC); s.v_fw = take(s.KC)
        s.v_mu = take(s.RC); s.v_w0 = take(s.NHP); s.v_a0 = take(s.NHP); s.v_kk = take(s.NHP)
        s.v_ka = take(s.NHP); s.v_rk = take(s.NHP); s.v_lw = take(s.NHP); s.v_lb = take(s.NHP)
        s.v_cw = take(4 * 3 * s.GH); s.v_gnw = take(1)
        s.NV = o


def KW(m, *a, **k):
    return lambda en: getattr(en, m)(*a, **k)


class StopBuild(Exception):
    pass


class Res:
    __slots__ = ("w", "rs")
    def __init__(s):
        s.w = None; s.rs = []


class Til:
    def __init__(s, t, nres=1):
        s.t = t; s.r = [Res() for _ in range(nres)]
    def __getitem__(s, k):
        return s.t[k]


ENGS = ["pe", "act", "dve", "pool", "sp"]


class Prog:
    def __init__(s, nc):
        s.nc = nc
        s.q = {e: [] for e in ENGS}
        s.cnt = {e: 0 for e in ENGS}
        s.seen = {e: {} for e in ENGS}
        s.streams = []
        s.sb_off = 0
        s.rr = 0

    def sb(s, name, shape, dt, nres=1, at=None):
        esz = 4 if dt == F32 else 2
        n = 1
        for d in shape[1:]:
            n *= d
        nbytes = (n * esz + 31) // 32 * 32
        if at is None:
            at = s.sb_off
            s.sb_off += nbytes
        t = s.nc.alloc_sbuf_tensor("sb_" + name, list(shape), dt)
        return Til(t, nres)

    def stream(s):
        st = {"id": len(s.streams), "n": 0}
        s.streams.append(st)
        return st

    def _need(s, e, ev, waits, raw=False):
        if ev is None:
            return
        key, idx = ev
        if key == e and not (raw and e != "pe"):
            return
        if not isinstance(key, str):
            idx = s.streams[key[1]]["n"]
        if s.seen[e].get(key, 0) < idx:
            waits[key] = max(waits.get(key, 0), idx)

    def _deps(s, e, reads, writes):
        waits = {}
        for r in reads:
            s._need(e, r.w, waits, raw=True)
        for r in writes:
            s._need(e, r.w, waits)
            for x in r.rs:
                s._need(e, x, waits)
        for k, v in waits.items():
            s.seen[e][k] = v
        return list(waits.items())

    def op(s, e, fn, reads=(), writes=()):
        reads = [r for x in reads for r in (x.r if isinstance(x, Til) else [x])]
        writes = [r for x in writes for r in (x.r if isinstance(x, Til) else [x])]
        waits = s._deps(e, reads, writes)
        s.cnt[e] += 1
        me = (e, s.cnt[e])
        s.q[e].append(["op", fn, waits, s.cnt[e]])
        for r in reads:
            r.rs.append(me)
        for r in writes:
            r.w = me; r.rs = []

    def dma(s, q, st, out, in_, reads=(), writes=()):
        reads = [r for x in reads for r in (x.r if isinstance(x, Til) else [x])]
        writes = [r for x in writes for r in (x.r if isinstance(x, Til) else [x])]
        waits = s._deps(q, reads, writes)
        st["n"] += 1
        me = (("dma", st["id"]), st["n"])
        s.q[q].append(["dma", (out, in_), waits, st])
        for r in reads:
            r.rs.append(me)
        for r in writes:
            r.w = me; r.rs = []

    def barrier(s):
        for e in ENGS:
            waits = {}
            for f in ENGS:
                if f != e and s.cnt[f] > 0:
                    s._need(e, (f, s.cnt[f]), waits)
            for st in s.streams:
                if st["n"] > 0:
                    s._need(e, (("dma", st["id"]), st["n"]), waits)
            for k, v in waits.items():
                s.seen[e][k] = v
            if waits:
                s.q[e].append(["wait", None, list(waits.items()), None])

    def act(s, reads, writes, **kw):
        s.op("act", lambda e: e.activation(**kw), reads, writes)

    def emit(s, es):
        nc = s.nc
        EPOCH = getattr(s, 'epoch', 30000)
        DSEG = getattr(s, 'dseg', 1800)
        marked = {e: set() for e in ENGS}
        for e in ENGS:
            for item in s.q[e]:
                for key, idx in item[2]:
                    if isinstance(key, str):
                        marked[key].add(idx)
        rank = {}
        for e in ENGS:
            rank[e] = {idx: i for i, idx in enumerate(sorted(marked[e]))}
        sems = {e: [es.enter_context(nc.semaphore("sem_%s_%d" % (e, k))) for k in range(len(rank[e]) // EPOCH + 1)] for e in ENGS}
        dsems = [[es.enter_context(nc.semaphore("dsem%d_%d" % (st["id"], k))) for k in range((max(st["n"], 1) - 1) // DSEG + 1)] for st in s.streams]
        block = es.enter_context(nc.Block())

        def dma_wait(eng, sid, n):
            k = (n - 1) // DSEG
            if k > 0:
                eng.wait_ge(dsems[sid][k - 1], 16 * DSEG)
            eng.wait_ge(dsems[sid][k], 16 * (n - k * DSEG))

        def run(e, eng):
            dcnt = {}
            for item in s.q[e]:
                kind, payload, waits, extra = item
                for key, idx in waits:
                    if isinstance(key, str):
                        r = rank[key][idx]
                        eng.wait_ge(sems[key][r // EPOCH], r % EPOCH + 1)
                    else:
                        dma_wait(eng, key[1], idx)
                if kind == "op":
                    ins = payload(eng)
                    if extra in rank[e]:
                        ins.then_inc(sems[e][rank[e][extra] // EPOCH], 1)
                elif kind == "dma":
                    out, in_ = payload
                    sid = extra["id"]
                    dcnt[sid] = dcnt.get(sid, 0) + 1
                    eng.dma_start(out=out, in_=in_).then_inc(dsems[sid][(dcnt[sid] - 1) // DSEG], 16)
            if e == "sp":
                for st in s.streams:
                    if st["n"] > 0:
                        for k in range((st["n"] - 1) // DSEG + 1):
                            eng.wait_ge(dsems[st["id"]][k], 16 * min(DSEG, st["n"] - k * DSEG))

        @block.tensor
        def _(eng):
            run("pe", eng)

        @block.scalar
        def _(eng):
            run("act", eng)

        @block.vector
        def _(eng):
            run("dve", eng)

        @block.gpsimd
        def _(eng):
            run("pool", eng)

        @block.sync
        def _(eng):
            run("sp", eng)


def build(cfg, dbg=None):
    nc = bass.Bass("TRN2", target_bir_lowering=False)
    P = Prog(nc)
    P.final_waits = []
    g = cfg
    D, KC, TP, C = g.D, g.KC, g.TP, g.C
    NS, LS, TS = g.NS, g.LS, g.TS
    NHP, GH, RC = g.NHP, g.GH, g.RC
    TM = max(TP, TS)

    def din(name, shape, dt=F32):
        return nc.dram_tensor(name, list(shape), dt, kind="ExternalInput").ap()

    def dout(name, shape, dt=F32):
        return nc.dram_tensor(name, list(shape), dt, kind="ExternalOutput").ap()

    d_xp = din("xp", [g.NSLAB, 128, KC, TP])
    d_xs = din("xs", [128, KC, TS])
    d_cT = din("cT", [128, KC, 1 + NS])
    d_flag = din("flag", [128, 1])
    d_vecs = din("vecsT", [128, g.NV])
    d_gsc = din("gsc", [128, 2 * GH])
    d_const = din("consts", [128, 10, 128])
    d_cm = din("cm", [128, NS, C], BF16)
    d_sm = din("sm", [C, NS])
    d_rst = din("rst", [128, 2, TM])
    d_wada = din("wada", [6 * KC, 128, KC, 128])
    d_win = din("win", [g.NT_IN, 128, KC, 128])
    d_ww2 = din("ww2", [128, g.RW])
    d_wg2 = din("wg2", [128, 2, g.RW])
    d_woa = din("woa", [KC, 128, NHP, 128])
    d_wob = din("wob", [KC, 128, GH, 128])
    d_wo = din("wo", [KC, 128, KC, 128])
    d_wup = din("wup", [g.FC, 128, KC, 128])
    d_wdn = din("wdn", [g.FC // 8, KC, 128, 8, 128])
    d_swkv = din("s_wkv", [NHP, 128, NS, 64])
    d_sshift = din("s_shift", [128, RC, NS])
    d_sgdn = din("s_gdn", [GH, 128, NS, 128])
    d_sconv = din("s_conv", [128, 3 * GH, NS, 3])

    o_yp = dout("o_yp", [g.NSLAB // 2, 128, KC, TP])
    o_ys = dout("o_ys", [128, KC, TS])
    o_pwkv = dout("o_pwkv", [NHP, 128, 1, 64])
    o_pshift = dout("o_pshift", [128, RC, 1])
    o_pgdn = dout("o_pgdn", [GH, 128, 1, 128])
    o_pconv = dout("o_pconv", [128, 3 * GH, 1, 3])
    o_swkv = dout("o_swkv", [NHP, 128, NS, 64])
    o_sshift = dout("o_sshift", [128, RC, NS])
    o_sgdn = dout("o_sgdn", [GH, 128, NS, 128])
    o_sconv = dout("o_sconv", [128, 3 * GH, NS, 3])
    if dbg:
        o_dbg = dout("o_dbg", dbg["shape"])

    vec = P.sb("vec", [128, g.NV], F32)
    gsc = P.sb("gsc", [128, 2 * GH], F32)
    flag = P.sb("flag", [128, 1], F32)
    cst = P.sb("cst", [128, 10, 128], F32)
    cstb = P.sb("cstb", [128, 10, 128], BF16)
    cm = P.sb("cm", [128, NS, C], BF16)
    smk = P.sb("smk", [C, NS], F32)
    rst = P.sb("rst", [128, 2, TM], F32)
    mT = P.sb("mT", [128, 6, KC, 1 + NS], F32)
    cTf = P.sb("cTf", [128, KC, 1 + NS], F32)
    cTb = P.sb("cTb", [128, KC, 1 + NS], BF16)
    ww2b = P.sb("ww2b", [128, g.RW], BF16)
    wg2b = P.sb("wg2b", [128, 2, g.RW], BF16)
    okc = P.sb("okc", [128, NHP], F32)
    nw0 = P.sb("nw0", [128, NHP], F32)
    pZf = [P.sb("pZf%d" % i, [128, 1, 64], F32) for i in range(NHP)]
    pSf = [P.sb("pSf%d" % i, [128, 1, 128], F32) for i in range(GH)]
    pshift = P.sb("pshift", [128, RC, 1], F32)
    pconv = P.sb("pconv", [128, 3 * GH, 1, 3], F32)
    hT = P.sb("hT", [128, KC, TM], BF16, nres=KC)
    yaT = P.sb("yaT", [128, NHP, TM], BF16, nres=NHP)
    ybT = P.sb("ybT", [128, GH, TM], BF16, nres=GH)
    mgT = P.sb("mgT", [128, KC, TM], BF16, nres=KC)
    x1T = P.sb("x1T", [128, KC, TM], F32, nres=KC)
    rstd = P.sb("rstd", [128, TM], F32)
    NWB = 2
    WCOLS = max(KC * 128, 1024, 2 * g.RW)
    wst = [P.sb("wst%d" % i, [128, WCOLS], F32) for i in range(NWB)]
    wbf = [P.sb("wbf%d" % i, [128, WCOLS], BF16) for i in range(NWB)]
    wst_s = [P.stream() for _ in range(NWB)]
    NTMP = 26
    tmpf = [P.sb("tmp%d" % i, [128, TM + 8], F32) for i in range(NTMP)]
    NTB = 10
    tmpb = [P.sb("tmb%d" % i, [128, TM], BF16) for i in range(NTB)]
    NCH = TM // C
    tokb = [P.sb("tok%d" % i, [C, NCH, 128], BF16) for i in range(3)]
    pw = P.sb("pw", [C, 2, 4, C], BF16)
    invP = [P.sb("invP%d" % i, [C, 2, 2, C], F32) for i in range(2)]
    invT = [P.sb("invT%d" % i, [C, 2, C], F32) for i in range(2)]
    TTb = P.sb("TTb", [C, 2, C], BF16)
    chs = P.sb("chs", [C, 128], BF16)
    upad = P.sb("upad", [C, 2, 128], BF16)
    vpad = P.sb("vpad", [C, 2, 128], BF16)
    vnew = P.sb("vnew", [C, 128], BF16)
    expa = [P.sb("expa%d" % i, [128, NS, C], BF16) for i in range(2)]
    expt = [P.sb("expt%d" % i, [C, NS, 128], BF16) for i in range(2)]
    T2T = P.sb("T2T", [C, NCH, C], BF16)
    ATs = P.sb("ATs", [C, NCH, C], BF16)
    cds = P.sb("cds", [128, NCH, NS], F32)
    tk = P.sb("tk", [C, NCH, 6, GH], F32)
    Zf_w = P.sb("Zf_w", [128, NS, 64], F32)
    Zb = P.sb("Zb", [128, NS, 128], BF16)
    Sf_w = P.sb("Sf_w", [128, NS, 128], F32)
    Sb = Zb
    sst = P.sb("sst", [128, RC, NS], F32)
    cvin = [P.sb("cvin%d" % i, [128, NS, 3], F32) for i in range(2)]
    cvin_s = [P.stream() for _ in range(2)]
    cvout_s = P.stream()
    sso = P.sb("sso", [128, RC, NS], F32)
    xch = [P.sb("xch%d" % i, [128, TM], F32) for i in range(2)]
    xch_s = [P.stream() for _ in range(2)]
    ych = [P.sb("ych%d" % i, [128, TM], F32) for i in range(2)]
    ych_s = [P.stream() for _ in range(2)]
    uh = P.sb("uh", [128, 8, TM], BF16, nres=8)
    assert P.sb_off <= 208 * 1024, P.sb_off

    acc = [Til(nc.alloc_psum_tensor("acc%d" % i, [128, 512], F32)) for i in range(4)]
    smp = [Til(nc.alloc_psum_tensor("smp%d" % i, [128, 512], F32)) for i in range(4)]
    misc_s = P.stream()

    ident = cst[:, 0, :]; bones = cst[:, 1, :]; ones = cst[:, 2, :]

    def load(dst_til, dst_ap, src_ap, st=None, q="pool"):
        P.dma(q, st or misc_s, dst_ap, src_ap, writes=[dst_til])

    def store(dst_ap, src_til, src_ap, st=None, q="pool"):
        P.dma(q, st or misc_s, dst_ap, src_ap, reads=[src_til])

    cast_rr = [0]

    wplan = cfg.wplan if getattr(cfg, "wplan", None) else None
    wreq = []
    wissued = [0]

    scr = {
        "win": nc.dram_tensor("scr_win", [g.NT_IN, 128, KC * 128], BF16, kind="Internal").ap(),
        "woa": nc.dram_tensor("scr_woa", [KC, 128, NHP * 128], BF16, kind="Internal").ap(),
        "wob": nc.dram_tensor("scr_wob", [KC, 128, GH * 128], BF16, kind="Internal").ap(),
        "wo": nc.dram_tensor("scr_wo", [KC, 128, KC * 128], BF16, kind="Internal").ap(),
        "wup": nc.dram_tensor("scr_wup", [g.FC, 128, KC * 128], BF16, kind="Internal").ap(),
        "wdn": nc.dram_tensor("scr_wdn", [(g.FC // 8) * KC, 128, 8 * 128], BF16, kind="Internal").ap(),
    }
    scr_res = {}
    wbk_s = P.stream()

    def _wissue(k, key, src_ap, npart, ncols):
        i = k % NWB
        if key is not None and key in scr_res:
            P.dma("sp", wst_s[i], wbf[i][0:npart, 0:ncols], scr[key[0]][key[1]][0:npart, 0:ncols], reads=[scr_res[key]], writes=[wbf[i]])
            return
        P.dma("sp", wst_s[i], wst[i][0:npart, 0:ncols], src_ap, writes=[wst[i]])
        e = ["dve", "act", "pool"][cast_rr[0] % 3]
        cast_rr[0] += 1
        if e == "act":
            P.op("act", KW("activation", out=wbf[i][0:npart, 0:ncols], in_=wst[i][0:npart, 0:ncols], func=AF.Copy),
                 [wst[i]], [wbf[i]])
        else:
            P.op(e, KW("tensor_copy", out=wbf[i][0:npart, 0:ncols], in_=wst[i][0:npart, 0:ncols]),
                 [wst[i]], [wbf[i]])
        if key is not None:
            scr_res[key] = Res()
            P.dma("pool", wbk_s, scr[key[0]][key[1]][0:npart, 0:ncols], wbf[i][0:npart, 0:ncols], reads=[wbf[i]], writes=[scr_res[key]])

    def wget(src_ap, npart, ncols, key=None):
        k = len(wreq)
        wreq.append((key, src_ap, npart, ncols))
        if wplan is None:
            _wissue(k, key, src_ap, npart, ncols)
        else:
            while wissued[0] <= min(k + 1, len(wplan) - 1):
                _wissue(wissued[0], *wplan[wissued[0]])
                wissued[0] += 1
        return wbf[k % NWB]

    def mm(out_til, out_ap, lhsT, rhs, start, stop, reads):
        P.op("pe", KW("matmul", out_ap, lhsT=lhsT, rhs=rhs, start=start, stop=stop), reads, [out_til])

    def big_mm(acc_t, T, wt, act_til, kcn, npart=128, rhs_fn=None):
        for kc in range(kcn):
            mm(acc_t, acc_t[:, 0:T], wt[0:npart, kc * 128:(kc + 1) * 128], rhs_fn(kc), kc == 0, kc == kcn - 1,
               [wt, act_til.r[kc]])

    def dve(fn, reads, writes):
        P.op("dve", fn, reads, writes)

    def act(reads, writes, **kw):
        P.op("act", KW("activation", **kw), reads, writes)

    def rsqrt_to(dst_til, dst_ap, src_til, src_ap, scale, eps, tmp_til, tmp_ap):
        act([src_til, cst], [tmp_til], out=tmp_ap, in_=src_ap, func=AF.Sqrt, scale=scale, bias=epsc[eps])
        dve(KW("reciprocal", out=dst_ap, in_=tmp_ap), [tmp_til], [dst_til])

    load(vec, vec[:], d_vecs); load(gsc, gsc[:], d_gsc); load(flag, flag[:], d_flag)
    load(cst, cst[:], d_const); load(cm, cm[:], d_cm); load(smk, smk[:], d_sm); load(rst, rst[:], d_rst)
    load(cTf, cTf[:], d_cT)
    load(sst, sst[:], d_sshift)
    dve(KW("tensor_copy", out=cstb[:], in_=cst[:]), [cst], [cstb])
    P.dma("sp", wst_s[0], wst[0][:, 0:g.RW], d_ww2, writes=[wst[0]])
    dve(KW("tensor_copy", out=ww2b[:], in_=wst[0][:, 0:g.RW]), [wst[0]], [ww2b])
    P.dma("sp", wst_s[1], wst[1][:, 0:2 * g.RW], d_wg2.rearrange("p a n -> p (a n)"), writes=[wst[1]])
    dve(KW("tensor_copy", out=wg2b[:].rearrange("p a n -> p (a n)"), in_=wst[1][:, 0:2 * g.RW]), [wst[1]], [wg2b])
    epst = P.sb("epst", [128, 8], F32)
    epsc = {}
    for i, v in enumerate([NORM_EPS, GN_EPS, 1e-6, -0.5, 1.0, 0.0]):
        dve(KW("memset", epst[:, i:i + 1], v), [], [epst])
        epsc[v] = epst[:, i:i + 1]
    dve(KW("tensor_scalar", out=okc[:], in0=vec[:, g.v_ka:g.v_ka + NHP], scalar1=-1.0, scalar2=1.0, op0=ALU.mult, op1=ALU.add), [vec], [okc])
    dve(KW("tensor_scalar", out=nw0[:], in0=vec[:, g.v_w0:g.v_w0 + NHP], scalar1=-1.0, scalar2=None, op0=ALU.mult), [vec], [nw0])
    act([gsc], [gsc], out=gsc[:, 0:GH], in_=gsc[:, 0:GH], func=AF.Exp)
    dve(KW("tensor_scalar", out=gsc[:, 0:GH], in0=gsc[:, 0:GH], scalar1=-1.0, scalar2=None, op0=ALU.mult), [gsc], [gsc])
    act([cTf], [cTb], out=cTb[:], in_=cTf[:], func=AF.Silu)
    NSQ = 1 + NS
    for j in range(6):
        for c in range(KC):
            wt = wget(d_wada[j * KC + c].rearrange("p k n -> p (k n)"), 128, KC * 128)
            a_ = acc[(j * KC + c) % 2]
            for kc in range(KC):
                mm(a_, a_[:, 0:NSQ], wt[:, kc * 128:(kc + 1) * 128], cTb[:, kc, :], kc == 0, kc == KC - 1, [wt, cTb])
            col = g.v_bada + j * KC + c
            act([a_, vec], [mT], out=mT[:, j, c, :], in_=a_[:, 0:NSQ], func=AF.Identity, bias=vec[:, col:col + 1], scale=1.0)
    for slot, vcol in ((1, g.v_n1), (4, g.v_n2)):
        dve(KW("tensor_scalar", out=mT[:, slot], in0=mT[:, slot], scalar1=1.0, scalar2=None, op0=ALU.add), [mT], [mT])
        dve(KW("tensor_tensor", out=mT[:, slot], in0=mT[:, slot],
                                                              in1=vec[:, vcol:vcol + KC].unsqueeze(2).broadcast_to([128, KC, NSQ]), op=ALU.mult), [mT, vec], [mT])

    for t_ in pZf + pSf + [pshift, pconv]:
        dve(KW("memset", t_[:], 0.0), [], [t_])
    dve(KW("memset", Zb[:], 0.0), [], [Zb])
    dve(KW("memset", upad[:], 0.0), [], [upad])
    dve(KW("memset", vpad[:], 0.0), [], [vpad])

    def stop(n):
        if dbg and dbg.get('stop') == n:
            raise StopBuild()

    def dump(name, til, ap, np_=128):
        if dbg and dbg['what'] == name:
            t0 = tmpf[0]
            P.op("pool", KW("tensor_copy", out=t0[0:np_, 0:ap.shape[1]], in_=ap), [til], [t0])
            store(o_dbg[0:np_, 0, 0:ap.shape[1]], t0, t0[0:np_, 0:ap.shape[1]])
            raise StopBuild()

    def dump_chunks(name, til, n, T, kind, si):
        if dbg and dbg['what'] == name and dbg['slab'] == (kind, si):
            t0 = tmpf[5]
            for c in range(n):
                P.op("pool", KW("tensor_copy", out=t0[:, 0:T], in_=til[:, c, 0:T]), [til.r[c]], [t0])
                store(o_dbg[:, c, 0:T], t0, t0[:, 0:T])
            raise StopBuild()

    def _run_slabs():
        for si_ in range((dbg['slab'][1] + 1 if dbg['slab'][0] == 'p' else 0) if dbg else g.NSLAB):
            slab('p', si_)
        if not dbg or dbg['slab'][0] == 's':
            slab('s', 0)

    def slab(kind, si):
        if kind == "p":
            T = TP; nseg = 1; L = TP; full = si >= g.NSLAB // 2 or bool(dbg and dbg.get('allfull'))
            xsrc = d_xp[si]; seg0 = 0; ri = 0
            mskS = cstb[0:C, 3, 0:C]; mskI = cstb[0:C, 4, 0:C]; mskSf = cst[0:C, 3, 0:C]; mskIf = cst[0:C, 4, 0:C]
            segones = cst[0:C, 2, 0:C]
        else:
            T = TS; nseg = NS; L = LS; full = True
            xsrc = d_xs; seg0 = 1; ri = 1
            mskS = cstb[0:C, 5, 0:C]; mskI = cstb[0:C, 6, 0:C]; mskSf = cst[0:C, 5, 0:C]; mskIf = cst[0:C, 6, 0:C]
            segones = cst[0:C, 7, 0:C]
        nch = T // C
        W = nseg * (L + 3)

        def segv(ap2d, lo, n):
            return ap2d[:, 0:W].rearrange("p (s l) -> p s l", s=nseg)[:, :, lo:lo + n]

        def tokv(ap2d):
            return ap2d[:, 0:T].rearrange("p (s l) -> p s l", s=nseg)

        def seg_affine(eng, out_ap, in_til, in_ap, sc_slot, sc_c, add_til, add_ap, reads, writes):
            for sgi in range(nseg):
                sl = slice(sgi * L, (sgi + 1) * L)
                scol = mT[:, sc_slot, sc_c, seg0 + sgi:seg0 + sgi + 1]
                if isinstance(add_ap, tuple):
                    bcol = mT[:, add_ap[0], add_ap[1], seg0 + sgi:seg0 + sgi + 1]
                    act(reads, writes, out=out_ap[:, sl], in_=in_ap[:, sl], func=AF.Identity, bias=bcol, scale=scol)
                else:
                    dve(KW("scalar_tensor_tensor", out=out_ap[:, sl], in0=in_ap[:, sl], scalar=scol,
                                                                              in1=add_ap[:, sl], op0=ALU.mult, op1=ALU.add), reads, writes)

        def stats_rstd(src_fn, nchunks, scale, eps):
            a_ = acc[3]
            for c in range(nchunks):
                til, ap = src_fn(c)
                tq = tmpf[c % 2]
                act([til], [tq], out=tq[:, 0:T], in_=ap, func=AF.Square)
                mm(a_, a_[:, 0:T], ones, tq[:, 0:T], c == 0, c == nchunks - 1, [cst, tq])
            rsqrt_to(rstd, rstd[:, 0:T], a_, a_[:, 0:T], scale, eps, tmpf[2], tmpf[2][:, 0:T])

        def xload(c):
            i = c % 2
            P.dma("pool", xch_s[i], xch[i][:, 0:T], xsrc[:, c, :], writes=[xch[i]])
            return xch[i], xch[i][:, 0:T]
        stats_rstd(xload, KC, 1.0 / D, NORM_EPS)
        for c in range(KC):
            til, ap = xload(c)
            t0 = tmpf[c % 2]
            dve(KW("tensor_tensor", out=t0[:, 0:T], in0=ap, in1=rstd[:, 0:T], op=ALU.mult), [til, rstd], [t0])
            seg_affine("act", hT[:, c, :], t0, t0, 1, c, None, (0, c), [t0, mT], [hT.r[c]])
        if dbg and dbg["what"] == "hT" and dbg["slab"] == (kind, si):
            t0 = tmpf[5]
            for c in range(KC):
                P.op("pool", KW("tensor_copy", out=t0[:, 0:T], in_=hT[:, c, 0:T]), [hT.r[c]], [t0])
                store(o_dbg[:, c, 0:T], t0, t0[:, 0:T])
            return

        Cg = C if kind == "p" else LS
        ngrp = T // Cg
        nlev = 6 if kind == "p" else 2
        mskL = cst[0:C, 8 if kind == "p" else 9, 0:C]
        shsrc = pshift if kind == "p" else sst
        shdst = pshift if kind == "p" else sso
        acc_rr = [0]

        def nacc():
            acc_rr[0] += 1
            return acc[acc_rr[0] % 3]

        def proj(tid, ncols=128):
            wt = wget(d_win[tid].rearrange("p k n -> p (k n)"), 128, KC * 128, key=("win", tid))
            a_ = nacc()
            for kc in range(KC):
                mm(a_, a_[0:ncols, 0:T], wt[:, kc * 128:kc * 128 + ncols], hT[:, kc, 0:T], kc == 0, kc == KC - 1, [wt, hT.r[kc]])
            return a_

        def shift_mix(chunk, a_, xs_t):
            pe_, d_ = tmpf[3], tmpf[4]
            W1 = nseg * (L + 1)
            pv = pe_[:, 0:W1].rearrange("p (s l) -> p s l", s=nseg)
            act([a_], [pe_], out=pv[:, :, 1:L + 1], in_=tokv(a_.t), func=AF.Copy)
            P.op("pool", KW("tensor_copy", out=pv[:, :, 0:1], in_=shsrc[:, chunk, :].unsqueeze(2)), [shsrc], [pe_])
            P.op("pool", KW("tensor_copy", out=shdst[:, chunk, :].unsqueeze(2), in_=pv[:, :, L:L + 1]), [pe_], [shdst])
            dve(KW("tensor_tensor", out=tokv(d_.t), in0=pv[:, :, 0:L], in1=pv[:, :, 1:L + 1], op=ALU.subtract), [pe_], [d_])
            mucol = vec[:, g.v_mu + chunk:g.v_mu + chunk + 1]
            dve(KW("scalar_tensor_tensor", out=tokv(xs_t.t), in0=tokv(d_.t), scalar=mucol, in1=pv[:, :, 1:L + 1],
                                                   op0=ALU.mult, op1=ALU.add), [d_, pe_, vec], [xs_t])

        dve(KW("memset", Zb[:], 0.0), [], [Zb])
        ldw, sdg0, sdg1 = tmpb[0], tmpb[1], tmpb[2]
        xs_t = tmpf[5]
        a_ = proj(g.t_lora + 0); shift_mix(g.t_lora + 0, a_, xs_t)
        act([xs_t], [ldw], out=ldw[0:64, 0:T], in_=xs_t[0:64, 0:T], func=AF.Tanh)
        act([xs_t], [ldw], out=ldw[64:128, 0:T], in_=xs_t[64:128, 0:T], func=AF.Copy)
        for j, sd in ((1, sdg0), (2, sdg1)):
            a_ = proj(g.t_lora + j); shift_mix(g.t_lora + j, a_, xs_t)
            if full:
                act([xs_t], [sd], out=sd[:, 0:T], in_=xs_t[:, 0:T], func=AF.Sigmoid)

        stop(1)
        chn = {"rhs": smp[0], "u": smp[1], "o": smp[0]}
        sun = [smp[2], smp[3]]
        tts = smp[1]
        ivp = smp[0]
        B_RHS, B_U, B_O, B_IVP, B_TTS = smp[0], smp[1], smp[0], smp[0], smp[1]

        def inverse(nl):
            pp, tt = 0, 0
            for lev in range(1, nl):
                last = lev == nl - 1
                for hh in range(2):
                    mm(ivp, B_IVP[0:C, (hh * 2) * C:(hh * 2 + 1) * C], invP[pp][:, hh, 1, :], invP[pp][:, hh, 0, :], True, True, [invP[pp]])
                    if True:
                        mm(ivp, B_IVP[0:C, (hh * 2 + 1) * C:(hh * 2 + 2) * C], invP[pp][:, hh, 0, :], invP[pp][:, hh, 1, :], True, True, [invP[pp]])
                stop(51)
                pn = 1 - pp
                act([ivp], [invP[pn]], out=invP[pn][:].rearrange("p a b c -> p (a b c)"), in_=B_IVP[0:C, 0:4 * C], func=AF.Copy)
                stop(52)
                for hh in range(2):
                    mm(tts, B_TTS[0:C, 4 * C + hh * C:4 * C + (hh + 1) * C], invP[pn][:, hh, 0, :], invT[tt][:, hh, :], True, True, [invP[pn], invT[tt]])
                stop(53)
                tn = 1 - tt
                dve(KW("tensor_tensor", out=invT[tn][:].rearrange("p a c -> p (a c)"), in0=B_TTS[0:C, 4 * C:6 * C],
                                                              in1=invT[tt][:].rearrange("p a c -> p (a c)"), op=ALU.add), [tts, invT[tt]], [invT[tn]])
                pp, tt = pn, tn
                stop(60 + lev)
            return tt

        for hp in range(NHP):
            xr, xk, xv = tmpf[6], tmpf[7], tmpf[8]
            for tid, xt_ in ((g.t_r + hp, xr), (g.t_k + hp, xk), (g.t_v + hp, xv)):
                a_ = proj(tid); shift_mix(tid, a_, xt_)
            hc = slice(hp * 128, (hp + 1) * 128)
            e2, at_, gt_, kk, sc_, kt_, kka, bon, cs, E1, E2, E3, E4, Bh, Kh, oT, sc2 = (tmpf[i] for i in range(9, 26))
            a_ = nacc()
            mm(a_, a_[:, 0:T], ww2b[0:64, hc], ldw[0:64, 0:T], True, True, [ww2b, ldw])
            act([a_, nw0], [e2], out=e2[:, 0:T], in_=a_[:, 0:T], func=AF.Exp, scale=-1.0, bias=nw0[:, hp:hp + 1])
            act([e2, epst], [sc_], out=sc_[:, 0:T], in_=e2[:, 0:T], func=AF.Ln, bias=epsc[1.0], scale=1.0)
            act([sc_, epst], [e2], out=e2[:, 0:T], in_=sc_[:, 0:T], func=AF.Exp, scale=-1.0, bias=epsc[-0.5])
            a_ = nacc()
            mm(a_, a_[:, 0:T], ww2b[64:128, hc], ldw[64:128, 0:T], True, True, [ww2b, ldw])
            act([a_, vec], [at_], out=at_[:, 0:T], in_=a_[:, 0:T], func=AF.Sigmoid, bias=vec[:, g.v_a0 + hp:g.v_a0 + hp + 1], scale=1.0)
            if full:
                a_ = nacc()
                mm(a_, a_[:, 0:T], wg2b[:, 0, hc], sdg0[:, 0:T], True, False, [wg2b, sdg0])
                mm(a_, a_[:, 0:T], wg2b[:, 1, hc], sdg1[:, 0:T], False, True, [wg2b, sdg1])
                act([a_], [gt_], out=gt_[:, 0:T], in_=a_[:, 0:T], func=AF.Copy)
            dve(KW("tensor_scalar", out=kk[:, 0:T], in0=xk[:, 0:T], scalar1=vec[:, g.v_kk + hp:g.v_kk + hp + 1], scalar2=None, op0=ALU.mult), [xk, vec], [kk])
            act([kk], [sc_], out=sc_[:, 0:T], in_=kk[:, 0:T], func=AF.Square)
            a_ = nacc()
            mm(a_, a_[:, 0:T], bones, sc_[:, 0:T], True, True, [cst, sc_])
            rsqrt_to(sc_, sc_[:, 0:T], a_, a_[:, 0:T], 1.0, 1e-6, sc2, sc2[:, 0:T])
            dve(KW("tensor_tensor", out=kk[:, 0:T], in0=kk[:, 0:T], in1=sc_[:, 0:T], op=ALU.mult), [kk, sc_], [kk])
            dve(KW("tensor_scalar", out=kt_[:, 0:T], in0=at_[:, 0:T], scalar1=vec[:, g.v_ka + hp:g.v_ka + hp + 1], scalar2=okc[:, hp:hp + 1],
                                            op0=ALU.mult, op1=ALU.add), [at_, vec, okc], [kt_])
            dve(KW("tensor_tensor", out=kt_[:, 0:T], in0=kt_[:, 0:T], in1=xk[:, 0:T], op=ALU.mult), [kt_, xk], [kt_])
            dve(KW("tensor_tensor", out=kka[:, 0:T], in0=kk[:, 0:T], in1=at_[:, 0:T], op=ALU.mult), [kk, at_], [kka])
            if full:
                dve(KW("scalar_tensor_tensor", out=bon[:, 0:T], in0=xr[:, 0:T], scalar=vec[:, g.v_rk + hp:g.v_rk + hp + 1], in1=kt_[:, 0:T],
                                                       op0=ALU.mult, op1=ALU.mult), [xr, vec, kt_], [bon])
                a_ = nacc()
                mm(a_, a_[:, 0:T], bones, bon[:, 0:T], True, True, [cst, bon])
                dve(KW("tensor_tensor", out=bon[:, 0:T], in0=a_[:, 0:T], in1=xv[:, 0:T], op=ALU.mult), [a_, xv], [bon])
            dve(KW("tensor_tensor_scan", out=cs[:, 0:T], data0=rst[:, ri, 0:T], data1=e2[:, 0:T], initial=0.0, op0=ALU.mult, op1=ALU.add), [rst, e2], [cs])
            act([cs], [E1], out=E1[:, 0:T], in_=cs[:, 0:T], func=AF.Exp, scale=-1.0)
            act([cs], [E2], out=E2[:, 0:T], in_=cs[:, 0:T], func=AF.Exp, scale=1.0)
            dve(KW("tensor_tensor", out=E3[:, 0:T], in0=cs[:, 0:T], in1=e2[:, 0:T], op=ALU.subtract), [cs, e2], [E3])
            act([E3], [E3], out=E3[:, 0:T], in_=E3[:, 0:T], func=AF.Exp, scale=-1.0)
            gv = lambda t_: t_[:, 0:T].rearrange("p (n c) -> p n c", c=Cg)
            dve(KW("tensor_tensor", out=gv(E4), in0=gv(E2), in1=gv(E1)[:, :, Cg - 1:Cg].broadcast_to([128, ngrp, Cg]), op=ALU.mult), [E2, E1], [E4])
            At, Bt, Kt, Rt = tmpb[3], tmpb[4], tmpb[5], tmpb[6]
            dve(KW("scalar_tensor_tensor", out=At[:, 0:T], in0=kk[:, 0:T], scalar=-1.0, in1=E3[:, 0:T], op0=ALU.mult, op1=ALU.mult), [kk, E3], [At])
            dve(KW("tensor_tensor", out=Bt[:, 0:T], in0=kka[:, 0:T], in1=E2[:, 0:T], op=ALU.mult), [kka, E2], [Bt])
            dve(KW("tensor_tensor", out=Kt[:, 0:T], in0=kt_[:, 0:T], in1=E2[:, 0:T], op=ALU.mult), [kt_, E2], [Kt])
            if full:
                dve(KW("tensor_tensor", out=Rt[:, 0:T], in0=xr[:, 0:T], in1=E1[:, 0:T], op=ALU.mult), [xr, E1], [Rt])
            dve(KW("tensor_tensor", out=Bh[:, 0:T], in0=kka[:, 0:T], in1=E4[:, 0:T], op=ALU.mult), [kka, E4], [Bh])
            dve(KW("tensor_tensor", out=Kh[:, 0:T], in0=kt_[:, 0:T], in1=E4[:, 0:T], op=ALU.mult), [kt_, E4], [Kh])
            dump('xr', xr, xr[:, 0:T]); dump('xk', xk, xk[:, 0:T]); dump('e2', e2, e2[:, 0:T]); dump('a', at_, at_[:, 0:T]); dump('kk', kk, kk[:, 0:T])
            dump('kmod', kt_, kt_[:, 0:T]); dump('cs', cs, cs[:, 0:T]); dump('E4', E4, E4[:, 0:T])
            if full:
                dump('g', gt_, gt_[:, 0:T]); dump('bon', bon, bon[:, 0:T])
            stop(2)
            if kind == "p":
                Zf = pZf[hp]
            else:
                Zf = Zf_w
                load(Zf_w, Zf_w[:], d_swkv[hp])
            for hh in range(2):
                rs_ = slice(hh * 64, hh * 64 + 64)
                P.op("pool", KW("tensor_copy", out=Zb[rs_, 0:nseg, rs_], in_=Zf[rs_, 0:nseg, :]), [Zf], [Zb])
            for ch in range(nch):
                cc = slice(ch * C, (ch + 1) * C)
                for src, dst in ((Bh, 0), (Kh, 1), (xv, 2)):
                    a_ = nacc()
                    P.op("pe", KW("transpose", a_[0:C, 0:128], src[:, cc], ident), [src, cst], [a_])
                    act([a_], [tokb[dst]], out=tokb[dst][:, ch, :], in_=a_[0:C, 0:128], func=AF.Copy)
                stop(3)
                for hh in range(2):
                    rs_ = slice(hh * 64, hh * 64 + 64)
                    pairs = [(Bt, At), (Kt, At)] + ([(Bt, Rt), (Kt, Rt)] if full else [])
                    bk = smp[hh]
                    for sl_, (l_, r_) in enumerate(pairs):
                        mm(bk, bk[0:C, sl_ * C:(sl_ + 1) * C], l_[rs_, cc], r_[rs_, cc], True, True, [l_, r_])
                    mm(bk, bk[0:C, 4 * C:5 * C], At[rs_, cc], Bt[rs_, cc], True, True, [At, Bt])
                stop(35)
                for hh in range(2):
                    bk = smp[hh]
                    psv = bk[0:C, 0:4 * C].rearrange("p (s c) -> p s c", s=4)
                    dve(KW("tensor_tensor", out=pw[:, hh, 0:2, :], in0=psv[:, 0:2, :], in1=mskSf.unsqueeze(1).broadcast_to([C, 2, C]), op=ALU.mult), [bk, cst], [pw])
                    if full:
                        dve(KW("tensor_tensor", out=pw[:, hh, 2:4, :], in0=psv[:, 2:4, :], in1=mskIf.unsqueeze(1).broadcast_to([C, 2, C]), op=ALU.mult), [bk, cst], [pw])
                    dve(KW("tensor_tensor", out=invP[0][:, hh, 0, :], in0=bk[0:C, 4 * C:5 * C], in1=mskL, op=ALU.mult), [bk, cst], [invP[0]])
                    dve(KW("tensor_tensor", out=invP[0][:, hh, 1, :], in0=bk[0:C, 0:C], in1=mskSf, op=ALU.mult), [bk, cst], [invP[0]])
                stop(4)
                dve(KW("tensor_tensor", out=invT[0][:], in0=invP[0][:, :, 1, :], in1=cst[0:C, 0, 0:C].unsqueeze(1).broadcast_to([C, 2, C]), op=ALU.add), [invP[0], cst], [invT[0]])
                for i_ in range(4):
                    dump('pw%d' % i_, pw, pw[:, 0, i_, :], C)
                dump('X', invP[0], invP[0][:, 0, 0, :], C); dump('tokB', tokb[0], tokb[0][:, ch, :], C); dump('tokK', tokb[1], tokb[1][:, ch, :], C); dump('tokV', tokb[2], tokb[2][:, ch, :], C)
                stop(5)
                tt = inverse(nlev)
                dump('TT', invT[tt], invT[tt][:, 0, :], C)
                stop(6)
                P.op("pool", KW("tensor_copy", out=TTb[:], in_=invT[tt][:]), [invT[tt]], [TTb])
                TT = TTb
                if nseg > 1:
                    dve(KW("tensor_tensor", out=expa[0][:], in0=At[:, cc].unsqueeze(1).broadcast_to([128, nseg, C]), in1=cm[:], op=ALU.mult), [At, cm], [expa[0]])
                    dve(KW("tensor_tensor", out=expa[1][:], in0=Rt[:, cc].unsqueeze(1).broadcast_to([128, nseg, C]), in1=cm[:], op=ALU.mult), [Rt, cm], [expa[1]])
                    for i_ in range(2):
                        dve(KW("tensor_tensor", out=expt[i_][:], in0=tokb[i_][:, ch, :].unsqueeze(1).broadcast_to([C, nseg, 128]),
                                                               in1=smk[:].unsqueeze(2).broadcast_to([C, nseg, 128]), op=ALU.mult), [tokb[i_], smk], [expt[i_]])
                    At_s = lambda s_: expa[0][:, s_, :]
                    Rt_s = (lambda s_: expa[1][:, s_, :]) if not (dbg and dbg.get('exp') == 'X') else (lambda s_: expa[0][:, s_, :])
                    Bh_s = lambda s_: expt[0][:, s_, :]
                    Kh_s = lambda s_: expt[1][:, s_, :]
                    rdA, rdR, rdB, rdK = [expa[0]], [expa[1]], [expt[0]], [expt[1]]
                else:
                    At_s = lambda s_: At[:, cc]
                    Rt_s = lambda s_: Rt[:, cc]
                    Bh_s = lambda s_: tokb[0][:, ch, :]
                    Kh_s = lambda s_: tokb[1][:, ch, :]
                    rdA, rdR, rdB, rdK = [At], [Rt], [tokb[0]], [tokb[1]]
                stop(66)
                Vt = tokb[2]
                rhs_ap = B_RHS[0:C, 0:128]
                for s_ in range(nseg):
                    mm(chn["rhs"], rhs_ap, At_s(s_), Zb[:, s_, :], s_ == 0, False, rdA + [Zb])
                for hh in range(2):
                    mm(chn["rhs"], B_RHS[0:C, hh * 64:hh * 64 + 64], pw[:, hh, 1, :], Vt[:, ch, hh * 64:hh * 64 + 64], False, hh == 1, [pw, Vt])
                stop(67)
                act([chn["rhs"]], [chs], out=chs[:], in_=rhs_ap, func=AF.Copy)
                dump('chs', chs, chs[:], C)
                for hh in range(2):
                    mm(chn["u"], B_U[0:C, 128 + hh * 64:128 + hh * 64 + 64], TT[:, hh, :], chs[:, hh * 64:hh * 64 + 64], True, True, [TT, chs])
                for hh in range(2):
                    rs_ = slice(hh * 64, hh * 64 + 64)
                    act([chn["u"]], [upad], out=upad[:, hh, rs_], in_=B_U[0:C, 128 + hh * 64:128 + hh * 64 + 64], func=AF.Copy)
                    P.op("pool", KW("tensor_copy", out=vpad[:, hh, rs_], in_=Vt[:, ch, rs_]), [Vt], [vpad])
                stop(68)
                dump('upad', upad, upad[:, 0, :], C)
                if full and nseg == 1:
                    o_ap = B_O[:, 256:256 + C]
                    mm(chn["o"], o_ap, Zb[:, 0, :], Rt_s(0), True, False, rdR + [Zb])
                    for hh in range(2):
                        mm(chn["o"], o_ap, upad[:, hh, :], pw[:, hh, 2, :], False, False, [upad, pw])
                        mm(chn["o"], o_ap, vpad[:, hh, :], pw[:, hh, 3, :], False, hh == 1, [vpad, pw])
                    act([chn["o"]], [oT], out=oT[:, cc], in_=o_ap, func=AF.Copy)
                    dump('oc', oT, oT[:, cc])
                elif full:
                    ot_ap = B_O[0:C, 256:384]
                    for s_ in range(nseg if not (dbg and str(dbg.get('exp', '')).isdigit()) else int(dbg['exp'])):
                        mm(chn["o"], ot_ap, Rt_s(s_), Zb[:, s_, :], s_ == 0, False, rdR + [Zb])
                    stop(69)
                    for hh in range(2):
                        rs_ = slice(hh * 64, hh * 64 + 64)
                        mm(chn["o"], B_O[0:C, 256 + hh * 64:256 + hh * 64 + 64], pw[:, hh, 2, :], upad[:, hh, rs_], False, False, [pw, upad])
                        mm(chn["o"], B_O[0:C, 256 + hh * 64:256 + hh * 64 + 64], pw[:, hh, 3, :], Vt[:, ch, rs_], False, hh == 1, [pw, Vt])
                    stop(70)
                    act([chn["o"]], [sc2], out=sc2[0:C, 0:128], in_=ot_ap, func=AF.Copy)
                    stop(71)
                    a_ = nacc()
                    P.op("pe", KW("transpose", a_[:, 0:C], sc2[0:C, 0:128], cst[0:C, 0, 0:C]), [sc2, cst], [a_])
                    act([a_], [oT], out=oT[:, cc], in_=a_[:, 0:C], func=AF.Copy)
                stop(7)
                for s_ in range(nseg):
                    su = sun[s_ % 2]
                    su_ap = smp[2 + (s_ % 2)][:, 384:512]
                    for hh in range(2):
                        rs_ = slice(hh * 64, hh * 64 + 64)
                        mm(su, su_ap[:, rs_], Bh_s(s_), upad[:, hh, rs_], True, False, rdB + [upad])
                        mm(su, su_ap[:, rs_], Kh_s(s_), Vt[:, ch, rs_], False, True, rdK + [Vt])
                    gi = (ch * C) // Cg + (s_ if nseg > 1 else (C // Cg - 1))
                    gcol = gi * Cg + Cg - 1
                    for hh in range(2):
                        rs_ = slice(hh * 64, hh * 64 + 64)
                        dve(KW("scalar_tensor_tensor",
                            out=Zf[rs_, s_, :], in0=Zf[rs_, s_, :], scalar=E1[rs_, gcol:gcol + 1], in1=su_ap[rs_, rs_], op0=ALU.mult, op1=ALU.add), [Zf, E1, su], [Zf])
                        P.op("pool", KW("tensor_copy", out=Zb[rs_, s_, rs_], in_=Zf[rs_, s_, :]), [Zf], [Zb])
            if kind == "s":
                store(o_swkv[hp], Zf_w, Zf_w[:])
            elif si == g.NSLAB - 1:
                store(o_pwkv[hp], Zf, Zf[:])
            if full:
                dump('oT', oT, oT[:, 0:T])
                a_ = nacc()
                mm(a_, a_[:, 0:T], bones, oT[:, 0:T], True, True, [cst, oT])
                dve(KW("scalar_tensor_tensor", out=sc2[:, 0:T], in0=a_[:, 0:T], scalar=-1.0 / 64, in1=oT[:, 0:T], op0=ALU.mult, op1=ALU.add), [a_, oT], [sc2])
                act([sc2], [sc_], out=sc_[:, 0:T], in_=sc2[:, 0:T], func=AF.Square)
                a_ = nacc()
                mm(a_, a_[:, 0:T], bones, sc_[:, 0:T], True, True, [cst, sc_])
                rsqrt_to(sc_, sc_[:, 0:T], a_, a_[:, 0:T], 1.0 / 64, GN_EPS, E3, E3[:, 0:T])
                dve(KW("tensor_tensor", out=sc2[:, 0:T], in0=sc2[:, 0:T], in1=sc_[:, 0:T], op=ALU.mult), [sc2, sc_], [sc2])
                dve(KW("tensor_scalar", out=sc2[:, 0:T], in0=sc2[:, 0:T], scalar1=vec[:, g.v_lw + hp:g.v_lw + hp + 1], scalar2=vec[:, g.v_lb + hp:g.v_lb + hp + 1],
                                                op0=ALU.mult, op1=ALU.add), [sc2, vec], [sc2])
                dve(KW("tensor_tensor", out=sc2[:, 0:T], in0=sc2[:, 0:T], in1=bon[:, 0:T], op=ALU.add), [sc2, bon], [sc2])
                dve(KW("tensor_tensor", out=yaT[:, hp, 0:T], in0=sc2[:, 0:T], in1=gt_[:, 0:T], op=ALU.mult), [sc2, gt_], [yaT.r[hp]])
        if kind == "s":
            store(o_sshift, sso, sso[:])
        elif si == g.NSLAB - 1:
            store(o_pshift, pshift, pshift[:])
        if dbg and dbg["what"] == "yaT" and dbg["slab"] == (kind, si):
            t0 = tmpf[5]
            for c in range(NHP):
                P.op("pool", KW("tensor_copy", out=t0[:, 0:T], in_=yaT[:, c, 0:T]), [yaT.r[c]], [t0])
                store(o_dbg[:, c, 0:T], t0, t0[:, 0:T])
            return
        if dbg and (dbg['what'] in ('hT', 'yaT', 'xr', 'xk', 'e2', 'a', 'kk', 'kmod', 'cs', 'E4', 'g', 'bon', 'oT') or dbg['what'][:2] in ('pw', 'X', 'to', 'TT', 'ch', 'up', 'oc')):
            return

        wtb = wget(d_win[g.t_gb].rearrange("p k n -> p (k n)"), 128, KC * 128, key=("win", g.t_gb))
        for ch in range(nch):
            cc = slice(ch * C, (ch + 1) * C)
            a_ = nacc()
            for kc in range(KC):
                mm(a_, a_[0:C, 0:GH], hT[:, kc, cc], wtb[:, kc * 128:kc * 128 + GH], kc == 0, kc == KC - 1, [wtb, hT.r[kc]])
            act([a_], [tk], out=tk[:, ch, 0, :], in_=a_[0:C, 0:GH], func=AF.Sigmoid)
        wta = wget(d_win[g.t_ga].rearrange("p k n -> p (k n)"), 128, KC * 128, key=("win", g.t_ga))
        for ch in range(nch):
            cc = slice(ch * C, (ch + 1) * C)
            a_ = nacc()
            for kc in range(KC):
                mm(a_, a_[0:C, GH:2 * GH], hT[:, kc, cc], wta[:, kc * 128:kc * 128 + GH], kc == 0, kc == KC - 1, [wta, hT.r[kc]])
            sc_ = tmpf[4]
            dve(KW("tensor_tensor", out=sc_[0:C, 0:GH], in0=a_[0:C, GH:2 * GH], in1=gsc[0:C, GH:2 * GH], op=ALU.add), [a_, gsc], [sc_])
            act([sc_], [sc_], out=sc_[0:C, 0:GH], in_=sc_[0:C, 0:GH], func=AF.Exp)
            act([sc_, epst], [sc_], out=sc_[0:C, 0:GH], in_=sc_[0:C, 0:GH], func=AF.Ln, bias=epsc[1.0][0:C, :], scale=1.0)
            dve(KW("tensor_tensor", out=tk[:, ch, 1, :], in0=sc_[0:C, 0:GH], in1=gsc[0:C, 0:GH], op=ALU.mult), [sc_, gsc], [tk])
            a_ = nacc()
            mm(a_, a_[0:C, 0:GH], mskIf, tk[:, ch, 1, :], True, True, [cst, tk])
            mm(a_, a_[0:C, GH:2 * GH], segones, tk[:, ch, 1, :], True, True, [cst, tk])
            act([a_], [tk], out=tk[:, ch, 2, :], in_=a_[0:C, 0:GH], func=AF.Copy)
            act([a_], [tk], out=tk[:, ch, 3, :], in_=a_[0:C, 0:GH], func=AF.Exp)
            dve(KW("tensor_scalar", out=tk[:, ch, 4, :], in0=tk[:, ch, 3, :], scalar1=-1.0, scalar2=None, op0=ALU.mult), [tk], [tk])
            dve(KW("tensor_tensor", out=tk[:, ch, 5, :], in0=a_[0:C, GH:2 * GH], in1=tk[:, ch, 2, :], op=ALU.subtract), [a_, tk], [tk])
            act([tk], [tk], out=tk[:, ch, 5, :], in_=tk[:, ch, 5, :], func=AF.Exp)

        def conv_mix(chunk, a_, out_t):
            pe_ = tmpf[3]
            W3 = nseg * (L + 3)
            pv = pe_[:, 0:W3].rearrange("p (s l) -> p s l", s=nseg)
            act([a_], [pe_], out=pv[:, :, 3:L + 3], in_=tokv(a_.t), func=AF.Copy)
            if kind == "p":
                P.op("pool", KW("tensor_copy", out=pv[:, :, 0:3], in_=pconv[:, chunk, :, :]), [pconv], [pe_])
                P.op("pool", KW("tensor_copy", out=pconv[:, chunk, :, :], in_=pv[:, :, L:L + 3]), [pe_], [pconv])
            else:
                ci = chunk % 2
                P.dma("pool", cvin_s[ci], cvin[ci][:], d_sconv[:, chunk, :, :], writes=[cvin[ci]])
                P.op("pool", KW("tensor_copy", out=pv[:, :, 0:3], in_=cvin[ci][:]), [cvin[ci]], [pe_])
                P.dma("pool", cvout_s, o_sconv[:, chunk, :, :], pv[:, :, L:L + 3], reads=[pe_])
            ov = tokv(out_t.t)
            cwc = lambda i_: vec[:, g.v_cw + i_ * 3 * GH + chunk:g.v_cw + i_ * 3 * GH + chunk + 1]
            dve(KW("tensor_scalar", out=ov, in0=pv[:, :, 0:L], scalar1=cwc(0), scalar2=None, op0=ALU.mult), [pe_, vec], [out_t])
            for i_ in range(1, 4):
                dve(KW("scalar_tensor_tensor", out=ov, in0=pv[:, :, i_:i_ + L], scalar=cwc(i_), in1=ov, op0=ALU.mult, op1=ALU.add), [pe_, vec, out_t], [out_t])
            act([out_t], [out_t], out=out_t[:, 0:T], in_=out_t[:, 0:T], func=AF.Silu)

        kso, kw_, koo = smp[0], smp[1], smp[0]
        for h in range(GH):
            qn, kn, vn, zs = tmpf[6], tmpf[7], tmpf[8], tmpf[9]
            sq_, rn_, sc2 = tmpf[10], tmpf[11], tmpf[12]
            for tid, cchunk, xt_ in ((g.t_gq + h, h, qn), (g.t_gk + h, GH + h, kn), (g.t_gv + h, 2 * GH + h, vn)):
                if tid == g.t_gq + h and not full:
                    pass
                a_ = proj(tid); conv_mix(cchunk, a_, xt_)
            if full:
                a_ = proj(g.t_gz + h)
                act([a_], [zs], out=zs[:, 0:T], in_=a_[:, 0:T], func=AF.Silu)
            qb, kb, Qg = tmpb[3], tmpb[4], tmpb[5]
            for x_, scl, xb_ in ((qn, 128.0 ** -0.5, qb), (kn, 1.0, kb)):
                act([x_], [sq_], out=sq_[:, 0:T], in_=x_[:, 0:T], func=AF.Square)
                a_ = nacc()
                mm(a_, a_[:, 0:T], ones, sq_[:, 0:T], True, True, [cst, sq_])
                rsqrt_to(rn_, rn_[:, 0:T], a_, a_[:, 0:T], 1.0, 1e-6, sc2, sc2[:, 0:T])
                dve(KW("scalar_tensor_tensor", out=x_[:, 0:T], in0=x_[:, 0:T], scalar=scl, in1=rn_[:, 0:T], op0=ALU.mult, op1=ALU.mult), [x_, rn_], [x_])
                dve(KW("tensor_copy", out=xb_[:, 0:T], in_=x_[:, 0:T]), [x_], [xb_])
            dump('gq', qn, qn[:, 0:T]); dump('gk', kn, kn[:, 0:T]); dump('gv', vn, vn[:, 0:T])
            if kind == "p":
                Sf = pSf[h]
            else:
                Sf = Sf_w
                load(Sf_w, Sf_w[:], d_sgdn[h])
            P.op("pool", KW("tensor_copy", out=Sb[:, 0:nseg, :], in_=Sf[:, 0:nseg, :]), [Sf], [Sb])
            egB, eDT, dsT, diT, XTf = tmpf[18], tmpf[19], tmpf[20], tmpf[21], tmpf[22]
            for cp in range((nch + 1) // 2):
                for hh in range(2):
                    ch = min(cp * 2 + hh, nch - 1)
                    cc = slice(ch * C, (ch + 1) * C)
                    col = lambda j_: tk[:, ch, j_, h:h + 1]
                    if hh == 0 or cp * 2 + hh < nch:
                        for src, dst, scol in ((kn, 1, col(5)), (vn, 2, None)):
                            a_ = nacc()
                            P.op("pe", KW("transpose", a_[0:C, 0:128], src[:, cc], ident), [src, cst], [a_])
                            if scol is None:
                                act([a_], [tokb[dst]], out=tokb[dst][:, ch, :], in_=a_[0:C, 0:128], func=AF.Copy)
                            else:
                                act([a_, tk], [tokb[dst]], out=tokb[dst][:, ch, :], in_=a_[0:C, 0:128], func=AF.Copy, scale=scol)
                    mm(smp[0], smp[0][0:C, 0:C], kb[:, cc], kb[:, cc], True, True, [kb])
                    mm(smp[0], smp[0][0:C, C:2 * C], kb[:, cc], qb[:, cc], True, True, [kb, qb])
                    dve(KW("tensor_scalar", out=sc2[0:C, 0:C], in0=mskIf, scalar1=col(1), scalar2=None, op0=ALU.mult), [cst, tk], [sc2])
                    mm(smp[1], smp[1][:, 0:C], cst[0:C, 2, :], sc2[0:C, 0:C], True, True, [cst, sc2])
                    dve(KW("tensor_scalar", out=eDT[0:C, 0:C], in0=smp[1][0:C, 0:C], scalar1=col(2), scalar2=0.0, op0=ALU.subtract, op1=ALU.min), [smp[1], tk], [eDT])
                    act([eDT], [eDT], out=eDT[0:C, 0:C], in_=eDT[0:C, 0:C], func=AF.Exp)
                    act([smp[1]], [egB], out=egB[:, 0:C], in_=smp[1][:, 0:C], func=AF.Exp)
                    dve(KW("scalar_tensor_tensor", out=dsT[0:C, 0:C], in0=eDT[0:C, 0:C], scalar=col(0), in1=mskSf, op0=ALU.mult, op1=ALU.mult), [eDT, tk, cst], [dsT])
                    dve(KW("scalar_tensor_tensor", out=XTf[0:C, 0:C], in0=smp[0][0:C, 0:C], scalar=-1.0, in1=dsT[0:C, 0:C], op0=ALU.mult, op1=ALU.mult), [smp[0], dsT], [XTf])
                    P.op("pool", KW("tensor_copy", out=invP[0][:, hh, 1, :], in_=XTf[0:C, 0:C]), [XTf], [invP[0]])
                    dve(KW("tensor_tensor", out=invT[0][:, hh, :], in0=XTf[0:C, 0:C], in1=cst[0:C, 0, 0:C], op=ALU.add), [XTf, cst], [invT[0]])
                    P.op("pe", KW("transpose", smp[1][0:C, 128:128 + C], XTf[0:C, 0:C], cst[0:C, 0, 0:C]), [XTf, cst], [smp[1]])
                    act([smp[1]], [invP[0]], out=invP[0][:, hh, 0, :], in_=smp[1][0:C, 128:128 + C], func=AF.Copy)
                    if hh == 0 or cp * 2 + hh < nch:
                        if full:
                            dve(KW("tensor_tensor", out=diT[0:C, 0:C], in0=eDT[0:C, 0:C], in1=mskIf, op=ALU.mult), [eDT, cst], [diT])
                            dve(KW("tensor_tensor", out=ATs[:, ch, :], in0=smp[0][0:C, C:2 * C], in1=diT[0:C, 0:C], op=ALU.mult), [smp[0], diT], [ATs])
                            dve(KW("tensor_tensor", out=Qg[:, cc], in0=qn[:, cc], in1=egB[:, 0:C], op=ALU.mult), [qn, egB], [Qg])
                        if nseg == 1:
                            P.op("pool", KW("tensor_copy", out=cds[:, ch, 0:1], in_=egB[:, C - 1:C]), [egB], [cds])
                        else:
                            P.op("pool", KW("tensor_copy", out=cds[:, ch, :], in_=egB[:, 0:C].rearrange("p (s l) -> p s l", l=LS)[:, :, LS - 1]), [egB], [cds])
                tt = inverse(nlev)
                for hh in range(2):
                    if cp * 2 + hh < nch:
                        P.op("pool", KW("tensor_copy", out=T2T[:, cp * 2 + hh, :], in_=invT[tt][:, hh, :]), [invT[tt]], [T2T])
            dump('T2T', T2T, T2T[:, 0, :], C); dump('ATs', ATs, ATs[:, 0, :], C)
            oT = tmpf[24]
            for ch in range(nch):
                cc = slice(ch * C, (ch + 1) * C)
                col = lambda j_: tk[:, ch, j_, h:h + 1]
                if nseg > 1:
                    dve(KW("tensor_tensor", out=expa[0][:], in0=kb[:, cc].unsqueeze(1).broadcast_to([128, nseg, C]), in1=cm[:], op=ALU.mult), [kb, cm], [expa[0]])
                    if full:
                        dve(KW("tensor_tensor", out=expa[1][:], in0=Qg[:, cc].unsqueeze(1).broadcast_to([128, nseg, C]), in1=cm[:], op=ALU.mult), [Qg, cm], [expa[1]])
                    dve(KW("tensor_tensor", out=expt[0][:], in0=tokb[1][:, ch, :].unsqueeze(1).broadcast_to([C, nseg, 128]),
                           in1=smk[:].unsqueeze(2).broadcast_to([C, nseg, 128]), op=ALU.mult), [tokb[1], smk], [expt[0]])
                    K_s = lambda s_: expa[0][:, s_, :]
                    Q_s = lambda s_: expa[1][:, s_, :]
                    Kt_s = lambda s_: expt[0][:, s_, :]
                    rdK, rdQ, rdT = [expa[0]], [expa[1]], [expt[0]]
                else:
                    K_s = lambda s_: kb[:, cc]
                    Q_s = lambda s_: Qg[:, cc]
                    Kt_s = lambda s_: tokb[1][:, ch, :]
                    rdK, rdQ, rdT = [kb], [Qg], [tokb[1]]
                ks_ap = smp[0][0:C, 0:128]
                for s_ in range(nseg):
                    mm(kso, ks_ap, K_s(s_), Sb[:, s_, :], s_ == 0, s_ == nseg - 1, rdK + [Sb])
                dve(KW("scalar_tensor_tensor", out=chs[:], in0=ks_ap, scalar=col(4), in1=tokb[2][:, ch, :], op0=ALU.mult, op1=ALU.add), [kso, tk, tokb[2]], [chs])
                w_ap = smp[1][0:C, 128:256]
                mm(kw_, w_ap, T2T[:, ch, :], chs[:], True, True, [T2T, chs])
                act([kw_, tk], [vnew], out=vnew[:], in_=w_ap, func=AF.Copy, scale=col(0))
                if full and nseg == 1:
                    o_ap = smp[0][:, 256:256 + C]
                    mm(koo, o_ap, Sb[:, 0, :], Q_s(0), True, False, rdQ + [Sb])
                    mm(koo, o_ap, vnew[:], ATs[:, ch, :], False, True, [vnew, ATs])
                    act([koo], [oT], out=oT[:, cc], in_=o_ap, func=AF.Copy)
                elif full:
                    ot_ap = smp[0][0:C, 256:384]
                    for s_ in range(nseg):
                        mm(koo, ot_ap, Q_s(s_), Sb[:, s_, :], s_ == 0, False, rdQ + [Sb])
                    mm(koo, ot_ap, ATs[:, ch, :], vnew[:], False, True, [ATs, vnew])
                    act([koo], [sc2], out=sc2[0:C, 0:128], in_=ot_ap, func=AF.Copy)
                    a_ = nacc()
                    P.op("pe", KW("transpose", a_[:, 0:C], sc2[0:C, 0:128], cst[0:C, 0, 0:C]), [sc2, cst], [a_])
                    act([a_], [oT], out=oT[:, cc], in_=a_[:, 0:C], func=AF.Copy)
                for s_ in range(nseg):
                    su = sun[s_ % 2]
                    su_ap = smp[2 + (s_ % 2)][:, 384:512]
                    mm(su, su_ap, Kt_s(s_), vnew[:], True, True, rdT + [vnew])
                    dve(KW("scalar_tensor_tensor", out=Sf[:, s_, :], in0=Sf[:, s_, :], scalar=cds[:, ch, s_:s_ + 1], in1=su_ap, op0=ALU.mult, op1=ALU.add), [Sf, cds, su], [Sf])
                    P.op("pool", KW("tensor_copy", out=Sb[:, s_, :], in_=Sf[:, s_, :]), [Sf], [Sb])
            if kind == "s":
                store(o_sgdn[h], Sf_w, Sf_w[:])
            elif si == g.NSLAB - 1:
                store(o_pgdn[h], Sf, Sf[:])
            if full:
                dump('goT', oT, oT[:, 0:T])
                act([oT], [sq_], out=sq_[:, 0:T], in_=oT[:, 0:T], func=AF.Square)
                a_ = nacc()
                mm(a_, a_[:, 0:T], ones, sq_[:, 0:T], True, True, [cst, sq_])
                rsqrt_to(rn_, rn_[:, 0:T], a_, a_[:, 0:T], 1.0 / 128, NORM_EPS, sc2, sc2[:, 0:T])
                dve(KW("tensor_tensor", out=sc2[:, 0:T], in0=oT[:, 0:T], in1=rn_[:, 0:T], op=ALU.mult), [oT, rn_], [sc2])
                dve(KW("scalar_tensor_tensor", out=ybT[:, h, 0:T], in0=sc2[:, 0:T], scalar=vec[:, g.v_gnw:g.v_gnw + 1], in1=zs[:, 0:T], op0=ALU.mult, op1=ALU.mult), [sc2, vec, zs], [ybT.r[h]])
        if kind == "p" and si == g.NSLAB - 1:
            store(o_pconv, pconv, pconv[:])
        if dbg and dbg["what"] == "ybT" and dbg["slab"] == (kind, si):
            t0 = tmpf[5]
            for c in range(GH):
                P.op("pool", KW("tensor_copy", out=t0[:, 0:T], in_=ybT[:, c, 0:T]), [ybT.r[c]], [t0])
                store(o_dbg[:, c, 0:T], t0, t0[:, 0:T])
            return
        if dbg and dbg['what'] in ('ybT', 'gq', 'gk', 'gv', 'T2T', 'ATs', 'goT'):
            return

        if not full:
            if si == g.NSLAB // 2 - 1:
                for t_ in pZf + pSf + [pshift, pconv]:
                    fl = t_[:].rearrange("p a b -> p (a b)") if len(t_.t.shape) == 3 else t_[:].rearrange("p a b c -> p (a b c)")
                    dve(KW("tensor_scalar", out=fl, in0=fl, scalar1=flag[:, 0:1], scalar2=None, op0=ALU.mult), [t_, flag], [t_])
            return
        for c in range(KC):
            sg = []
            for tid, wsrc, nk, yT, wkey in ((g.t_ma + c, d_woa[c], NHP, yaT, ("woa", c)), (g.t_mb + c, d_wob[c], GH, ybT, ("wob", c))):
                a_ = proj(tid)
                sgt = tmpf[3 + len(sg)]
                act([a_], [sgt], out=sgt[:, 0:T], in_=a_[:, 0:T], func=AF.Sigmoid)
                wt = wget(wsrc.rearrange("p k n -> p (k n)"), 128, nk * 128, key=wkey)
                a2 = nacc()
                for kc in range(nk):
                    mm(a2, a2[:, 0:T], wt[:, kc * 128:(kc + 1) * 128], yT[:, kc, 0:T], kc == 0, kc == nk - 1, [wt, yT.r[kc]])
                dve(KW("tensor_tensor", out=sgt[:, 0:T], in0=sgt[:, 0:T], in1=a2[:, 0:T], op=ALU.mult), [sgt, a2], [sgt])
                sg.append(sgt)
            dve(KW("tensor_tensor", out=mgT[:, c, 0:T], in0=sg[0][:, 0:T], in1=sg[1][:, 0:T], op=ALU.add), [sg[0], sg[1]], [mgT.r[c]])
        dump_chunks('mgT', mgT, KC, T, kind, si)
        for c in range(KC):
            wt = wget(d_wo[c].rearrange("p k n -> p (k n)"), 128, KC * 128, key=("wo", c))
            a_ = nacc()
            for kc in range(KC):
                mm(a_, a_[:, 0:T], wt[:, kc * 128:(kc + 1) * 128], mgT[:, kc, 0:T], kc == 0, kc == KC - 1, [wt, mgT.r[kc]])
            xt_, xap = xload(c)
            seg_affine("dve", x1T[:, c, :], a_, a_, 2, c, xt_, xap, [a_, mT, xt_], [x1T.r[c]])
        dump_chunks('x1T', x1T, KC, T, kind, si)
        stats_rstd(lambda c: (x1T.r[c], x1T[:, c, 0:T]), KC, 1.0 / D, NORM_EPS)
        for c in range(KC):
            t0 = tmpf[3 + c % 2]
            dve(KW("tensor_tensor", out=t0[:, 0:T], in0=x1T[:, c, 0:T], in1=rstd[:, 0:T], op=ALU.mult), [x1T.r[c], rstd], [t0])
            seg_affine("act", hT[:, c, :], t0, t0, 4, c, None, (3, c), [t0, mT], [hT.r[c]])
        dump_chunks('h2T', hT, KC, T, kind, si)
        for g8 in range(g.FC // 8):
            for j in range(8):
                wt = wget(d_wup[g8 * 8 + j].rearrange("p k n -> p (k n)"), 128, KC * 128, key=("wup", g8 * 8 + j))
                a_ = nacc()
                for kc in range(KC):
                    mm(a_, a_[:, 0:T], wt[:, kc * 128:(kc + 1) * 128], hT[:, kc, 0:T], kc == 0, kc == KC - 1, [wt, hT.r[kc]])
                t0 = tmpf[3 + j % 2]
                act([a_], [t0], out=t0[:, 0:T], in_=a_[:, 0:T], func=AF.Relu)
                P.op("pool" if j % 2 else "dve", KW("tensor_tensor", out=uh[:, j, 0:T], in0=t0[:, 0:T], in1=t0[:, 0:T], op=ALU.mult), [t0], [uh.r[j]])
            for c in range(KC):
                wt = wget(d_wdn[g8, c].rearrange("p k n -> p (k n)"), 128, 8 * 128, key=("wdn", g8 * KC + c))
                a_ = nacc()
                for j in range(8):
                    mm(a_, a_[:, 0:T], wt[:, j * 128:(j + 1) * 128], uh[:, j, 0:T], j == 0, j == 7, [wt, uh.r[j]])
                seg_affine("dve", x1T[:, c, :], a_, a_, 5, c, x1T.r[c], x1T[:, c, :], [a_, mT, x1T.r[c]], [x1T.r[c]])
        dump_chunks('x2T', x1T, KC, T, kind, si)
        stats_rstd(lambda c: (x1T.r[c], x1T[:, c, 0:T]), KC, 1.0 / D, NORM_EPS)
        for c in range(KC):
            i = c % 2
            dve(KW("tensor_tensor", out=ych[i][:, 0:T], in0=x1T[:, c, 0:T], in1=rstd[:, 0:T], op=ALU.mult), [x1T.r[c], rstd], [ych[i]])
            dve(KW("tensor_scalar", out=ych[i][:, 0:T], in0=ych[i][:, 0:T], scalar1=vec[:, g.v_fw + c:g.v_fw + c + 1], scalar2=None, op0=ALU.mult), [ych[i], vec], [ych[i]])
            dst = o_yp[si - g.NSLAB // 2][:, c, :] if kind == "p" else o_ys[:, c, :]
            P.dma("pool", ych_s[i], dst, ych[i][:, 0:T], reads=[ych[i]])

    try:
        _run_slabs()
    except StopBuild:
        pass
    P.wreq = wreq
    return nc, P


def tile_cols(Wm, col_lists):
    K = Wm.shape[0]
    out = np.zeros((len(col_lists), 128, K // 128, 128), np.float32)
    for t, (c0, n) in enumerate(col_lists):
        blk = Wm[:, c0:c0 + n].reshape(K // 128, 128, n)
        out[t, :, :, :n] = blk.transpose(1, 0, 2)
    return out


def fm(v, n):
    return np.ascontiguousarray(np.asarray(v, np.float32).reshape(n, 128).T)


def host_shared(cfg, inp):
    g = cfg
    D, KC, NHP, GH, RW, GW = g.D, g.KC, g.NHP, g.GH, g.RW, g.GW
    sh = {}
    w_in = np.asarray(inp["w_in"][0]); w_ada = np.asarray(inp["w_ada"][0])
    sh["wada"] = tile_cols(w_ada, [(j * D + c * 128, 128) for j in range(6) for c in range(KC)])
    cols = [(i * 128, 128) for i in range(g.RC)]
    G0 = g.RCOLS
    cols += [(G0 + i * 128, 128) for i in range(4 * GH)]
    cols += [(G0 + 4 * GW, GH), (G0 + 4 * GW + GH, GH)]
    M0 = g.RCOLS + g.GCOLS
    cols += [(M0 + i * 128, 128) for i in range(2 * KC)]
    assert len(cols) == g.NT_IN
    sh["win"] = tile_cols(w_in, cols)
    sh["ww2"] = np.concatenate([inp["r_w_w2"][0], inp["r_w_a2"][0]], 0).astype(np.float32)
    sh["wg2"] = np.ascontiguousarray(np.asarray(inp["r_w_g2"][0]).reshape(2, 128, RW).transpose(1, 0, 2))
    woa = np.asarray(inp["w_out_a"][0])
    sh["woa"] = tile_cols(woa, [(c * 128, 128) for c in range(KC)])
    wob = np.asarray(inp["w_out_b"][0])
    sh["wob"] = tile_cols(wob, [(c * 128, 128) for c in range(KC)])
    sh["wo"] = tile_cols(np.asarray(inp["w_out"][0]), [(c * 128, 128) for c in range(KC)])
    sh["wup"] = tile_cols(np.asarray(inp["w_up"][0]), [(c * 128, 128) for c in range(g.FC)])
    wdn = np.asarray(inp["w_down"][0])
    sh["wdn"] = np.ascontiguousarray(wdn.reshape(g.FC // 8, 8, 128, KC, 128).transpose(0, 3, 2, 1, 4))
    v = np.zeros((128, g.NV), np.float32)
    v[:, g.v_bada:g.v_bada + 6 * KC] = fm(inp["b_ada"][0], 6 * KC)
    v[:, g.v_n1:g.v_n1 + KC] = fm(inp["norm1_w"][0], KC)
    v[:, g.v_n2:g.v_n2 + KC] = fm(inp["norm2_w"][0], KC)
    v[:, g.v_fw:g.v_fw + KC] = fm(inp["final_norm_w"], KC)
    v[:, g.v_mu:g.v_mu + g.RC] = fm(inp["r_mu"][0], g.RC)
    for nm, col in (("r_w0", g.v_w0), ("r_a0", g.v_a0), ("r_k_k", g.v_kk), ("r_k_a", g.v_ka), ("r_lnx_w", g.v_lw), ("r_lnx_b", g.v_lb)):
        v[:, col:col + NHP] = fm(inp[nm][0], NHP)
    v[:, g.v_rk:g.v_rk + NHP] = fm(np.asarray(inp["r_r_k"][0]).reshape(-1), NHP)
    cw = np.asarray(inp["g_conv_w"][0])
    for i in range(4):
        v[:, g.v_cw + i * 3 * GH:g.v_cw + (i + 1) * 3 * GH] = fm(cw[i], 3 * GH)
    v[:, g.v_gnw] = np.asarray(inp["g_norm_w"][0])
    sh["vecsT"] = v
    sh["gsc"] = np.concatenate([np.tile(np.asarray(inp["g_a_log"][0])[None], (128, 1)), np.tile(np.asarray(inp["g_dt_bias"][0])[None], (128, 1))], 1).astype(np.float32)
    C = g.C
    cst = np.zeros((128, 10, 128), np.float32)
    cst[:, 0] = np.eye(128)
    bo = np.zeros((128, 128)); bo[:64, :64] = 1; bo[64:, 64:] = 1
    cst[:, 1] = bo
    cst[:, 2] = 1.0
    ii = np.arange(C)
    cst[:C, 3, :C] = (ii[None, :] > ii[:, None])
    cst[:C, 4, :C] = (ii[None, :] >= ii[:, None])
    sg = ii // g.LS
    same = sg[None, :] == sg[:, None]
    cst[:C, 5, :C] = (ii[None, :] > ii[:, None]) & same
    cst[:C, 6, :C] = (ii[None, :] >= ii[:, None]) & same
    cst[:C, 7, :C] = same
    cst[:C, 8, :C] = (ii[None, :] < ii[:, None])
    cst[:C, 9, :C] = (ii[None, :] < ii[:, None]) & same
    sh["consts"] = cst
    cmk = np.zeros((128, g.NS, C), np.float32)
    smk = np.zeros((C, g.NS), np.float32)
    for s_ in range(g.NS):
        cmk[:, s_, s_ * g.LS:(s_ + 1) * g.LS] = 1; smk[s_ * g.LS:(s_ + 1) * g.LS, s_] = 1
    import ml_dtypes
    sh["cm"] = cmk.astype(ml_dtypes.bfloat16); sh["sm"] = smk
    TM = max(g.TP, g.TS)
    r = np.ones((128, 2, TM), np.float32)
    r[:, 0, 0::C] = 0; r[:, 1, 0::g.LS] = 0
    sh["rst"] = r
    return sh


def host_core(cfg, inp, core):
    g = cfg
    KC, NS, LS = g.KC, g.NS, g.LS
    b = core // 2; half = core % 2
    m = {}
    xp = np.asarray(inp["x_prompt"][b])
    H = g.SEQ // 2
    if half == 1:
        toks = xp
    else:
        toks = np.concatenate([xp[:H], xp[:H]], 0)
    m["xp"] = np.ascontiguousarray(toks.reshape(g.NSLAB, g.TP, KC, 128).transpose(0, 3, 2, 1))
    sl = slice(core * NS, (core + 1) * NS)
    xs = np.asarray(inp["x_sample"][sl]).reshape(NS * LS, KC, 128)
    m["xs"] = np.ascontiguousarray(xs.transpose(2, 1, 0))
    cc = np.concatenate([np.asarray(inp["c_prompt"][b])[None], np.asarray(inp["c_sample"][sl])], 0)
    m["cT"] = np.ascontiguousarray(cc.reshape(1 + NS, KC, 128).transpose(2, 1, 0))
    m["flag"] = np.full((128, 1), float(half), np.float32)
    wkv = np.asarray(inp["state_rwkv_wkv"][0, sl])
    m["s_wkv"] = np.ascontiguousarray(wkv.reshape(NS, g.NHP, 2, 64, 64).transpose(1, 2, 4, 0, 3).reshape(g.NHP, 128, NS, 64))
    shf = np.asarray(inp["state_rwkv_shift"][0, sl])
    m["s_shift"] = np.ascontiguousarray(shf.reshape(NS, g.RC, 128).transpose(2, 1, 0))
    gdn = np.asarray(inp["state_gdn"][0, sl])
    m["s_gdn"] = np.ascontiguousarray(gdn.transpose(1, 2, 0, 3))
    cv = np.asarray(inp["state_gdn_conv"][0, sl])
    m["s_conv"] = np.ascontiguousarray(cv.reshape(NS, 3, 3 * g.GH, 128).transpose(3, 2, 0, 1))
    return m


_CACHE = {}


def _get_prog(cfg_key):
    if cfg_key not in _CACHE:
        cfg = Cfg(*cfg_key)
        nc0, P0 = build(cfg)
        cfg.wplan = P0.wreq
        nc, P = build(cfg)
        assert len(P.wreq) == len(cfg.wplan)
        if cfg.D < 4096:
            P.epoch = 300; P.dseg = 20
        from contextlib import ExitStack
        es = ExitStack()
        P.emit(es)
        _CACHE[cfg_key] = (cfg, nc, P, es)
    return _CACHE[cfg_key]


def run(cfg_key, inputs):
    cfg, nc, P, es = _get_prog(cfg_key)
    g = cfg
    inp = {k: np.asarray(v) for k, v in inputs.items()}
    sh = host_shared(cfg, inp)
    maps = []
    for c in range(8):
        m = dict(sh); m.update(host_core(cfg, inp, c)); maps.append(m)
    res = run_bass_kernel_spmd(nc, maps, core_ids=list(range(8)))
    R = res.results
    B = 4; D = g.D; NS = g.NS; LS = g.LS; KC = g.KC
    f32 = np.float32
    tokm = lambda a: np.ascontiguousarray(np.asarray(a).transpose(2, 1, 0)).reshape(a.shape[2], KC * 128)
    y_p = np.zeros((B, g.SEQ, D), f32); y_s = np.zeros((8 * NS, LS, D), f32)
    p_wkv = np.zeros((1, B, g.RH, 64, 64), f32); p_shift = np.zeros((1, B, g.RCOLS), f32)
    p_gdn = np.zeros((1, B, g.GH, 128, 128), f32); p_conv = np.zeros((1, B, 3, 3 * g.GW), f32)
    s_wkv = np.zeros((1, 8 * NS, g.RH, 64, 64), f32); s_shift = np.zeros((1, 8 * NS, g.RCOLS), f32)
    s_gdn = np.zeros((1, 8 * NS, g.GH, 128, 128), f32); s_conv = np.zeros((1, 8 * NS, 3, 3 * g.GW), f32)

    def wkv_host(a, n):
        a = np.asarray(a).reshape(g.NHP, 2, 64, n, 64)
        return np.ascontiguousarray(a.transpose(3, 0, 1, 4, 2)).reshape(n, g.RH, 64, 64)

    def shift_host(a, n):
        return np.ascontiguousarray(np.asarray(a).transpose(2, 1, 0)).reshape(n, g.RCOLS)

    def gdn_host(a, n):
        return np.ascontiguousarray(np.asarray(a).transpose(2, 0, 1, 3))

    def conv_host(a, n):
        return np.ascontiguousarray(np.asarray(a).transpose(2, 3, 1, 0)).reshape(n, 3, 3 * g.GW)

    H = g.SEQ // 2
    for c in range(8):
        b = c // 2; half = c % 2
        r = R[c]
        for j in range(g.NSLAB // 2):
            y_p[b, half * H + j * g.TP: half * H + (j + 1) * g.TP] = tokm(r["o_yp"][j])
        y_s[c * NS:(c + 1) * NS] = tokm(r["o_ys"]).reshape(NS, LS, D)
        if half == 1:
            p_wkv[0, b] = wkv_host(r["o_pwkv"], 1)[0]; p_shift[0, b] = shift_host(r["o_pshift"], 1)[0]
            p_gdn[0, b] = gdn_host(r["o_pgdn"], 1)[0]; p_conv[0, b] = conv_host(r["o_pconv"], 1)[0]
        sl = slice(c * NS, (c + 1) * NS)
        s_wkv[0, sl] = wkv_host(r["o_swkv"], NS); s_shift[0, sl] = shift_host(r["o_sshift"], NS)
        s_gdn[0, sl] = gdn_host(r["o_sgdn"], NS); s_conv[0, sl] = conv_host(r["o_sconv"], NS)
    return (y_p, y_s, p_wkv, p_shift, p_gdn, p_conv, s_wkv, s_shift, s_gdn, s_conv)


def kernel(**inputs):
    return run((4096, 2048, 128, 16, 4), inputs)
```

```python
import numpy as np
import concourse.bass as bass
import concourse.mybir as mybir
from concourse.bass_utils import run_bass_kernel_spmd

F32 = mybir.dt.float32
BF16 = mybir.dt.bfloat16
AF = mybir.ActivationFunctionType
ALU = mybir.AluOpType
NORM_EPS = 1e-6
GN_EPS = 64e-5


class Cfg:
    def __init__(s, D=4096, SEQ=2048, TP=256, NS=16, LS=4):
        s.D = D; s.KC = D // 128; s.SEQ = SEQ; s.TP = TP
        s.NSLAB = SEQ // TP
        s.NS = NS; s.LS = LS; s.TS = NS * LS
        s.RW = D // 2; s.RH = s.RW // 64; s.NHP = s.RH // 2
        s.GW = D // 2; s.GH = s.GW // 128
        s.RC = 3 * s.NHP + 3
        s.RCOLS = 3 * s.RW + 384
        s.GCOLS = 4 * s.GW + 2 * s.GH
        s.FF = 4 * D; s.FC = s.FF // 128
        s.C = 64
        s.t_r = 0; s.t_k = s.NHP; s.t_v = 2 * s.NHP; s.t_lora = 3 * s.NHP
        s.t_gq = s.RC; s.t_gk = s.RC + s.GH; s.t_gv = s.RC + 2 * s.GH; s.t_gz = s.RC + 3 * s.GH
        s.t_gb = s.RC + 4 * s.GH; s.t_ga = s.t_gb + 1
        s.t_ma = s.t_ga + 1; s.t_mb = s.t_ma + s.KC
        s.NT_IN = s.t_mb + s.KC
        o = 0
        def take(n):
            nonlocal o
            r = o; o += n; return r
        s.v_bada = take(6 * s.KC); s.v_n1 = take(s.KC); s.v_n2 = take(s.KC); s.v_fw = take(s.KC)
        s.v_mu = take(s.RC); s.v_w0 = take(s.NHP); s.v_a0 = take(s.NHP); s.v_kk = take(s.NHP)
        s.v_ka = take(s.NHP); s.v_rk = take(s.NHP); s.v_lw = take(s.NHP); s.v_lb = take(s.NHP)
        s.v_cw = take(4 * 3 * s.GH); s.v_gnw = take(1)
        s.NV = o


def KW(m, *a, **k):
    return lambda en: getattr(en, m)(*a, **k)


class StopBuild(Exception):
    pass


class Res:
    __slots__ = ("w", "rs")
    def __init__(s):
        s.w = None; s.rs = []


class Til:
    def __init__(s, t, nres=1):
        s.t = t; s.r = [Res() for _ in range(nres)]
    def __getitem__(s, k):
        return s.t[k]


ENGS = ["pe", "act", "dve", "pool", "sp"]


class Prog:
    def __init__(s, nc):
        s.nc = nc
        s.q = {e: [] for e in ENGS}
        s.cnt = {e: 0 for e in ENGS}
        s.seen = {e: {} for e in ENGS}
        s.streams = []
        s.sb_off = 0
        s.rr = 0

    def sb(s, name, shape, dt, nres=1, at=None):
        esz = 4 if dt == F32 else 2
        n = 1
        for d in shape[1:]:
            n *= d
        nbytes = (n * esz + 31) // 32 * 32
        if at is None:
            at = s.sb_off
            s.sb_off += nbytes
        t = s.nc.alloc_sbuf_tensor("sb_" + name, list(shape), dt)
        return Til(t, nres)

    def stream(s):
        st = {"id": len(s.streams), "n": 0}
        s.streams.append(st)
        return st

    def _need(s, e, ev, waits, raw=False):
        if ev is None:
            return
        key, idx = ev
        if key == e and not (raw and e != "pe"):
            return
        if not isinstance(key, str):
            idx = s.streams[key[1]]["n"]
        if s.seen[e].get(key, 0) < idx:
            waits[key] = max(waits.get(key, 0), idx)

    def _deps(s, e, reads, writes):
        waits = {}
        for r in reads:
            s._need(e, r.w, waits, raw=True)
        for r in writes:
            s._need(e, r.w, waits)
            for x in r.rs:
                s._need(e, x, waits)
        for k, v in waits.items():
            s.seen[e][k] = v
        return list(waits.items())

    def op(s, e, fn, reads=(), writes=()):
        reads = [r for x in reads for r in (x.r if isinstance(x, Til) else [x])]
        writes = [r for x in writes for r in (x.r if isinstance(x, Til) else [x])]
        waits = s._deps(e, reads, writes)
        s.cnt[e] += 1
        me = (e, s.cnt[e])
        s.q[e].append(["op", fn, waits, s.cnt[e]])
        for r in reads:
            r.rs.append(me)
        for r in writes:
            r.w = me; r.rs = []

    def dma(s, q, st, out, in_, reads=(), writes=()):
        reads = [r for x in reads for r in (x.r if isinstance(x, Til) else [x])]
        writes = [r for x in writes for r in (x.r if isinstance(x, Til) else [x])]
        waits = s._deps(q, reads, writes)
        st["n"] += 1
        me = (("dma", st["id"]), st["n"])
        s.q[q].append(["dma", (out, in_), waits, st])
        for r in reads:
            r.rs.append(me)
        for r in writes:
            r.w = me; r.rs = []

    def barrier(s):
        for e in ENGS:
            waits = {}
            for f in ENGS:
                if f != e and s.cnt[f] > 0:
                    s._need(e, (f, s.cnt[f]), waits)
            for st in s.streams:
                if st["n"] > 0:
                    s._need(e, (("dma", st["id"]), st["n"]), waits)
            for k, v in waits.items():
                s.seen[e][k] = v
            if waits:
                s.q[e].append(["wait", None, list(waits.items()), None])

    def act(s, reads, writes, **kw):
        s.op("act", lambda e: e.activation(**kw), reads, writes)

    def emit(s, es):
        nc = s.nc
        EPOCH = getattr(s, 'epoch', 30000)
        DSEG = getattr(s, 'dseg', 1800)
        marked = {e: set() for e in ENGS}
        for e in ENGS:
            for item in s.q[e]:
                for key, idx in item[2]:
                    if isinstance(key, str):
                        marked[key].add(idx)
        rank = {}
        for e in ENGS:
            rank[e] = {idx: i for i, idx in enumerate(sorted(marked[e]))}
        sems = {e: [es.enter_context(nc.semaphore("sem_%s_%d" % (e, k))) for k in range(len(rank[e]) // EPOCH + 1)] for e in ENGS}
        dsems = [[es.enter_context(nc.semaphore("dsem%d_%d" % (st["id"], k))) for k in range((max(st["n"], 1) - 1) // DSEG + 1)] for st in s.streams]
        block = es.enter_context(nc.Block())

        def dma_wait(eng, sid, n):
            k = (n - 1) // DSEG
            if k > 0:
                eng.wait_ge(dsems[sid][k - 1], 16 * DSEG)
            eng.wait_ge(dsems[sid][k], 16 * (n - k * DSEG))

        def run(e, eng):
            dcnt = {}
            for item in s.q[e]:
                kind, payload, waits, extra = item
                for key, idx in waits:
                    if isinstance(key, str):
                        r = rank[key][idx]
                        eng.wait_ge(sems[key][r // EPOCH], r % EPOCH + 1)
                    else:
                        dma_wait(eng, key[1], idx)
                if kind == "op":
                    ins = payload(eng)
                    if extra in rank[e]:
                        ins.then_inc(sems[e][rank[e][extra] // EPOCH], 1)
                elif kind == "dma":
                    out, in_ = payload
                    sid = extra["id"]
                    dcnt[sid] = dcnt.get(sid, 0) + 1
                    eng.dma_start(out=out, in_=in_).then_inc(dsems[sid][(dcnt[sid] - 1) // DSEG], 16)
            if e == "sp":
                for st in s.streams:
                    if st["n"] > 0:
                        for k in range((st["n"] - 1) // DSEG + 1):
                            eng.wait_ge(dsems[st["id"]][k], 16 * min(DSEG, st["n"] - k * DSEG))

        @block.tensor
        def _(eng):
            run("pe", eng)

        @block.scalar
        def _(eng):
            run("act", eng)

        @block.vector
        def _(eng):
            run("dve", eng)

        @block.gpsimd
        def _(eng):
            run("pool", eng)

        @block.sync
        def _(eng):
            run("sp", eng)


def build(cfg, dbg=None):
    nc = bass.Bass("TRN2", target_bir_lowering=False)
    P = Prog(nc)
    P.final_waits = []
    g = cfg
    D, KC, TP, C = g.D, g.KC, g.TP, g.C
    NS, LS, TS = g.NS, g.LS, g.TS
    NHP, GH, RC = g.NHP, g.GH, g.RC
    TM = max(TP, TS)

    def din(name, shape, dt=F32):
        return nc.dram_tensor(name, list(shape), dt, kind="ExternalInput").ap()

    def dout(name, shape, dt=F32):
        return nc.dram_tensor(name, list(shape), dt, kind="ExternalOutput").ap()

    d_xp = din("xp", [g.NSLAB, 128, KC, TP])
    d_xs = din("xs", [128, KC, TS])
    d_cT = din("cT", [128, KC, 1 + NS])
    d_flag = din("flag", [128, 1])
    d_vecs = din("vecsT", [128, g.NV])
    d_gsc = din("gsc", [128, 2 * GH])
    d_const = din("consts", [128, 10, 128])
    d_cm = din("cm", [128, NS, C], BF16)
    d_sm = din("sm", [C, NS])
    d_rst = din("rst", [128, 2, TM])
    d_wada = din("wada", [6 * KC, 128, KC, 128])
    d_win = din("win", [g.NT_IN, 128, KC, 128])
    d_ww2 = din("ww2", [128, g.RW])
    d_wg2 = din("wg2", [128, 2, g.RW])
    d_woa = din("woa", [KC, 128, NHP, 128])
    d_wob = din("wob", [KC, 128, GH, 128])
    d_wo = din("wo", [KC, 128, KC, 128])
    d_wup = din("wup", [g.FC, 128, KC, 128])
    d_wdn = din("wdn", [g.FC // 8, KC, 128, 8, 128])
    d_swkv = din("s_wkv", [NHP, 128, NS, 64])
    d_sshift = din("s_shift", [128, RC, NS])
    d_sgdn = din("s_gdn", [GH, 128, NS, 128])
    d_sconv = din("s_conv", [128, 3 * GH, NS, 3])

    o_yp = dout("o_yp", [g.NSLAB // 2, 128, KC, TP])
    o_ys = dout("o_ys", [128, KC, TS])
    o_pwkv = dout("o_pwkv", [NHP, 128, 1, 64])
    o_pshift = dout("o_pshift", [128, RC, 1])
    o_pgdn = dout("o_pgdn", [GH, 128, 1, 128])
    o_pconv = dout("o_pconv", [128, 3 * GH, 1, 3])
    o_swkv = dout("o_swkv", [NHP, 128, NS, 64])
    o_sshift = dout("o_sshift", [128, RC, NS])
    o_sgdn = dout("o_sgdn", [GH, 128, NS, 128])
    o_sconv = dout("o_sconv", [128, 3 * GH, NS, 3])
    if dbg:
        o_dbg = dout("o_dbg", dbg["shape"])

    vec = P.sb("vec", [128, g.NV], F32)
    gsc = P.sb("gsc", [128, 2 * GH], F32)
    flag = P.sb("flag", [128, 1], F32)
    cst = P.sb("cst", [128, 10, 128], F32)
    cstb = P.sb("cstb", [128, 10, 128], BF16)
    cm = P.sb("cm", [128, NS, C], BF16)
    smk = P.sb("smk", [C, NS], F32)
    rst = P.sb("rst", [128, 2, TM], F32)
    mT = P.sb("mT", [128, 6, KC, 1 + NS], F32)
    cTf = P.sb("cTf", [128, KC, 1 + NS], F32)
    cTb = P.sb("cTb", [128, KC, 1 + NS], BF16)
    ww2b = P.sb("ww2b", [128, g.RW], BF16)
    wg2b = P.sb("wg2b", [128, 2, g.RW], BF16)
    okc = P.sb("okc", [128, NHP], F32)
    nw0 = P.sb("nw0", [128, NHP], F32)
    pZf = [P.sb("pZf%d" % i, [128, 1, 64], F32) for i in range(NHP)]
    pSf = [P.sb("pSf%d" % i, [128, 1, 128], F32) for i in range(GH)]
    pshift = P.sb("pshift", [128, RC, 1], F32)
    pconv = P.sb("pconv", [128, 3 * GH, 1, 3], F32)
    hT = P.sb("hT", [128, KC, TM], BF16, nres=KC)
    yaT = P.sb("yaT", [128, NHP, TM], BF16, nres=NHP)
    ybT = P.sb("ybT", [128, GH, TM], BF16, nres=GH)
    mgT = P.sb("mgT", [128, KC, TM], BF16, nres=KC)
    x1T = P.sb("x1T", [128, KC, TM], F32, nres=KC)
    rstd = P.sb("rstd", [128, TM], F32)
    NWB = 4
    WCOLS = max(KC * 128, 1024, 2 * g.RW)
    wst = [P.sb("wst0", [128, WCOLS], F32)] * NWB
    wbf = [P.sb("wbf%d" % i, [128, WCOLS], BF16) for i in range(NWB)]
    wst_s = [P.stream() for _ in range(NWB)]
    NTMP = 26
    tmpf = [P.sb("tmp%d" % i, [128, TM + 8], F32) for i in range(NTMP)]
    NTB = 10
    tmpb = [P.sb("tmb%d" % i, [128, TM], BF16) for i in range(NTB)]
    NCH = TM // C
    tokb = [P.sb("tok%d" % i, [C, NCH, 128], BF16) for i in range(3)]
    pw = P.sb("pw", [C, 2, 4, C], BF16)
    invP = [P.sb("invP%d" % i, [C, 2, 2, C], F32) for i in range(2)]
    invT = [P.sb("invT%d" % i, [C, 2, C], F32) for i in range(2)]
    TTb = P.sb("TTb", [C, 2, C], BF16)
    chs = P.sb("chs", [C, 128], BF16)
    upad = P.sb("upad", [C, 2, 128], BF16)
    vpad = P.sb("vpad", [C, 2, 128], BF16)
    vnew = P.sb("vnew", [C, 128], BF16)
    expa = [P.sb("expa%d" % i, [128, NS, C], BF16) for i in range(2)]
    expt = [P.sb("expt%d" % i, [C, NS, 128], BF16) for i in range(2)]
    T2T = P.sb("T2T", [C, NCH, C], BF16)
    ATs = P.sb("ATs", [C, NCH, C], BF16)
    cds = P.sb("cds", [128, NCH, NS], F32)
    tk = P.sb("tk", [C, NCH, 6, GH], F32)
    Zf_w = P.sb("Zf_w", [128, NS, 64], F32)
    Zb = P.sb("Zb", [128, NS, 128], BF16)
    Sf_w = P.sb("Sf_w", [128, NS, 128], F32)
    Sb = Zb
    sst = P.sb("sst", [128, RC, NS], F32)
    cvin = [P.sb("cvin%d" % i, [128, NS, 3], F32) for i in range(2)]
    cvin_s = [P.stream() for _ in range(2)]
    cvout_s = P.stream()
    sso = P.sb("sso", [128, RC, NS], F32)
    xch = [P.sb("xch%d" % i, [128, TM], F32) for i in range(2)]
    xch_s = [P.stream() for _ in range(2)]
    ych = [P.sb("ych%d" % i, [128, TM], F32) for i in range(2)]
    ych_s = [P.stream() for _ in range(2)]
    uh = P.sb("uh", [128, 8, TM], BF16, nres=8)
    assert P.sb_off <= 208 * 1024, P.sb_off

    acc = [Til(nc.alloc_psum_tensor("acc%d" % i, [128, 512], F32)) for i in range(4)]
    smp = [Til(nc.alloc_psum_tensor("smp%d" % i, [128, 512], F32)) for i in range(4)]
    misc_s = P.stream()

    ident = cst[:, 0, :]; bones = cst[:, 1, :]; ones = cst[:, 2, :]

    def load(dst_til, dst_ap, src_ap, st=None, q="pool"):
        P.dma(q, st or misc_s, dst_ap, src_ap, writes=[dst_til])

    def store(dst_ap, src_til, src_ap, st=None, q="pool"):
        P.dma(q, st or misc_s, dst_ap, src_ap, reads=[src_til])

    cast_rr = [0]

    wplan = cfg.wplan if getattr(cfg, "wplan", None) else None
    wreq = []
    wissued = [0]

    scr = {
        "win": nc.dram_tensor("scr_win", [g.NT_IN, 128, KC * 128], BF16, kind="Internal").ap(),
        "woa": nc.dram_tensor("scr_woa", [KC, 128, NHP * 128], BF16, kind="Internal").ap(),
        "wob": nc.dram_tensor("scr_wob", [KC, 128, GH * 128], BF16, kind="Internal").ap(),
        "wo": nc.dram_tensor("scr_wo", [KC, 128, KC * 128], BF16, kind="Internal").ap(),
        "wup": nc.dram_tensor("scr_wup", [g.FC, 128, KC * 128], BF16, kind="Internal").ap(),
        "wdn": nc.dram_tensor("scr_wdn", [(g.FC // 8) * KC, 128, 8 * 128], BF16, kind="Internal").ap(),
    }
    scr_res = {}
    wbk_s = P.stream()

    def _wissue(k, key, src_ap, npart, ncols):
        i = k % NWB
        if key is not None and key in scr_res:
            P.dma("sp", wst_s[i], wbf[i][0:npart, 0:ncols], scr[key[0]][key[1]][0:npart, 0:ncols], reads=[scr_res[key]], writes=[wbf[i]])
            return
        P.dma("sp", wst_s[i], wst[i][0:npart, 0:ncols], src_ap, writes=[wst[i]])
        e = ["dve", "act", "pool"][cast_rr[0] % 3]
        cast_rr[0] += 1
        if e == "act":
            P.op("act", KW("activation", out=wbf[i][0:npart, 0:ncols], in_=wst[i][0:npart, 0:ncols], func=AF.Copy),
                 [wst[i]], [wbf[i]])
        else:
            P.op(e, KW("tensor_copy", out=wbf[i][0:npart, 0:ncols], in_=wst[i][0:npart, 0:ncols]),
                 [wst[i]], [wbf[i]])
        if key is not None:
            scr_res[key] = Res()
            P.dma("pool", wbk_s, scr[key[0]][key[1]][0:npart, 0:ncols], wbf[i][0:npart, 0:ncols], reads=[wbf[i]], writes=[scr_res[key]])

    def wget(src_ap, npart, ncols, key=None):
        k = len(wreq)
        wreq.append((key, src_ap, npart, ncols))
        if wplan is None:
            _wissue(k, key, src_ap, npart, ncols)
        else:
            while wissued[0] <= min(k + NWB - 1, len(wplan) - 1):
                _wissue(wissued[0], *wplan[wissued[0]])
                wissued[0] += 1
        return wbf[k % NWB]

    def mm(out_til, out_ap, lhsT, rhs, start, stop, reads):
        P.op("pe", KW("matmul", out_ap, lhsT=lhsT, rhs=rhs, start=start, stop=stop), reads, [out_til])

    def big_mm(acc_t, T, wt, act_til, kcn, npart=128, rhs_fn=None):
        for kc in range(kcn):
            mm(acc_t, acc_t[:, 0:T], wt[0:npart, kc * 128:(kc + 1) * 128], rhs_fn(kc), kc == 0, kc == kcn - 1,
               [wt, act_til.r[kc]])

    def dve(fn, reads, writes):
        P.op("dve", fn, reads, writes)

    def act(reads, writes, **kw):
        P.op("act", KW("activation", **kw), reads, writes)

    def rsqrt_to(dst_til, dst_ap, src_til, src_ap, scale, eps, tmp_til, tmp_ap):
        act([src_til, cst], [tmp_til], out=tmp_ap, in_=src_ap, func=AF.Sqrt, scale=scale, bias=epsc[eps])
        dve(KW("reciprocal", out=dst_ap, in_=tmp_ap), [tmp_til], [dst_til])

    load(vec, vec[:], d_vecs); load(gsc, gsc[:], d_gsc); load(flag, flag[:], d_flag)
    load(cst, cst[:], d_const); load(cm, cm[:], d_cm); load(smk, smk[:], d_sm); load(rst, rst[:], d_rst)
    load(cTf, cTf[:], d_cT)
    load(sst, sst[:], d_sshift)
    dve(KW("tensor_copy", out=cstb[:], in_=cst[:]), [cst], [cstb])
    P.dma("sp", wst_s[0], wst[0][:, 0:g.RW], d_ww2, writes=[wst[0]])
    dve(KW("tensor_copy", out=ww2b[:], in_=wst[0][:, 0:g.RW]), [wst[0]], [ww2b])
    P.dma("sp", wst_s[1], wst[1][:, 0:2 * g.RW], d_wg2.rearrange("p a n -> p (a n)"), writes=[wst[1]])
    dve(KW("tensor_copy", out=wg2b[:].rearrange("p a n -> p (a n)"), in_=wst[1][:, 0:2 * g.RW]), [wst[1]], [wg2b])
    epst = P.sb("epst", [128, 8], F32)
    epsc = {}
    for i, v in enumerate([NORM_EPS, GN_EPS, 1e-6, -0.5, 1.0, 0.0]):
        dve(KW("memset", epst[:, i:i + 1], v), [], [epst])
        epsc[v] = epst[:, i:i + 1]
    dve(KW("tensor_scalar", out=okc[:], in0=vec[:, g.v_ka:g.v_ka + NHP], scalar1=-1.0, scalar2=1.0, op0=ALU.mult, op1=ALU.add), [vec], [okc])
    dve(KW("tensor_scalar", out=nw0[:], in0=vec[:, g.v_w0:g.v_w0 + NHP], scalar1=-1.0, scalar2=None, op0=ALU.mult), [vec], [nw0])
    act([gsc], [gsc], out=gsc[:, 0:GH], in_=gsc[:, 0:GH], func=AF.Exp)
    dve(KW("tensor_scalar", out=gsc[:, 0:GH], in0=gsc[:, 0:GH], scalar1=-1.0, scalar2=None, op0=ALU.mult), [gsc], [gsc])
    act([cTf], [cTb], out=cTb[:], in_=cTf[:], func=AF.Silu)
    NSQ = 1 + NS
    for j in range(6):
        for c in range(KC):
            wt = wget(d_wada[j * KC + c].rearrange("p k n -> p (k n)"), 128, KC * 128)
            a_ = acc[(j * KC + c) % 2]
            for kc in range(KC):
                mm(a_, a_[:, 0:NSQ], wt[:, kc * 128:(kc + 1) * 128], cTb[:, kc, :], kc == 0, kc == KC - 1, [wt, cTb])
            col = g.v_bada + j * KC + c
            act([a_, vec], [mT], out=mT[:, j, c, :], in_=a_[:, 0:NSQ], func=AF.Identity, bias=vec[:, col:col + 1], scale=1.0)
    for slot, vcol in ((1, g.v_n1), (4, g.v_n2)):
        dve(KW("tensor_scalar", out=mT[:, slot], in0=mT[:, slot], scalar1=1.0, scalar2=None, op0=ALU.add), [mT], [mT])
        dve(KW("tensor_tensor", out=mT[:, slot], in0=mT[:, slot],
                                                              in1=vec[:, vcol:vcol + KC].unsqueeze(2).broadcast_to([128, KC, NSQ]), op=ALU.mult), [mT, vec], [mT])

    for t_ in pZf + pSf + [pshift, pconv]:
        dve(KW("memset", t_[:], 0.0), [], [t_])
    dve(KW("memset", Zb[:], 0.0), [], [Zb])
    dve(KW("memset", upad[:], 0.0), [], [upad])
    dve(KW("memset", vpad[:], 0.0), [], [vpad])

    def stop(n):
        if dbg and dbg.get('stop') == n:
            raise StopBuild()

    def dump(name, til, ap, np_=128):
        if dbg and dbg['what'] == name:
            t0 = tmpf[0]
            P.op("pool", KW("tensor_copy", out=t0[0:np_, 0:ap.shape[1]], in_=ap), [til], [t0])
            store(o_dbg[0:np_, 0, 0:ap.shape[1]], t0, t0[0:np_, 0:ap.shape[1]])
            raise StopBuild()

    def dump_chunks(name, til, n, T, kind, si):
        if dbg and dbg['what'] == name and dbg['slab'] == (kind, si):
            t0 = tmpf[5]
            for c in range(n):
                P.op("pool", KW("tensor_copy", out=t0[:, 0:T], in_=til[:, c, 0:T]), [til.r[c]], [t0])
                store(o_dbg[:, c, 0:T], t0, t0[:, 0:T])
            raise StopBuild()

    def _run_slabs():
        for si_ in range((dbg['slab'][1] + 1 if dbg['slab'][0] == 'p' else 0) if dbg else g.NSLAB):
            slab('p', si_)
        if not dbg or dbg['slab'][0] == 's':
            slab('s', 0)

    def slab(kind, si):
        if kind == "p":
            T = TP; nseg = 1; L = TP; full = si >= g.NSLAB // 2 or bool(dbg and dbg.get('allfull'))
            xsrc = d_xp[si]; seg0 = 0; ri = 0
            mskS = cstb[0:C, 3, 0:C]; mskI = cstb[0:C, 4, 0:C]; mskSf = cst[0:C, 3, 0:C]; mskIf = cst[0:C, 4, 0:C]
            segones = cst[0:C, 2, 0:C]
        else:
            T = TS; nseg = NS; L = LS; full = True
            xsrc = d_xs; seg0 = 1; ri = 1
            mskS = cstb[0:C, 5, 0:C]; mskI = cstb[0:C, 6, 0:C]; mskSf = cst[0:C, 5, 0:C]; mskIf = cst[0:C, 6, 0:C]
            segones = cst[0:C, 7, 0:C]
        nch = T // C
        W = nseg * (L + 3)

        def segv(ap2d, lo, n):
            return ap2d[:, 0:W].rearrange("p (s l) -> p s l", s=nseg)[:, :, lo:lo + n]

        def tokv(ap2d):
            return ap2d[:, 0:T].rearrange("p (s l) -> p s l", s=nseg)

        def seg_affine(eng, out_ap, in_til, in_ap, sc_slot, sc_c, add_til, add_ap, reads, writes):
            for sgi in range(nseg):
                sl = slice(sgi * L, (sgi + 1) * L)
                scol = mT[:, sc_slot, sc_c, seg0 + sgi:seg0 + sgi + 1]
                if isinstance(add_ap, tuple):
                    bcol = mT[:, add_ap[0], add_ap[1], seg0 + sgi:seg0 + sgi + 1]
                    act(reads, writes, out=out_ap[:, sl], in_=in_ap[:, sl], func=AF.Identity, bias=bcol, scale=scol)
                else:
                    dve(KW("scalar_tensor_tensor", out=out_ap[:, sl], in0=in_ap[:, sl], scalar=scol,
                                                                              in1=add_ap[:, sl], op0=ALU.mult, op1=ALU.add), reads, writes)

        def stats_rstd(src_fn, nchunks, scale, eps):
            a_ = acc[3]
            for c in range(nchunks):
                til, ap = src_fn(c)
                tq = tmpf[c % 2]
                act([til], [tq], out=tq[:, 0:T], in_=ap, func=AF.Square)
                mm(a_, a_[:, 0:T], ones, tq[:, 0:T], c == 0, c == nchunks - 1, [cst, tq])
            rsqrt_to(rstd, rstd[:, 0:T], a_, a_[:, 0:T], scale, eps, tmpf[2], tmpf[2][:, 0:T])

        def xload(c):
            i = c % 2
            P.dma("pool", xch_s[i], xch[i][:, 0:T], xsrc[:, c, :], writes=[xch[i]])
            return xch[i], xch[i][:, 0:T]
        stats_rstd(xload, KC, 1.0 / D, NORM_EPS)
        for c in range(KC):
            til, ap = xload(c)
            t0 = tmpf[c % 2]
            dve(KW("tensor_tensor", out=t0[:, 0:T], in0=ap, in1=rstd[:, 0:T], op=ALU.mult), [til, rstd], [t0])
            seg_affine("act", hT[:, c, :], t0, t0, 1, c, None, (0, c), [t0, mT], [hT.r[c]])
        if dbg and dbg["what"] == "hT" and dbg["slab"] == (kind, si):
            t0 = tmpf[5]
            for c in range(KC):
                P.op("pool", KW("tensor_copy", out=t0[:, 0:T], in_=hT[:, c, 0:T]), [hT.r[c]], [t0])
                store(o_dbg[:, c, 0:T], t0, t0[:, 0:T])
            return

        Cg = C if kind == "p" else LS
        ngrp = T // Cg
        nlev = 6 if kind == "p" else 2
        mskL = cst[0:C, 8 if kind == "p" else 9, 0:C]
        shsrc = pshift if kind == "p" else sst
        shdst = pshift if kind == "p" else sso
        acc_rr = [0]

        def nacc():
            acc_rr[0] += 1
            return acc[acc_rr[0] % 3]

        def proj(tid, ncols=128):
            wt = wget(d_win[tid].rearrange("p k n -> p (k n)"), 128, KC * 128, key=("win", tid))
            a_ = nacc()
            for kc in range(KC):
                mm(a_, a_[0:ncols, 0:T], wt[:, kc * 128:kc * 128 + ncols], hT[:, kc, 0:T], kc == 0, kc == KC - 1, [wt, hT.r[kc]])
            return a_

        def shift_mix(chunk, a_, xs_t):
            pe_, d_ = tmpf[3], tmpf[4]
            W1 = nseg * (L + 1)
            pv = pe_[:, 0:W1].rearrange("p (s l) -> p s l", s=nseg)
            act([a_], [pe_], out=pv[:, :, 1:L + 1], in_=tokv(a_.t), func=AF.Copy)
            P.op("pool", KW("tensor_copy", out=pv[:, :, 0:1], in_=shsrc[:, chunk, :].unsqueeze(2)), [shsrc], [pe_])
            P.op("pool", KW("tensor_copy", out=shdst[:, chunk, :].unsqueeze(2), in_=pv[:, :, L:L + 1]), [pe_], [shdst])
            dve(KW("tensor_tensor", out=tokv(d_.t), in0=pv[:, :, 0:L], in1=pv[:, :, 1:L + 1], op=ALU.subtract), [pe_], [d_])
            mucol = vec[:, g.v_mu + chunk:g.v_mu + chunk + 1]
            dve(KW("scalar_tensor_tensor", out=tokv(xs_t.t), in0=tokv(d_.t), scalar=mucol, in1=pv[:, :, 1:L + 1],
                                                   op0=ALU.mult, op1=ALU.add), [d_, pe_, vec], [xs_t])

        dve(KW("memset", Zb[:], 0.0), [], [Zb])
        ldw, sdg0, sdg1 = tmpb[0], tmpb[1], tmpb[2]
        xs_t = tmpf[5]
        a_ = proj(g.t_lora + 0); shift_mix(g.t_lora + 0, a_, xs_t)
        act([xs_t], [ldw], out=ldw[0:64, 0:T], in_=xs_t[0:64, 0:T], func=AF.Tanh)
        act([xs_t], [ldw], out=ldw[64:128, 0:T], in_=xs_t[64:128, 0:T], func=AF.Copy)
        for j, sd in ((1, sdg0), (2, sdg1)):
            a_ = proj(g.t_lora + j); shift_mix(g.t_lora + j, a_, xs_t)
            if full:
                act([xs_t], [sd], out=sd[:, 0:T], in_=xs_t[:, 0:T], func=AF.Sigmoid)

        stop(1)
        chn = {"rhs": smp[0], "u": smp[1], "o": smp[0]}
        sun = [smp[2], smp[3]]
        tts = smp[1]
        ivp = smp[0]
        B_RHS, B_U, B_O, B_IVP, B_TTS = smp[0], smp[1], smp[0], smp[0], smp[1]

        def inverse(nl):
            pp, tt = 0, 0
            for lev in range(1, nl):
                last = lev == nl - 1
                for hh in range(2):
                    mm(ivp, B_IVP[0:C, (hh * 2) * C:(hh * 2 + 1) * C], invP[pp][:, hh, 1, :], invP[pp][:, hh, 0, :], True, True, [invP[pp]])
                    if True:
                        mm(ivp, B_IVP[0:C, (hh * 2 + 1) * C:(hh * 2 + 2) * C], invP[pp][:, hh, 0, :], invP[pp][:, hh, 1, :], True, True, [invP[pp]])
                stop(51)
                pn = 1 - pp
                act([ivp], [invP[pn]], out=invP[pn][:].rearrange("p a b c -> p (a b c)"), in_=B_IVP[0:C, 0:4 * C], func=AF.Copy)
                stop(52)
                for hh in range(2):
                    mm(tts, B_TTS[0:C, 4 * C + hh * C:4 * C + (hh + 1) * C], invP[pn][:, hh, 0, :], invT[tt][:, hh, :], True, True, [invP[pn], invT[tt]])
                stop(53)
                tn = 1 - tt
                dve(KW("tensor_tensor", out=invT[tn][:].rearrange("p a c -> p (a c)"), in0=B_TTS[0:C, 4 * C:6 * C],
                                                              in1=invT[tt][:].rearrange("p a c -> p (a c)"), op=ALU.add), [tts, invT[tt]], [invT[tn]])
                pp, tt = pn, tn
                stop(60 + lev)
            return tt

        for hp in range(NHP):
            xr, xk, xv = tmpf[6], tmpf[7], tmpf[8]
            for tid, xt_ in ((g.t_r + hp, xr), (g.t_k + hp, xk), (g.t_v + hp, xv)):
                a_ = proj(tid); shift_mix(tid, a_, xt_)
            hc = slice(hp * 128, (hp + 1) * 128)
            e2, at_, gt_, kk, sc_, kt_, kka, bon, cs, E1, E2, E3, E4, Bh, Kh, oT, sc2 = (tmpf[i] for i in range(9, 26))
            a_ = nacc()
            mm(a_, a_[:, 0:T], ww2b[0:64, hc], ldw[0:64, 0:T], True, True, [ww2b, ldw])
            act([a_, nw0], [e2], out=e2[:, 0:T], in_=a_[:, 0:T], func=AF.Exp, scale=-1.0, bias=nw0[:, hp:hp + 1])
            act([e2, epst], [sc_], out=sc_[:, 0:T], in_=e2[:, 0:T], func=AF.Ln, bias=epsc[1.0], scale=1.0)
            act([sc_, epst], [e2], out=e2[:, 0:T], in_=sc_[:, 0:T], func=AF.Exp, scale=-1.0, bias=epsc[-0.5])
            a_ = nacc()
            mm(a_, a_[:, 0:T], ww2b[64:128, hc], ldw[64:128, 0:T], True, True, [ww2b, ldw])
            act([a_, vec], [at_], out=at_[:, 0:T], in_=a_[:, 0:T], func=AF.Sigmoid, bias=vec[:, g.v_a0 + hp:g.v_a0 + hp + 1], scale=1.0)
            if full:
                a_ = nacc()
                mm(a_, a_[:, 0:T], wg2b[:, 0, hc], sdg0[:, 0:T], True, False, [wg2b, sdg0])
                mm(a_, a_[:, 0:T], wg2b[:, 1, hc], sdg1[:, 0:T], False, True, [wg2b, sdg1])
                act([a_], [gt_], out=gt_[:, 0:T], in_=a_[:, 0:T], func=AF.Copy)
            dve(KW("tensor_scalar", out=kk[:, 0:T], in0=xk[:, 0:T], scalar1=vec[:, g.v_kk + hp:g.v_kk + hp + 1], scalar2=None, op0=ALU.mult), [xk, vec], [kk])
            act([kk], [sc_], out=sc_[:, 0:T], in_=kk[:, 0:T], func=AF.Square)
            a_ = nacc()
            mm(a_, a_[:, 0:T], bones, sc_[:, 0:T], True, True, [cst, sc_])
            rsqrt_to(sc_, sc_[:, 0:T], a_, a_[:, 0:T], 1.0, 1e-6, sc2, sc2[:, 0:T])
            dve(KW("tensor_tensor", out=kk[:, 0:T], in0=kk[:, 0:T], in1=sc_[:, 0:T], op=ALU.mult), [kk, sc_], [kk])
            dve(KW("tensor_scalar", out=kt_[:, 0:T], in0=at_[:, 0:T], scalar1=vec[:, g.v_ka + hp:g.v_ka + hp + 1], scalar2=okc[:, hp:hp + 1],
                                            op0=ALU.mult, op1=ALU.add), [at_, vec, okc], [kt_])
            dve(KW("tensor_tensor", out=kt_[:, 0:T], in0=kt_[:, 0:T], in1=xk[:, 0:T], op=ALU.mult), [kt_, xk], [kt_])
            dve(KW("tensor_tensor", out=kka[:, 0:T], in0=kk[:, 0:T], in1=at_[:, 0:T], op=ALU.mult), [kk, at_], [kka])
            if full:
                dve(KW("scalar_tensor_tensor", out=bon[:, 0:T], in0=xr[:, 0:T], scalar=vec[:, g.v_rk + hp:g.v_rk + hp + 1], in1=kt_[:, 0:T],
                                                       op0=ALU.mult, op1=ALU.mult), [xr, vec, kt_], [bon])
                a_ = nacc()
                mm(a_, a_[:, 0:T], bones, bon[:, 0:T], True, True, [cst, bon])
                dve(KW("tensor_tensor", out=bon[:, 0:T], in0=a_[:, 0:T], in1=xv[:, 0:T], op=ALU.mult), [a_, xv], [bon])
            dve(KW("tensor_tensor_scan", out=cs[:, 0:T], data0=rst[:, ri, 0:T], data1=e2[:, 0:T], initial=0.0, op0=ALU.mult, op1=ALU.add), [rst, e2], [cs])
            act([cs], [E1], out=E1[:, 0:T], in_=cs[:, 0:T], func=AF.Exp, scale=-1.0)
            act([cs], [E2], out=E2[:, 0:T], in_=cs[:, 0:T], func=AF.Exp, scale=1.0)
            dve(KW("tensor_tensor", out=E3[:, 0:T], in0=cs[:, 0:T], in1=e2[:, 0:T], op=ALU.subtract), [cs, e2], [E3])
            act([E3], [E3], out=E3[:, 0:T], in_=E3[:, 0:T], func=AF.Exp, scale=-1.0)
            gv = lambda t_: t_[:, 0:T].rearrange("p (n c) -> p n c", c=Cg)
            dve(KW("tensor_tensor", out=gv(E4), in0=gv(E2), in1=gv(E1)[:, :, Cg - 1:Cg].broadcast_to([128, ngrp, Cg]), op=ALU.mult), [E2, E1], [E4])
            At, Bt, Kt, Rt = tmpb[3], tmpb[4], tmpb[5], tmpb[6]
            dve(KW("scalar_tensor_tensor", out=At[:, 0:T], in0=kk[:, 0:T], scalar=-1.0, in1=E3[:, 0:T], op0=ALU.mult, op1=ALU.mult), [kk, E3], [At])
            dve(KW("tensor_tensor", out=Bt[:, 0:T], in0=kka[:, 0:T], in1=E2[:, 0:T], op=ALU.mult), [kka, E2], [Bt])
            dve(KW("tensor_tensor", out=Kt[:, 0:T], in0=kt_[:, 0:T], in1=E2[:, 0:T], op=ALU.mult), [kt_, E2], [Kt])
            if full:
                dve(KW("tensor_tensor", out=Rt[:, 0:T], in0=xr[:, 0:T], in1=E1[:, 0:T], op=ALU.mult), [xr, E1], [Rt])
            dve(KW("tensor_tensor", out=Bh[:, 0:T], in0=kka[:, 0:T], in1=E4[:, 0:T], op=ALU.mult), [kka, E4], [Bh])
            dve(KW("tensor_tensor", out=Kh[:, 0:T], in0=kt_[:, 0:T], in1=E4[:, 0:T], op=ALU.mult), [kt_, E4], [Kh])
            dump('xr', xr, xr[:, 0:T]); dump('xk', xk, xk[:, 0:T]); dump('e2', e2, e2[:, 0:T]); dump('a', at_, at_[:, 0:T]); dump('kk', kk, kk[:, 0:T])
            dump('kmod', kt_, kt_[:, 0:T]); dump('cs', cs, cs[:, 0:T]); dump('E4', E4, E4[:, 0:T])
            if full:
                dump('g', gt_, gt_[:, 0:T]); dump('bon', bon, bon[:, 0:T])
            stop(2)
            if kind == "p":
                Zf = pZf[hp]
            else:
                Zf = Zf_w
                load(Zf_w, Zf_w[:], d_swkv[hp])
            for hh in range(2):
                rs_ = slice(hh * 64, hh * 64 + 64)
                P.op("pool", KW("tensor_copy", out=Zb[rs_, 0:nseg, rs_], in_=Zf[rs_, 0:nseg, :]), [Zf], [Zb])
            for ch in range(nch):
                cc = slice(ch * C, (ch + 1) * C)
                for src, dst in ((Bh, 0), (Kh, 1), (xv, 2)):
                    a_ = nacc()
                    P.op("pe", KW("transpose", a_[0:C, 0:128], src[:, cc], ident), [src, cst], [a_])
                    act([a_], [tokb[dst]], out=tokb[dst][:, ch, :], in_=a_[0:C, 0:128], func=AF.Copy)
                stop(3)
                for hh in range(2):
                    rs_ = slice(hh * 64, hh * 64 + 64)
                    pairs = [(Bt, At), (Kt, At)] + ([(Bt, Rt), (Kt, Rt)] if full else [])
                    bk = smp[hh]
                    for sl_, (l_, r_) in enumerate(pairs):
                        mm(bk, bk[0:C, sl_ * C:(sl_ + 1) * C], l_[rs_, cc], r_[rs_, cc], True, True, [l_, r_])
                    mm(bk, bk[0:C, 4 * C:5 * C], At[rs_, cc], Bt[rs_, cc], True, True, [At, Bt])
                stop(35)
                for hh in range(2):
                    bk = smp[hh]
                    psv = bk[0:C, 0:4 * C].rearrange("p (s c) -> p s c", s=4)
                    dve(KW("tensor_tensor", out=pw[:, hh, 0:2, :], in0=psv[:, 0:2, :], in1=mskSf.unsqueeze(1).broadcast_to([C, 2, C]), op=ALU.mult), [bk, cst], [pw])
                    if full:
                        dve(KW("tensor_tensor", out=pw[:, hh, 2:4, :], in0=psv[:, 2:4, :], in1=mskIf.unsqueeze(1).broadcast_to([C, 2, C]), op=ALU.mult), [bk, cst], [pw])
                    dve(KW("tensor_tensor", out=invP[0][:, hh, 0, :], in0=bk[0:C, 4 * C:5 * C], in1=mskL, op=ALU.mult), [bk, cst], [invP[0]])
                    dve(KW("tensor_tensor", out=invP[0][:, hh, 1, :], in0=bk[0:C, 0:C], in1=mskSf, op=ALU.mult), [bk, cst], [invP[0]])
                stop(4)
                dve(KW("tensor_tensor", out=invT[0][:], in0=invP[0][:, :, 1, :], in1=cst[0:C, 0, 0:C].unsqueeze(1).broadcast_to([C, 2, C]), op=ALU.add), [invP[0], cst], [invT[0]])
                for i_ in range(4):
                    dump('pw%d' % i_, pw, pw[:, 0, i_, :], C)
                dump('X', invP[0], invP[0][:, 0, 0, :], C); dump('tokB', tokb[0], tokb[0][:, ch, :], C); dump('tokK', tokb[1], tokb[1][:, ch, :], C); dump('tokV', tokb[2], tokb[2][:, ch, :], C)
                stop(5)
                tt = inverse(nlev)
                dump('TT', invT[tt], invT[tt][:, 0, :], C)
                stop(6)
                P.op("pool", KW("tensor_copy", out=TTb[:], in_=invT[tt][:]), [invT[tt]], [TTb])
                TT = TTb
                if nseg > 1:
                    dve(KW("tensor_tensor", out=expa[0][:], in0=At[:, cc].unsqueeze(1).broadcast_to([128, nseg, C]), in1=cm[:], op=ALU.mult), [At, cm], [expa[0]])
                    dve(KW("tensor_tensor", out=expa[1][:], in0=Rt[:, cc].unsqueeze(1).broadcast_to([128, nseg, C]), in1=cm[:], op=ALU.mult), [Rt, cm], [expa[1]])
                    for i_ in range(2):
                        dve(KW("tensor_tensor", out=expt[i_][:], in0=tokb[i_][:, ch, :].unsqueeze(1).broadcast_to([C, nseg, 128]),
                                                               in1=smk[:].unsqueeze(2).broadcast_to([C, nseg, 128]), op=ALU.mult), [tokb[i_], smk], [expt[i_]])
                    At_s = lambda s_: expa[0][:, s_, :]
                    Rt_s = (lambda s_: expa[1][:, s_, :]) if not (dbg and dbg.get('exp') == 'X') else (lambda s_: expa[0][:, s_, :])
                    Bh_s = lambda s_: expt[0][:, s_, :]
                    Kh_s = lambda s_: expt[1][:, s_, :]
                    rdA, rdR, rdB, rdK = [expa[0]], [expa[1]], [expt[0]], [expt[1]]
                else:
                    At_s = lambda s_: At[:, cc]
                    Rt_s = lambda s_: Rt[:, cc]
                    Bh_s = lambda s_: tokb[0][:, ch, :]
                    Kh_s = lambda s_: tokb[1][:, ch, :]
                    rdA, rdR, rdB, rdK = [At], [Rt], [tokb[0]], [tokb[1]]
                stop(66)
                Vt = tokb[2]
                rhs_ap = B_RHS[0:C, 0:128]
                for s_ in range(nseg):
                    mm(chn["rhs"], rhs_ap, At_s(s_), Zb[:, s_, :], s_ == 0, False, rdA + [Zb])
                for hh in range(2):
                    mm(chn["rhs"], B_RHS[0:C, hh * 64:hh * 64 + 64], pw[:, hh, 1, :], Vt[:, ch, hh * 64:hh * 64 + 64], False, hh == 1, [pw, Vt])
                stop(67)
                act([chn["rhs"]], [chs], out=chs[:], in_=rhs_ap, func=AF.Copy)
                dump('chs', chs, chs[:], C)
                for hh in range(2):
                    mm(chn["u"], B_U[0:C, 128 + hh * 64:128 + hh * 64 + 64], TT[:, hh, :], chs[:, hh * 64:hh * 64 + 64], True, True, [TT, chs])
                for hh in range(2):
                    rs_ = slice(hh * 64, hh * 64 + 64)
                    act([chn["u"]], [upad], out=upad[:, hh, rs_], in_=B_U[0:C, 128 + hh * 64:128 + hh * 64 + 64], func=AF.Copy)
                    P.op("pool", KW("tensor_copy", out=vpad[:, hh, rs_], in_=Vt[:, ch, rs_]), [Vt], [vpad])
                stop(68)
                dump('upad', upad, upad[:, 0, :], C)
                if full and nseg == 1:
                    o_ap = B_O[:, 256:256 + C]
                    mm(chn["o"], o_ap, Zb[:, 0, :], Rt_s(0), True, False, rdR + [Zb])
                    for hh in range(2):
                        mm(chn["o"], o_ap, upad[:, hh, :], pw[:, hh, 2, :], False, False, [upad, pw])
                        mm(chn["o"], o_ap, vpad[:, hh, :], pw[:, hh, 3, :], False, hh == 1, [vpad, pw])
                    act([chn["o"]], [oT], out=oT[:, cc], in_=o_ap, func=AF.Copy)
                    dump('oc', oT, oT[:, cc])
                elif full:
                    ot_ap = B_O[0:C, 256:384]
                    for s_ in range(nseg if not (dbg and str(dbg.get('exp', '')).isdigit()) else int(dbg['exp'])):
                        mm(chn["o"], ot_ap, Rt_s(s_), Zb[:, s_, :], s_ == 0, False, rdR + [Zb])
                    stop(69)
                    for hh in range(2):
                        rs_ = slice(hh * 64, hh * 64 + 64)
                        mm(chn["o"], B_O[0:C, 256 + hh * 64:256 + hh * 64 + 64], pw[:, hh, 2, :], upad[:, hh, rs_], False, False, [pw, upad])
                        mm(chn["o"], B_O[0:C, 256 + hh * 64:256 + hh * 64 + 64], pw[:, hh, 3, :], Vt[:, ch, rs_], False, hh == 1, [pw, Vt])
                    stop(70)
                    act([chn["o"]], [sc2], out=sc2[0:C, 0:128], in_=ot_ap, func=AF.Copy)
                    stop(71)
                    a_ = nacc()
                    P.op("pe", KW("transpose", a_[:, 0:C], sc2[0:C, 0:128], cst[0:C, 0, 0:C]), [sc2, cst], [a_])
                    act([a_], [oT], out=oT[:, cc], in_=a_[:, 0:C], func=AF.Copy)
                stop(7)
                for s_ in range(nseg):
                    su = sun[s_ % 2]
                    su_ap = smp[2 + (s_ % 2)][:, 384:512]
                    for hh in range(2):
                        rs_ = slice(hh * 64, hh * 64 + 64)
                        mm(su, su_ap[:, rs_], Bh_s(s_), upad[:, hh, rs_], True, False, rdB + [upad])
                        mm(su, su_ap[:, rs_], Kh_s(s_), Vt[:, ch, rs_], False, True, rdK + [Vt])
                    gi = (ch * C) // Cg + (s_ if nseg > 1 else (C // Cg - 1))
                    gcol = gi * Cg + Cg - 1
                    for hh in range(2):
                        rs_ = slice(hh * 64, hh * 64 + 64)
                        dve(KW("scalar_tensor_tensor",
                            out=Zf[rs_, s_, :], in0=Zf[rs_, s_, :], scalar=E1[rs_, gcol:gcol + 1], in1=su_ap[rs_, rs_], op0=ALU.mult, op1=ALU.add), [Zf, E1, su], [Zf])
                        P.op("pool", KW("tensor_copy", out=Zb[rs_, s_, rs_], in_=Zf[rs_, s_, :]), [Zf], [Zb])
            if kind == "s":
                store(o_swkv[hp], Zf_w, Zf_w[:])
            elif si == g.NSLAB - 1:
                store(o_pwkv[hp], Zf, Zf[:])
            if full:
                dump('oT', oT, oT[:, 0:T])
                a_ = nacc()
                mm(a_, a_[:, 0:T], bones, oT[:, 0:T], True, True, [cst, oT])
                dve(KW("scalar_tensor_tensor", out=sc2[:, 0:T], in0=a_[:, 0:T], scalar=-1.0 / 64, in1=oT[:, 0:T], op0=ALU.mult, op1=ALU.add), [a_, oT], [sc2])
                act([sc2], [sc_], out=sc_[:, 0:T], in_=sc2[:, 0:T], func=AF.Square)
                a_ = nacc()
                mm(a_, a_[:, 0:T], bones, sc_[:, 0:T], True, True, [cst, sc_])
                rsqrt_to(sc_, sc_[:, 0:T], a_, a_[:, 0:T], 1.0 / 64, GN_EPS, E3, E3[:, 0:T])
                dve(KW("tensor_tensor", out=sc2[:, 0:T], in0=sc2[:, 0:T], in1=sc_[:, 0:T], op=ALU.mult), [sc2, sc_], [sc2])
                dve(KW("tensor_scalar", out=sc2[:, 0:T], in0=sc2[:, 0:T], scalar1=vec[:, g.v_lw + hp:g.v_lw + hp + 1], scalar2=vec[:, g.v_lb + hp:g.v_lb + hp + 1],
                                                op0=ALU.mult, op1=ALU.add), [sc2, vec], [sc2])
                dve(KW("tensor_tensor", out=sc2[:, 0:T], in0=sc2[:, 0:T], in1=bon[:, 0:T], op=ALU.add), [sc2, bon], [sc2])
                dve(KW("tensor_tensor", out=yaT[:, hp, 0:T], in0=sc2[:, 0:T], in1=gt_[:, 0:T], op=ALU.mult), [sc2, gt_], [yaT.r[hp]])
        if kind == "s":
            store(o_sshift, sso, sso[:])
        elif si == g.NSLAB - 1:
            store(o_pshift, pshift, pshift[:])
        if dbg and dbg["what"] == "yaT" and dbg["slab"] == (kind, si):
            t0 = tmpf[5]
            for c in range(NHP):
                P.op("pool", KW("tensor_copy", out=t0[:, 0:T], in_=yaT[:, c, 0:T]), [yaT.r[c]], [t0])
                store(o_dbg[:, c, 0:T], t0, t0[:, 0:T])
            return
        if dbg and (dbg['what'] in ('hT', 'yaT', 'xr', 'xk', 'e2', 'a', 'kk', 'kmod', 'cs', 'E4', 'g', 'bon', 'oT') or dbg['what'][:2] in ('pw', 'X', 'to', 'TT', 'ch', 'up', 'oc')):
            return

        wtb = wget(d_win[g.t_gb].rearrange("p k n -> p (k n)"), 128, KC * 128, key=("win", g.t_gb))
        for ch in range(nch):
            cc = slice(ch * C, (ch + 1) * C)
            a_ = nacc()
            for kc in range(KC):
                mm(a_, a_[0:C, 0:GH], hT[:, kc, cc], wtb[:, kc * 128:kc * 128 + GH], kc == 0, kc == KC - 1, [wtb, hT.r[kc]])
            act([a_], [tk], out=tk[:, ch, 0, :], in_=a_[0:C, 0:GH], func=AF.Sigmoid)
        wta = wget(d_win[g.t_ga].rearrange("p k n -> p (k n)"), 128, KC * 128, key=("win", g.t_ga))
        for ch in range(nch):
            cc = slice(ch * C, (ch + 1) * C)
            a_ = nacc()
            for kc in range(KC):
                mm(a_, a_[0:C, GH:2 * GH], hT[:, kc, cc], wta[:, kc * 128:kc * 128 + GH], kc == 0, kc == KC - 1, [wta, hT.r[kc]])
            sc_ = tmpf[4]
            dve(KW("tensor_tensor", out=sc_[0:C, 0:GH], in0=a_[0:C, GH:2 * GH], in1=gsc[0:C, GH:2 * GH], op=ALU.add), [a_, gsc], [sc_])
            act([sc_], [sc_], out=sc_[0:C, 0:GH], in_=sc_[0:C, 0:GH], func=AF.Exp)
            act([sc_, epst], [sc_], out=sc_[0:C, 0:GH], in_=sc_[0:C, 0:GH], func=AF.Ln, bias=epsc[1.0][0:C, :], scale=1.0)
            dve(KW("tensor_tensor", out=tk[:, ch, 1, :], in0=sc_[0:C, 0:GH], in1=gsc[0:C, 0:GH], op=ALU.mult), [sc_, gsc], [tk])
            a_ = nacc()
            mm(a_, a_[0:C, 0:GH], mskIf, tk[:, ch, 1, :], True, True, [cst, tk])
            mm(a_, a_[0:C, GH:2 * GH], segones, tk[:, ch, 1, :], True, True, [cst, tk])
            act([a_], [tk], out=tk[:, ch, 2, :], in_=a_[0:C, 0:GH], func=AF.Copy)
            act([a_], [tk], out=tk[:, ch, 3, :], in_=a_[0:C, 0:GH], func=AF.Exp)
            dve(KW("tensor_scalar", out=tk[:, ch, 4, :], in0=tk[:, ch, 3, :], scalar1=-1.0, scalar2=None, op0=ALU.mult), [tk], [tk])
            dve(KW("tensor_tensor", out=tk[:, ch, 5, :], in0=a_[0:C, GH:2 * GH], in1=tk[:, ch, 2, :], op=ALU.subtract), [a_, tk], [tk])
            act([tk], [tk], out=tk[:, ch, 5, :], in_=tk[:, ch, 5, :], func=AF.Exp)

        def conv_mix(chunk, a_, out_t):
            pe_ = tmpf[3]
            W3 = nseg * (L + 3)
            pv = pe_[:, 0:W3].rearrange("p (s l) -> p s l", s=nseg)
            act([a_], [pe_], out=pv[:, :, 3:L + 3], in_=tokv(a_.t), func=AF.Copy)
            if kind == "p":
                P.op("pool", KW("tensor_copy", out=pv[:, :, 0:3], in_=pconv[:, chunk, :, :]), [pconv], [pe_])
                P.op("pool", KW("tensor_copy", out=pconv[:, chunk, :, :], in_=pv[:, :, L:L + 3]), [pe_], [pconv])
            else:
                ci = chunk % 2
                P.dma("pool", cvin_s[ci], cvin[ci][:], d_sconv[:, chunk, :, :], writes=[cvin[ci]])
                P.op("pool", KW("tensor_copy", out=pv[:, :, 0:3], in_=cvin[ci][:]), [cvin[ci]], [pe_])
                P.dma("pool", cvout_s, o_sconv[:, chunk, :, :], pv[:, :, L:L + 3], reads=[pe_])
            ov = tokv(out_t.t)
            cwc = lambda i_: vec[:, g.v_cw + i_ * 3 * GH + chunk:g.v_cw + i_ * 3 * GH + chunk + 1]
            dve(KW("tensor_scalar", out=ov, in0=pv[:, :, 0:L], scalar1=cwc(0), scalar2=None, op0=ALU.mult), [pe_, vec], [out_t])
            for i_ in range(1, 4):
                dve(KW("scalar_tensor_tensor", out=ov, in0=pv[:, :, i_:i_ + L], scalar=cwc(i_), in1=ov, op0=ALU.mult, op1=ALU.add), [pe_, vec, out_t], [out_t])
            act([out_t], [out_t], out=out_t[:, 0:T], in_=out_t[:, 0:T], func=AF.Silu)

        kso, kw_, koo = smp[0], smp[1], smp[0]
        for h in range(GH):
            qn, kn, vn, zs = tmpf[6], tmpf[7], tmpf[8], tmpf[9]
            sq_, rn_, sc2 = tmpf[10], tmpf[11], tmpf[12]
            for tid, cchunk, xt_ in ((g.t_gq + h, h, qn), (g.t_gk + h, GH + h, kn), (g.t_gv + h, 2 * GH + h, vn)):
                if tid == g.t_gq + h and not full:
                    pass
                a_ = proj(tid); conv_mix(cchunk, a_, xt_)
            if full:
                a_ = proj(g.t_gz + h)
                act([a_], [zs], out=zs[:, 0:T], in_=a_[:, 0:T], func=AF.Silu)
            qb, kb, Qg = tmpb[3], tmpb[4], tmpb[5]
            for x_, scl, xb_ in ((qn, 128.0 ** -0.5, qb), (kn, 1.0, kb)):
                act([x_], [sq_], out=sq_[:, 0:T], in_=x_[:, 0:T], func=AF.Square)
                a_ = nacc()
                mm(a_, a_[:, 0:T], ones, sq_[:, 0:T], True, True, [cst, sq_])
                rsqrt_to(rn_, rn_[:, 0:T], a_, a_[:, 0:T], 1.0, 1e-6, sc2, sc2[:, 0:T])
                dve(KW("scalar_tensor_tensor", out=x_[:, 0:T], in0=x_[:, 0:T], scalar=scl, in1=rn_[:, 0:T], op0=ALU.mult, op1=ALU.mult), [x_, rn_], [x_])
                dve(KW("tensor_copy", out=xb_[:, 0:T], in_=x_[:, 0:T]), [x_], [xb_])
            dump('gq', qn, qn[:, 0:T]); dump('gk', kn, kn[:, 0:T]); dump('gv', vn, vn[:, 0:T])
            if kind == "p":
                Sf = pSf[h]
            else:
                Sf = Sf_w
                load(Sf_w, Sf_w[:], d_sgdn[h])
            P.op("pool", KW("tensor_copy", out=Sb[:, 0:nseg, :], in_=Sf[:, 0:nseg, :]), [Sf], [Sb])
            egB, eDT, dsT, diT, XTf = tmpf[18], tmpf[19], tmpf[20], tmpf[21], tmpf[22]
            for cp in range((nch + 1) // 2):
                for hh in range(2):
                    ch = min(cp * 2 + hh, nch - 1)
                    cc = slice(ch * C, (ch + 1) * C)
                    col = lambda j_: tk[:, ch, j_, h:h + 1]
                    if hh == 0 or cp * 2 + hh < nch:
                        for src, dst, scol in ((kn, 1, col(5)), (vn, 2, None)):
                            a_ = nacc()
                            P.op("pe", KW("transpose", a_[0:C, 0:128], src[:, cc], ident), [src, cst], [a_])
                            if scol is None:
                                act([a_], [tokb[dst]], out=tokb[dst][:, ch, :], in_=a_[0:C, 0:128], func=AF.Copy)
                            else:
                                act([a_, tk], [tokb[dst]], out=tokb[dst][:, ch, :], in_=a_[0:C, 0:128], func=AF.Copy, scale=scol)
                    mm(smp[0], smp[0][0:C, 0:C], kb[:, cc], kb[:, cc], True, True, [kb])
                    mm(smp[0], smp[0][0:C, C:2 * C], kb[:, cc], qb[:, cc], True, True, [kb, qb])
                    dve(KW("tensor_scalar", out=sc2[0:C, 0:C], in0=mskIf, scalar1=col(1), scalar2=None, op0=ALU.mult), [cst, tk], [sc2])
                    mm(smp[1], smp[1][:, 0:C], cst[0:C, 2, :], sc2[0:C, 0:C], True, True, [cst, sc2])
                    dve(KW("tensor_scalar", out=eDT[0:C, 0:C], in0=smp[1][0:C, 0:C], scalar1=col(2), scalar2=0.0, op0=ALU.subtract, op1=ALU.min), [smp[1], tk], [eDT])
                    act([eDT], [eDT], out=eDT[0:C, 0:C], in_=eDT[0:C, 0:C], func=AF.Exp)
                    act([smp[1]], [egB], out=egB[:, 0:C], in_=smp[1][:, 0:C], func=AF.Exp)
                    dve(KW("scalar_tensor_tensor", out=dsT[0:C, 0:C], in0=eDT[0:C, 0:C], scalar=col(0), in1=mskSf, op0=ALU.mult, op1=ALU.mult), [eDT, tk, cst], [dsT])
                    dve(KW("scalar_tensor_tensor", out=XTf[0:C, 0:C], in0=smp[0][0:C, 0:C], scalar=-1.0, in1=dsT[0:C, 0:C], op0=ALU.mult, op1=ALU.mult), [smp[0], dsT], [XTf])
                    P.op("pool", KW("tensor_copy", out=invP[0][:, hh, 1, :], in_=XTf[0:C, 0:C]), [XTf], [invP[0]])
                    dve(KW("tensor_tensor", out=invT[0][:, hh, :], in0=XTf[0:C, 0:C], in1=cst[0:C, 0, 0:C], op=ALU.add), [XTf, cst], [invT[0]])
                    P.op("pe", KW("transpose", smp[1][0:C, 128:128 + C], XTf[0:C, 0:C], cst[0:C, 0, 0:C]), [XTf, cst], [smp[1]])
                    act([smp[1]], [invP[0]], out=invP[0][:, hh, 0, :], in_=smp[1][0:C, 128:128 + C], func=AF.Copy)
                    if hh == 0 or cp * 2 + hh < nch:
                        if full:
                            dve(KW("tensor_tensor", out=diT[0:C, 0:C], in0=eDT[0:C, 0:C], in1=mskIf, op=ALU.mult), [eDT, cst], [diT])
                            dve(KW("tensor_tensor", out=ATs[:, ch, :], in0=smp[0][0:C, C:2 * C], in1=diT[0:C, 0:C], op=ALU.mult), [smp[0], diT], [ATs])
                            dve(KW("tensor_tensor", out=Qg[:, cc], in0=qn[:, cc], in1=egB[:, 0:C], op=ALU.mult), [qn, egB], [Qg])
                        if nseg == 1:
                            P.op("pool", KW("tensor_copy", out=cds[:, ch, 0:1], in_=egB[:, C - 1:C]), [egB], [cds])
                        else:
                            P.op("pool", KW("tensor_copy", out=cds[:, ch, :], in_=egB[:, 0:C].rearrange("p (s l) -> p s l", l=LS)[:, :, LS - 1]), [egB], [cds])
                tt = inverse(nlev)
                for hh in range(2):
                    if cp * 2 + hh < nch:
                        P.op("pool", KW("tensor_copy", out=T2T[:, cp * 2 + hh, :], in_=invT[tt][:, hh, :]), [invT[tt]], [T2T])
            dump('T2T', T2T, T2T[:, 0, :], C); dump('ATs', ATs, ATs[:, 0, :], C)
            oT = tmpf[24]
            for ch in range(nch):
                cc = slice(ch * C, (ch + 1) * C)
                col = lambda j_: tk[:, ch, j_, h:h + 1]
                if nseg > 1:
                    dve(KW("tensor_tensor", out=expa[0][:], in0=kb[:, cc].unsqueeze(1).broadcast_to([128, nseg, C]), in1=cm[:], op=ALU.mult), [kb, cm], [expa[0]])
                    if full:
                        dve(KW("tensor_tensor", out=expa[1][:], in0=Qg[:, cc].unsqueeze(1).broadcast_to([128, nseg, C]), in1=cm[:], op=ALU.mult), [Qg, cm], [expa[1]])
                    dve(KW("tensor_tensor", out=expt[0][:], in0=tokb[1][:, ch, :].unsqueeze(1).broadcast_to([C, nseg, 128]),
                           in1=smk[:].unsqueeze(2).broadcast_to([C, nseg, 128]), op=ALU.mult), [tokb[1], smk], [expt[0]])
                    K_s = lambda s_: expa[0][:, s_, :]
                    Q_s = lambda s_: expa[1][:, s_, :]
                    Kt_s = lambda s_: expt[0][:, s_, :]
                    rdK, rdQ, rdT = [expa[0]], [expa[1]], [expt[0]]
                else:
                    K_s = lambda s_: kb[:, cc]
                    Q_s = lambda s_: Qg[:, cc]
                    Kt_s = lambda s_: tokb[1][:, ch, :]
                    rdK, rdQ, rdT = [kb], [Qg], [tokb[1]]
                ks_ap = smp[0][0:C, 0:128]
                for s_ in range(nseg):
                    mm(kso, ks_ap, K_s(s_), Sb[:, s_, :], s_ == 0, s_ == nseg - 1, rdK + [Sb])
                dve(KW("scalar_tensor_tensor", out=chs[:], in0=ks_ap, scalar=col(4), in1=tokb[2][:, ch, :], op0=ALU.mult, op1=ALU.add), [kso, tk, tokb[2]], [chs])
                w_ap = smp[1][0:C, 128:256]
                mm(kw_, w_ap, T2T[:, ch, :], chs[:], True, True, [T2T, chs])
                act([kw_, tk], [vnew], out=vnew[:], in_=w_ap, func=AF.Copy, scale=col(0))
                if full and nseg == 1:
                    o_ap = smp[0][:, 256:256 + C]
                    mm(koo, o_ap, Sb[:, 0, :], Q_s(0), True, False, rdQ + [Sb])
                    mm(koo, o_ap, vnew[:], ATs[:, ch, :], False, True, [vnew, ATs])
                    act([koo], [oT], out=oT[:, cc], in_=o_ap, func=AF.Copy)
                elif full:
                    ot_ap = smp[0][0:C, 256:384]
                    for s_ in range(nseg):
                        mm(koo, ot_ap, Q_s(s_), Sb[:, s_, :], s_ == 0, False, rdQ + [Sb])
                    mm(koo, ot_ap, ATs[:, ch, :], vnew[:], False, True, [ATs, vnew])
                    act([koo], [sc2], out=sc2[0:C, 0:128], in_=ot_ap, func=AF.Copy)
                    a_ = nacc()
                    P.op("pe", KW("transpose", a_[:, 0:C], sc2[0:C, 0:128], cst[0:C, 0, 0:C]), [sc2, cst], [a_])
                    act([a_], [oT], out=oT[:, cc], in_=a_[:, 0:C], func=AF.Copy)
                for s_ in range(nseg):
                    su = sun[s_ % 2]
                    su_ap = smp[2 + (s_ % 2)][:, 384:512]
                    mm(su, su_ap, Kt_s(s_), vnew[:], True, True, rdT + [vnew])
                    dve(KW("scalar_tensor_tensor", out=Sf[:, s_, :], in0=Sf[:, s_, :], scalar=cds[:, ch, s_:s_ + 1], in1=su_ap, op0=ALU.mult, op1=ALU.add), [Sf, cds, su], [Sf])
                    P.op("pool", KW("tensor_copy", out=Sb[:, s_, :], in_=Sf[:, s_, :]), [Sf], [Sb])
            if kind == "s":
                store(o_sgdn[h], Sf_w, Sf_w[:])
            elif si == g.NSLAB - 1:
                store(o_pgdn[h], Sf, Sf[:])
            if full:
                dump('goT', oT, oT[:, 0:T])
                act([oT], [sq_], out=sq_[:, 0:T], in_=oT[:, 0:T], func=AF.Square)
                a_ = nacc()
                mm(a_, a_[:, 0:T], ones, sq_[:, 0:T], True, True, [cst, sq_])
                rsqrt_to(rn_, rn_[:, 0:T], a_, a_[:, 0:T], 1.0 / 128, NORM_EPS, sc2, sc2[:, 0:T])
                dve(KW("tensor_tensor", out=sc2[:, 0:T], in0=oT[:, 0:T], in1=rn_[:, 0:T], op=ALU.mult), [oT, rn_], [sc2])
                dve(KW("scalar_tensor_tensor", out=ybT[:, h, 0:T], in0=sc2[:, 0:T], scalar=vec[:, g.v_gnw:g.v_gnw + 1], in1=zs[:, 0:T], op0=ALU.mult, op1=ALU.mult), [sc2, vec, zs], [ybT.r[h]])
        if kind == "p" and si == g.NSLAB - 1:
            store(o_pconv, pconv, pconv[:])
        if dbg and dbg["what"] == "ybT" and dbg["slab"] == (kind, si):
            t0 = tmpf[5]
            for c in range(GH):
                P.op("pool", KW("tensor_copy", out=t0[:, 0:T], in_=ybT[:, c, 0:T]), [ybT.r[c]], [t0])
                store(o_dbg[:, c, 0:T], t0, t0[:, 0:T])
            return
        if dbg and dbg['what'] in ('ybT', 'gq', 'gk', 'gv', 'T2T', 'ATs', 'goT'):
            return

        if not full:
            if si == g.NSLAB // 2 - 1:
                for t_ in pZf + pSf + [pshift, pconv]:
                    fl = t_[:].rearrange("p a b -> p (a b)") if len(t_.t.shape) == 3 else t_[:].rearrange("p a b c -> p (a b c)")
                    dve(KW("tensor_scalar", out=fl, in0=fl, scalar1=flag[:, 0:1], scalar2=None, op0=ALU.mult), [t_, flag], [t_])
            return
        for c in range(KC):
            sg = []
            for tid, wsrc, nk, yT, wkey in ((g.t_ma + c, d_woa[c], NHP, yaT, ("woa", c)), (g.t_mb + c, d_wob[c], GH, ybT, ("wob", c))):
                a_ = proj(tid)
                sgt = tmpf[3 + len(sg)]
                act([a_], [sgt], out=sgt[:, 0:T], in_=a_[:, 0:T], func=AF.Sigmoid)
                wt = wget(wsrc.rearrange("p k n -> p (k n)"), 128, nk * 128, key=wkey)
                a2 = nacc()
                for kc in range(nk):
                    mm(a2, a2[:, 0:T], wt[:, kc * 128:(kc + 1) * 128], yT[:, kc, 0:T], kc == 0, kc == nk - 1, [wt, yT.r[kc]])
                dve(KW("tensor_tensor", out=sgt[:, 0:T], in0=sgt[:, 0:T], in1=a2[:, 0:T], op=ALU.mult), [sgt, a2], [sgt])
                sg.append(sgt)
            dve(KW("tensor_tensor", out=mgT[:, c, 0:T], in0=sg[0][:, 0:T], in1=sg[1][:, 0:T], op=ALU.add), [sg[0], sg[1]], [mgT.r[c]])
        dump_chunks('mgT', mgT, KC, T, kind, si)
        for c in range(KC):
            wt = wget(d_wo[c].rearrange("p k n -> p (k n)"), 128, KC * 128, key=("wo", c))
            a_ = nacc()
            for kc in range(KC):
                mm(a_, a_[:, 0:T], wt[:, kc * 128:(kc + 1) * 128], mgT[:, kc, 0:T], kc == 0, kc == KC - 1, [wt, mgT.r[kc]])
            xt_, xap = xload(c)
            seg_affine("dve", x1T[:, c, :], a_, a_, 2, c, xt_, xap, [a_, mT, xt_], [x1T.r[c]])
        dump_chunks('x1T', x1T, KC, T, kind, si)
        stats_rstd(lambda c: (x1T.r[c], x1T[:, c, 0:T]), KC, 1.0 / D, NORM_EPS)
        for c in range(KC):
            t0 = tmpf[3 + c % 2]
            dve(KW("tensor_tensor", out=t0[:, 0:T], in0=x1T[:, c, 0:T], in1=rstd[:, 0:T], op=ALU.mult), [x1T.r[c], rstd], [t0])
            seg_affine("act", hT[:, c, :], t0, t0, 4, c, None, (3, c), [t0, mT], [hT.r[c]])
        dump_chunks('h2T', hT, KC, T, kind, si)
        for g8 in range(g.FC // 8):
            for j in range(8):
                wt = wget(d_wup[g8 * 8 + j].rearrange("p k n -> p (k n)"), 128, KC * 128, key=("wup", g8 * 8 + j))
                a_ = nacc()
                for kc in range(KC):
                    mm(a_, a_[:, 0:T], wt[:, kc * 128:(kc + 1) * 128], hT[:, kc, 0:T], kc == 0, kc == KC - 1, [wt, hT.r[kc]])
                t0 = tmpf[3 + j % 2]
                act([a_], [t0], out=t0[:, 0:T], in_=a_[:, 0:T], func=AF.Relu)
                P.op("pool" if j % 2 else "dve", KW("tensor_tensor", out=uh[:, j, 0:T], in0=t0[:, 0:T], in1=t0[:, 0:T], op=ALU.mult), [t0], [uh.r[j]])
            for c in range(KC):
                wt = wget(d_wdn[g8, c].rearrange("p k n -> p (k n)"), 128, 8 * 128, key=("wdn", g8 * KC + c))
                a_ = nacc()
                for j in range(8):
                    mm(a_, a_[:, 0:T], wt[:, j * 128:(j + 1) * 128], uh[:, j, 0:T], j == 0, j == 7, [wt, uh.r[j]])
                seg_affine("dve", x1T[:, c, :], a_, a_, 5, c, x1T.r[c], x1T[:, c, :], [a_, mT, x1T.r[c]], [x1T.r[c]])
        dump_chunks('x2T', x1T, KC, T, kind, si)
        stats_rstd(lambda c: (x1T.r[c], x1T[:, c, 0:T]), KC, 1.0 / D, NORM_EPS)
        for c in range(KC):
            i = c % 2
            dve(KW("tensor_tensor", out=ych[i][:, 0:T], in0=x1T[:, c, 0:T], in1=rstd[:, 0:T], op=ALU.mult), [x1T.r[c], rstd], [ych[i]])
            dve(KW("tensor_scalar", out=ych[i][:, 0:T], in0=ych[i][:, 0:T], scalar1=vec[:, g.v_fw + c:g.v_fw + c + 1], scalar2=None, op0=ALU.mult), [ych[i], vec], [ych[i]])
            dst = o_yp[si - g.NSLAB // 2][:, c, :] if kind == "p" else o_ys[:, c, :]
            P.dma("pool", ych_s[i], dst, ych[i][:, 0:T], reads=[ych[i]])

    try:
        _run_slabs()
    except StopBuild:
        pass
    P.wreq = wreq
    return nc, P


def tile_cols(Wm, col_lists):
    K = Wm.shape[0]
    out = np.zeros((len(col_lists), 128, K // 128, 128), np.float32)
    for t, (c0, n) in enumerate(col_lists):
        blk = Wm[:, c0:c0 + n].reshape(K // 128, 128, n)
        out[t, :, :, :n] = blk.transpose(1, 0, 2)
    return out


def fm(v, n):
    return np.ascontiguousarray(np.asarray(v, np.float32).reshape(n, 128).T)


def host_shared(cfg, inp):
    g = cfg
    D, KC, NHP, GH, RW, GW = g.D, g.KC, g.NHP, g.GH, g.RW, g.GW
    sh = {}
    w_in = np.asarray(inp["w_in"][0]); w_ada = np.asarray(inp["w_ada"][0])
    sh["wada"] = tile_cols(w_ada, [(j * D + c * 128, 128) for j in range(6) for c in range(KC)])
    cols = [(i * 128, 128) for i in range(g.RC)]
    G0 = g.RCOLS
    cols += [(G0 + i * 128, 128) for i in range(4 * GH)]
    cols += [(G0 + 4 * GW, GH), (G0 + 4 * GW + GH, GH)]
    M0 = g.RCOLS + g.GCOLS
    cols += [(M0 + i * 128, 128) for i in range(2 * KC)]
    assert len(cols) == g.NT_IN
    sh["win"] = tile_cols(w_in, cols)
    sh["ww2"] = np.concatenate([inp["r_w_w2"][0], inp["r_w_a2"][0]], 0).astype(np.float32)
    sh["wg2"] = np.ascontiguousarray(np.asarray(inp["r_w_g2"][0]).reshape(2, 128, RW).transpose(1, 0, 2))
    woa = np.asarray(inp["w_out_a"][0])
    sh["woa"] = tile_cols(woa, [(c * 128, 128) for c in range(KC)])
    wob = np.asarray(inp["w_out_b"][0])
    sh["wob"] = tile_cols(wob, [(c * 128, 128) for c in range(KC)])
    sh["wo"] = tile_cols(np.asarray(inp["w_out"][0]), [(c * 128, 128) for c in range(KC)])
    sh["wup"] = tile_cols(np.asarray(inp["w_up"][0]), [(c * 128, 128) for c in range(g.FC)])
    wdn = np.asarray(inp["w_down"][0])
    sh["wdn"] = np.ascontiguousarray(wdn.reshape(g.FC // 8, 8, 128, KC, 128).transpose(0, 3, 2, 1, 4))
    v = np.zeros((128, g.NV), np.float32)
    v[:, g.v_bada:g.v_bada + 6 * KC] = fm(inp["b_ada"][0], 6 * KC)
    v[:, g.v_n1:g.v_n1 + KC] = fm(inp["norm1_w"][0], KC)
    v[:, g.v_n2:g.v_n2 + KC] = fm(inp["norm2_w"][0], KC)
    v[:, g.v_fw:g.v_fw + KC] = fm(inp["final_norm_w"], KC)
    v[:, g.v_mu:g.v_mu + g.RC] = fm(inp["r_mu"][0], g.RC)
    for nm, col in (("r_w0", g.v_w0), ("r_a0", g.v_a0), ("r_k_k", g.v_kk), ("r_k_a", g.v_ka), ("r_lnx_w", g.v_lw), ("r_lnx_b", g.v_lb)):
        v[:, col:col + NHP] = fm(inp[nm][0], NHP)
    v[:, g.v_rk:g.v_rk + NHP] = fm(np.asarray(inp["r_r_k"][0]).reshape(-1), NHP)
    cw = np.asarray(inp["g_conv_w"][0])
    for i in range(4):
        v[:, g.v_cw + i * 3 * GH:g.v_cw + (i + 1) * 3 * GH] = fm(cw[i], 3 * GH)
    v[:, g.v_gnw] = np.asarray(inp["g_norm_w"][0])
    sh["vecsT"] = v
    sh["gsc"] = np.concatenate([np.tile(np.asarray(inp["g_a_log"][0])[None], (128, 1)), np.tile(np.asarray(inp["g_dt_bias"][0])[None], (128, 1))], 1).astype(np.float32)
    C = g.C
    cst = np.zeros((128, 10, 128), np.float32)
    cst[:, 0] = np.eye(128)
    bo = np.zeros((128, 128)); bo[:64, :64] = 1; bo[64:, 64:] = 1
    cst[:, 1] = bo
    cst[:, 2] = 1.0
    ii = np.arange(C)
    cst[:C, 3, :C] = (ii[None, :] > ii[:, None])
    cst[:C, 4, :C] = (ii[None, :] >= ii[:, None])
    sg = ii // g.LS
    same = sg[None, :] == sg[:, None]
    cst[:C, 5, :C] = (ii[None, :] > ii[:, None]) & same
    cst[:C, 6, :C] = (ii[None, :] >= ii[:, None]) & same
    cst[:C, 7, :C] = same
    cst[:C, 8, :C] = (ii[None, :] < ii[:, None])
    cst[:C, 9, :C] = (ii[None, :] < ii[:, None]) & same
    sh["consts"] = cst
    cmk = np.zeros((128, g.NS, C), np.float32)
    smk = np.zeros((C, g.NS), np.float32)
    for s_ in range(g.NS):
        cmk[:, s_, s_ * g.LS:(s_ + 1) * g.LS] = 1; smk[s_ * g.LS:(s_ + 1) * g.LS, s_] = 1
    import ml_dtypes
    sh["cm"] = cmk.astype(ml_dtypes.bfloat16); sh["sm"] = smk
    TM = max(g.TP, g.TS)
    r = np.ones((128, 2, TM), np.float32)
    r[:, 0, 0::C] = 0; r[:, 1, 0::g.LS] = 0
    sh["rst"] = r
    return sh


def host_core(cfg, inp, core):
    g = cfg
    KC, NS, LS = g.KC, g.NS, g.LS
    b = core // 2; half = core % 2
    m = {}
    xp = np.asarray(inp["x_prompt"][b])
    H = g.SEQ // 2
    if half == 1:
        toks = xp
    else:
        toks = np.concatenate([xp[:H], xp[:H]], 0)
    m["xp"] = np.ascontiguousarray(toks.reshape(g.NSLAB, g.TP, KC, 128).transpose(0, 3, 2, 1))
    sl = slice(core * NS, (core + 1) * NS)
    xs = np.asarray(inp["x_sample"][sl]).reshape(NS * LS, KC, 128)
    m["xs"] = np.ascontiguousarray(xs.transpose(2, 1, 0))
    cc = np.concatenate([np.asarray(inp["c_prompt"][b])[None], np.asarray(inp["c_sample"][sl])], 0)
    m["cT"] = np.ascontiguousarray(cc.reshape(1 + NS, KC, 128).transpose(2, 1, 0))
    m["flag"] = np.full((128, 1), float(half), np.float32)
    wkv = np.asarray(inp["state_rwkv_wkv"][0, sl])
    m["s_wkv"] = np.ascontiguousarray(wkv.reshape(NS, g.NHP, 2, 64, 64).transpose(1, 2, 4, 0, 3).reshape(g.NHP, 128, NS, 64))
    shf = np.asarray(inp["state_rwkv_shift"][0, sl])
    m["s_shift"] = np.ascontiguousarray(shf.reshape(NS, g.RC, 128).transpose(2, 1, 0))
    gdn = np.asarray(inp["state_gdn"][0, sl])
    m["s_gdn"] = np.ascontiguousarray(gdn.transpose(1, 2, 0, 3))
    cv = np.asarray(inp["state_gdn_conv"][0, sl])
    m["s_conv"] = np.ascontiguousarray(cv.reshape(NS, 3, 3 * g.GH, 128).transpose(3, 2, 0, 1))
    return m


_CACHE = {}


def _get_prog(cfg_key):
    if cfg_key not in _CACHE:
        cfg = Cfg(*cfg_key)
        nc0, P0 = build(cfg)
        cfg.wplan = P0.wreq
        nc, P = build(cfg)
        assert len(P.wreq) == len(cfg.wplan)
        if cfg.D < 4096:
            P.epoch = 300; P.dseg = 20
        from contextlib import ExitStack
        es = ExitStack()
        P.emit(es)
        _CACHE[cfg_key] = (cfg, nc, P, es)
    return _CACHE[cfg_key]


def run(cfg_key, inputs):
    cfg, nc, P, es = _get_prog(cfg_key)
    g = cfg
    inp = {k: np.asarray(v) for k, v in inputs.items()}
    sh = host_shared(cfg, inp)
    maps = []
    for c in range(8):
        m = dict(sh); m.update(host_core(cfg, inp, c)); maps.append(m)
    res = run_bass_kernel_spmd(nc, maps, core_ids=list(range(8)))
    R = res.results
    B = 4; D = g.D; NS = g.NS; LS = g.LS; KC = g.KC
    f32 = np.float32
    tokm = lambda a: np.ascontiguousarray(np.asarray(a).transpose(2, 1, 0)).reshape(a.shape[2], KC * 128)
    y_p = np.zeros((B, g.SEQ, D), f32); y_s = np.zeros((8 * NS, LS, D), f32)
    p_wkv = np.zeros((1, B, g.RH, 64, 64), f32); p_shift = np.zeros((1, B, g.RCOLS), f32)
    p_gdn = np.zeros((1, B, g.GH, 128, 128), f32); p_conv = np.zeros((1, B, 3, 3 * g.GW), f32)
    s_wkv = np.zeros((1, 8 * NS, g.RH, 64, 64), f32); s_shift = np.zeros((1, 8 * NS, g.RCOLS), f32)
    s_gdn = np.zeros((1, 8 * NS, g.GH, 128, 128), f32); s_conv = np.zeros((1, 8 * NS, 3, 3 * g.GW), f32)

    def wkv_host(a, n):
        a = np.asarray(a).reshape(g.NHP, 2, 64, n, 64)
        return np.ascontiguousarray(a.transpose(3, 0, 1, 4, 2)).reshape(n, g.RH, 64, 64)

    def shift_host(a, n):
        return np.ascontiguousarray(np.asarray(a).transpose(2, 1, 0)).reshape(n, g.RCOLS)

    def gdn_host(a, n):
        return np.ascontiguousarray(np.asarray(a).transpose(2, 0, 1, 3))

    def conv_host(a, n):
        return np.ascontiguousarray(np.asarray(a).transpose(2, 3, 1, 0)).reshape(n, 3, 3 * g.GW)

    H = g.SEQ // 2
    for c in range(8):
        b = c // 2; half = c % 2
        r = R[c]
        for j in range(g.NSLAB // 2):
            y_p[b, half * H + j * g.TP: half * H + (j + 1) * g.TP] = tokm(r["o_yp"][j])
        y_s[c * NS:(c + 1) * NS] = tokm(r["o_ys"]).reshape(NS, LS, D)
        if half == 1:
            p_wkv[0, b] = wkv_host(r["o_pwkv"], 1)[0]; p_shift[0, b] = shift_host(r["o_pshift"], 1)[0]
            p_gdn[0, b] = gdn_host(r["o_pgdn"], 1)[0]; p_conv[0, b] = conv_host(r["o_pconv"], 1)[0]
        sl = slice(c * NS, (c + 1) * NS)
        s_wkv[0, sl] = wkv_host(r["o_swkv"], NS); s_shift[0, sl] = shift_host(r["o_sshift"], NS)
        s_gdn[0, sl] = gdn_host(r["o_sgdn"], NS); s_conv[0, sl] = conv_host(r["o_sconv"], NS)
    return (y_p, y_s, p_wkv, p_shift, p_gdn, p_conv, s_wkv, s_shift, s_gdn, s_conv)


def kernel(**inputs):
    return run((4096, 2048, 128, 16, 4), inputs)
```
